# Optimizing a Trainium2 kernel written in Bass

```python
import jax, jax.numpy as jnp
from jax import lax
import numpy as np

D_MODEL = 1024
BATCH = 4
SEQ = 8192
DEPTH = 1

CHUNK = 64
A_HEADS = D_MODEL // 128
A_HEAD_DIM = 64
A_LEFT_CHUNKS = 8
A_BAND = A_LEFT_CHUNKS + 1
A_MAX_REL = 128
B_HEADS = D_MODEL // 128
B_Q_LORA = D_MODEL // 4
B_KV_LORA = D_MODEL // 8
B_NOPE = 64
B_ROPE = 32
B_V_DIM = 64
B_QK_DIM = B_NOPE + B_ROPE
ROPE_THETA = 10000.0
Q_BLOCK = 128
A_WIDTH = A_HEADS * A_HEAD_DIM
B_WIDTH = B_HEADS * B_V_DIM
D_MIX = A_WIDTH + B_WIDTH
IN_COLS = 3 * A_WIDTH + B_Q_LORA + B_KV_LORA + B_ROPE
D_FF = ((8 * D_MODEL // 3 + 127) // 128) * 128
EPS = 1e-6
NEG_INF = -1e30

kernel_name = "hymba_chunked_relbias_mla_macaron"


def rmsnorm(x, g):
    xf = x.astype(jnp.float32)
    y = xf * lax.rsqrt(jnp.mean(xf * xf, axis=-1, keepdims=True) + EPS)
    return (y * g.astype(jnp.float32)).astype(x.dtype)


def swiglu(h, w_gate, w_up, w_down):
    return (jax.nn.silu(h @ w_gate) * (h @ w_up)) @ w_down


def rope_tables(seq, dim):
    inv = 1.0 / (ROPE_THETA ** (jnp.arange(0, dim, 2, dtype=jnp.float32) / dim))
    ang = jnp.arange(seq, dtype=jnp.float32)[:, None] * inv[None, :]
    return jnp.cos(ang), jnp.sin(ang)


def apply_rope(x, cos, sin):
    x1, x2 = jnp.split(x, 2, axis=-1)
    c = cos[None, :, None, :].astype(x.dtype)
    s = sin[None, :, None, :].astype(x.dtype)
    return jnp.concatenate([x1 * c - x2 * s, x1 * s + x2 * c], axis=-1)


def chunked_relbias_attention(q, k, v, rel_bias):
    b, s, h, dh = q.shape
    nc = s // CHUNK
    band_len = A_BAND * CHUNK
    pad = ((0, 0), (A_LEFT_CHUNKS * CHUNK, 0), (0, 0), (0, 0))
    kp = jnp.pad(k, pad)
    vp = jnp.pad(v, pad)
    qc = q.reshape(b, nc, CHUNK, h, dh).transpose(1, 0, 2, 3, 4)
    qi = jnp.arange(CHUNK)
    kj = jnp.arange(band_len) - A_LEFT_CHUNKS * CHUNK
    rel = jnp.clip(qi[:, None] - kj[None, :], -A_MAX_REL, A_MAX_REL) + A_MAX_REL
    bias = rel_bias.astype(jnp.float32)[:, rel]
    scale = dh ** -0.5

    def one_chunk(args):
        q_blk, c = args
        kb = lax.dynamic_slice_in_dim(kp, c * CHUNK, band_len, axis=1)
        vb = lax.dynamic_slice_in_dim(vp, c * CHUNK, band_len, axis=1)
        sc = jnp.einsum('bihd,bjhd->bhij', q_blk, kb).astype(jnp.float32) * scale
        sc = sc + bias[None]
        valid = jnp.arange(band_len) >= (A_LEFT_CHUNKS - c) * CHUNK
        sc = jnp.where(valid[None, None, None, :], sc, NEG_INF)
        p = jax.nn.softmax(sc, axis=-1).astype(vb.dtype)
        return jnp.einsum('bhij,bjhd->bihd', p, vb)

    o = lax.map(one_chunk, (qc, jnp.arange(nc, dtype=jnp.int32)))
    return o.transpose(1, 0, 2, 3, 4).reshape(b, s, h * dh)


def block_causal_attention(q, k, v):
    b, s, h, dq = q.shape
    dv = v.shape[-1]
    nb = s // Q_BLOCK
    qb = q.reshape(b, nb, Q_BLOCK, h, dq).transpose(1, 0, 2, 3, 4)
    k_chunk = jnp.arange(s) // CHUNK
    scale = dq ** -0.5

    def one_block(args):
        q_blk, i = args
        q_chunk = (i * Q_BLOCK + jnp.arange(Q_BLOCK)) // CHUNK
        sc = jnp.einsum('bqhd,bkhd->bhqk', q_blk, k).astype(jnp.float32) * scale
        mask = k_chunk[None, :] <= q_chunk[:, None]
        sc = jnp.where(mask[None, None], sc, NEG_INF)
        p = jax.nn.softmax(sc, axis=-1).astype(v.dtype)
        return jnp.einsum('bhqk,bkhd->bqhd', p, v)

    o = lax.map(one_block, (qb, jnp.arange(nb, dtype=jnp.int32)))
    return o.transpose(1, 0, 2, 3, 4).reshape(b, s, h * dv)


def mla_attention(c_q, c_kv, k_rope, q_lat_norm, w_uq, kv_lat_norm, w_ukv,
                  q_nope_norm, q_rope_norm, k_nope_norm, k_rope_norm, cos, sin):
    b, s, _ = c_q.shape
    q = (rmsnorm(c_q, q_lat_norm) @ w_uq).reshape(b, s, B_HEADS, B_QK_DIM)
    kv = (rmsnorm(c_kv, kv_lat_norm) @ w_ukv).reshape(b, s, B_HEADS, B_NOPE + B_V_DIM)
    q_nope = rmsnorm(q[..., :B_NOPE], q_nope_norm)
    q_pe = apply_rope(rmsnorm(q[..., B_NOPE:], q_rope_norm), cos, sin)
    k_nope = rmsnorm(kv[..., :B_NOPE], k_nope_norm)
    v = kv[..., B_NOPE:]
    k_pe = apply_rope(rmsnorm(k_rope, k_rope_norm)[:, :, None, :], cos, sin)
    q_full = jnp.concatenate([q_nope, q_pe], axis=-1)
    k_full = jnp.concatenate([k_nope, jnp.broadcast_to(k_pe, (b, s, B_HEADS, B_ROPE))], axis=-1)
    return block_causal_attention(q_full, k_full, v)


def setup_inputs(seed: int = 0) -> dict:
    key = jax.random.key(seed)
    ks = iter(jax.random.split(key, 32))
    f32 = jnp.float32

    def w(shape, fan_in):
        return jax.random.normal(next(ks), (DEPTH,) + shape, f32) * fan_in ** -0.5

    def gain(n):
        return 1.0 + 0.05 * jax.random.normal(next(ks), (DEPTH, n), f32)

    x = jax.random.normal(next(ks), (BATCH, SEQ, D_MODEL), f32)
    return {
        "x": x,
        "ffn1_norm": gain(D_MODEL),
        "ffn1_w_gate": w((D_MODEL, D_FF), D_MODEL),
        "ffn1_w_up": w((D_MODEL, D_FF), D_MODEL),
        "ffn1_w_down": w((D_FF, D_MODEL), D_FF),
        "mix_norm": gain(D_MODEL),
        "w_in": w((D_MODEL, IN_COLS), D_MODEL),
        "a_q_norm": gain(A_HEAD_DIM),
        "a_k_norm": gain(A_HEAD_DIM),
        "a_rel_bias": 0.5 * jax.random.normal(next(ks), (DEPTH, A_HEADS, 2 * A_MAX_REL + 1), f32),
        "b_q_lat_norm": gain(B_Q_LORA),
        "b_w_uq": w((B_Q_LORA, B_HEADS * B_QK_DIM), B_Q_LORA),
        "b_kv_lat_norm": gain(B_KV_LORA),
        "b_w_ukv": w((B_KV_LORA, B_HEADS * (B_NOPE + B_V_DIM)), B_KV_LORA),
        "b_q_nope_norm": gain(B_NOPE),
        "b_q_rope_norm": gain(B_ROPE),
        "b_k_nope_norm": gain(B_NOPE),
        "b_k_rope_norm": gain(B_ROPE),
        "w_out": w((D_MIX, D_MODEL), D_MIX),
        "ffn2_norm": gain(D_MODEL),
        "ffn2_w_gate": w((D_MODEL, D_FF), D_MODEL),
        "ffn2_w_up": w((D_MODEL, D_FF), D_MODEL),
        "ffn2_w_down": w((D_FF, D_MODEL), D_FF),
        "final_norm": gain(D_MODEL),
    }


def reference(x, ffn1_norm, ffn1_w_gate, ffn1_w_up, ffn1_w_down, mix_norm, w_in,
              a_q_norm, a_k_norm, a_rel_bias, b_q_lat_norm, b_w_uq, b_kv_lat_norm,
              b_w_ukv, b_q_nope_norm, b_q_rope_norm, b_k_nope_norm, b_k_rope_norm,
              w_out, ffn2_norm, ffn2_w_gate, ffn2_w_up, ffn2_w_down, final_norm):
    b, s, _ = x.shape
    cos, sin = rope_tables(s, B_ROPE)
    o_qa, o_ka, o_va = 0, A_WIDTH, 2 * A_WIDTH
    o_cq = 3 * A_WIDTH
    o_ckv = o_cq + B_Q_LORA
    o_kr = o_ckv + B_KV_LORA
    for l in range(DEPTH):
        x = x + 0.5 * swiglu(rmsnorm(x, ffn1_norm[l]), ffn1_w_gate[l], ffn1_w_up[l], ffn1_w_down[l])
        proj = rmsnorm(x, mix_norm[l]) @ w_in[l]
        qa = rmsnorm(proj[..., o_qa:o_ka].reshape(b, s, A_HEADS, A_HEAD_DIM), a_q_norm[l])
        ka = rmsnorm(proj[..., o_ka:o_va].reshape(b, s, A_HEADS, A_HEAD_DIM), a_k_norm[l])
        va = proj[..., o_va:o_cq].reshape(b, s, A_HEADS, A_HEAD_DIM)
        out_a = chunked_relbias_attention(qa, ka, va, a_rel_bias[l])
        out_b = mla_attention(proj[..., o_cq:o_ckv], proj[..., o_ckv:o_kr], proj[..., o_kr:],
                              b_q_lat_norm[l], b_w_uq[l], b_kv_lat_norm[l], b_w_ukv[l],
                              b_q_nope_norm[l], b_q_rope_norm[l], b_k_nope_norm[l],
                              b_k_rope_norm[l], cos, sin)
        x = x + jnp.concatenate([out_a, out_b], axis=-1) @ w_out[l]
        x = x + 0.5 * swiglu(rmsnorm(x, ffn2_norm[l]), ffn2_w_gate[l], ffn2_w_up[l], ffn2_w_down[l])
        x = rmsnorm(x, final_norm[l])
    return x
```

```python
import numpy as np
import concourse.bass as bass
import concourse.mybir as mybir
from concourse.bass_utils import run_bass_kernel_spmd

F32 = mybir.dt.float32
BF16 = mybir.dt.bfloat16
AF = mybir.ActivationFunctionType
ALU = mybir.AluOpType
AX = mybir.AxisListType

ENGS = ("pe", "act", "dve", "pool", "sp")
DMA_RING = 8


class Op:
    __slots__ = ("eng", "fn", "deps", "marked", "ord", "is_dma", "dsem", "dval", "idx", "tag", "is_cc")

    def __init__(self, eng, fn, is_dma, tag=""):
        self.eng = eng
        self.fn = fn
        self.deps = []
        self.marked = False
        self.ord = 0
        self.is_dma = is_dma
        self.dsem = None
        self.dval = 0
        self.idx = 0
        self.tag = tag
        self.is_cc = False


class Buf:
    def __init__(self, sched, name):
        self.name = name
        self.w = {}
        self.r = {}
        self.base = list(sched.barrier_ops)

    def _keys(self, key):
        if key is None:
            return set(self.w.keys()) | set(self.r.keys())
        return {key, None}


class Sched:
    def __init__(self, same_eng_sync=True):
        self.ops = {e: [] for e in ENGS}
        self.barrier_ops = []
        self.same_eng_sync = same_eng_sync
        self.final_ops = []
        self.ndma = {e: 0 for e in ENGS}

    def buf(self, name):
        return Buf(self, name)

    @staticmethod
    def _norm(x):
        if isinstance(x, tuple):
            return x
        return (x, None)

    def add(self, eng, fn, reads=(), writes=(), dma=False, tag="", cc=False):
        o = Op(eng, fn, dma or cc, tag)
        o.is_cc = cc
        deps = {}

        def dep(d):
            if d is None or d is o:
                return
            deps[id(d)] = d

        lane = ("cc", id(o)) if cc else (eng if not dma else ("dma", eng, self.ndma[eng] % DMA_RING))
        for x in reads:
            b, k = self._norm(x)
            for d in b.base:
                dep(d)
            for kk in b._keys(k):
                dep(b.w.get(kk))
        for x in writes:
            b, k = self._norm(x)
            for d in b.base:
                dep(d)
            for kk in b._keys(k):
                dep(b.w.get(kk))
                for d in b.r.get(kk, {}).values():
                    dep(d)
        for x in reads:
            b, k = self._norm(x)
            b.r.setdefault(k, {})[lane] = o
        for x in writes:
            b, k = self._norm(x)
            if k is None:
                b.w = {None: o}
                b.r = {}
            else:
                b.w[k] = o
                b.r[k] = {}
            b.base = []
        for d in deps.values():
            if d.eng == eng and not d.is_dma and not (dma or cc):
                if eng == "pe" or not self.same_eng_sync:
                    continue
            o.deps.append(d)
            d.marked = True
        o.idx = len(self.ops[eng])
        self.ops[eng].append(o)
        if dma:
            self.ndma[eng] += 1
        return o

    def barrier(self):
        bo = []
        for e in ENGS:
            last = None
            dmas = []
            for o in reversed(self.ops[e]):
                if o.is_dma:
                    if len(dmas) < DMA_RING:
                        dmas.append(o)
                elif last is None:
                    last = o
                if last is not None and len(dmas) >= DMA_RING:
                    break
            if last is not None:
                bo.append(last)
            bo.extend(dmas)
        self.barrier_ops = bo

    def finalize_on(self, ops):
        self.final_ops.extend(ops)
        for o in ops:
            o.marked = True

    def emit(self, nc):
        stack_sems = {}
        import contextlib
        with contextlib.ExitStack() as es:
            SEM_LIM = 30000
            nmark = {e: sum(1 for o in self.ops[e] if (o.marked and not o.is_dma)) for e in ENGS}
            esem = {e: [es.enter_context(nc.semaphore("s_%s_%d" % (e, i)))
                        for i in range(max(1, (nmark[e] + SEM_LIM - 1) // SEM_LIM))] for e in ENGS}
            dsem = {e: [es.enter_context(nc.semaphore("d_%s_%d" % (e, i))) for i in range(DMA_RING)]
                    for e in ("sp", "act", "pool") if self.ndma[e] > 0}
            for e in ENGS:
                c = 0
                for o in self.ops[e]:
                    if not o.is_dma and o.marked:
                        c += 1
                    o.ord = c
                k = 0
                for o in self.ops[e]:
                    if o.is_cc:
                        o.dsem = es.enter_context(nc.semaphore("cc_%s_%d" % (e, o.idx)))
                        o.dval = 1
                    elif o.is_dma:
                        o.dsem = dsem[e][k % DMA_RING]
                        o.dval = 16 * (k // DMA_RING + 1)
                        k += 1
            block = es.enter_context(nc.Block())

            def run(e, eng):
                waited = {}

                def wait(sem, val):
                    key = id(sem)
                    if waited.get(key, 0) >= val:
                        return
                    waited[key] = val
                    eng.wait_ge(sem, val)

                for o in self.ops[e]:
                    for d in o.deps:
                        if d.is_dma:
                            wait(d.dsem, d.dval)
                        else:
                            wait(esem[d.eng][(d.ord - 1) // SEM_LIM], (d.ord - 1) % SEM_LIM + 1)
                    if o.is_cc:
                        ins = o.fn(eng)
                        ins.then_inc(o.dsem)
                    elif o.is_dma:
                        if o.dval > 16:
                            wait(o.dsem, o.dval - 16)
                        ins = o.fn(eng)
                        ins.then_inc(o.dsem, 16)
                    else:
                        ins = o.fn(eng)
                        if o.marked:
                            ins.then_inc(esem[e][(o.ord - 1) // SEM_LIM], 1)
                if e == "sp":
                    for d in self.final_ops:
                        if d.is_dma:
                            wait(d.dsem, d.dval)
                        else:
                            wait(esem[d.eng][(d.ord - 1) // SEM_LIM], (d.ord - 1) % SEM_LIM + 1)

            @block.tensor
            def _(eng):
                run("pe", eng)

            @block.scalar
            def _(eng):
                run("act", eng)

            @block.vector
            def _(eng):
                run("dve", eng)

            @block.gpsimd
            def _(eng):
                run("pool", eng)

            @block.sync
            def _(eng):
                run("sp", eng)

import contextlib

D = 1024
DFF = 2816
NFT = 22
INC = 1952
NBLK = 64
NOWN = 32
EPS = 1e-6
SB_LO, SB_HI = 16512, 229344


class KB:
    def __init__(self, nc):
        self.nc = nc
        self.S = Sched()
        self.off = SB_LO
        self.phase_base = SB_LO
        self.uid = 0

    def sb(self, name, shape, dt):
        sz = int(np.prod(shape[1:])) * (4 if dt == F32 else 2)
        sz = (sz + 63) // 64 * 64
        self.uid += 1
        t = self.nc.alloc_sbuf_tensor_at("%s_%d" % (name, self.uid), list(shape), dt, offset=self.off)
        self.off += sz
        assert self.off <= SB_HI, ("SBUF overflow", name, self.off)
        return t, self.S.buf(name)

    def persist_done(self):
        self.phase_base = self.off

    def new_phase(self, keep=None):
        self.S.barrier()
        self.off = self.phase_base if keep is None else keep


def build_program(debug=False):
    nc = bass.Bass("TRN2", target_bir_lowering=False)
    k = KB(nc)
    S = k.S
    A = S.add

    def din(name, shape, dt=F32):
        return nc.dram_tensor(name, list(shape), dt, kind="ExternalInput").ap()

    xs = din("xs", [NOWN * 128, D])
    cs = din("cs", [NOWN * 128, 64])
    biasA = din("biasA", [128, 8 * 6 * 128])
    maskO_in = din("maskO", [128, 128])
    maskE_in = din("maskE", [128, 128])
    ident_in = din("ident", [128, 128])
    w = {}
    for nm, shp in [("ffn1_norm", [1, D]), ("ffn1_w_gate", [D, DFF]), ("ffn1_w_up", [D, DFF]), ("ffn1_w_down", [DFF, D]),
                    ("mix_norm", [1, D]), ("w_in", [D, INC]), ("a_q_norm", [1, 64]), ("a_k_norm", [1, 64]),
                    ("b_q_lat_norm", [1, 256]), ("b_w_uq", [256, 768]), ("b_kv_lat_norm", [1, 128]), ("b_w_ukv", [128, 1024]),
                    ("b_q_nope_norm", [1, 64]), ("b_q_rope_norm", [1, 32]), ("b_k_nope_norm", [1, 64]), ("b_k_rope_norm", [1, 32]),
                    ("w_out", [D, D]), ("ffn2_norm", [1, D]), ("ffn2_w_gate", [D, DFF]), ("ffn2_w_up", [D, DFF]),
                    ("ffn2_w_down", [DFF, D]), ("final_norm", [1, D])]:
        w[nm] = din(nm, shp)
    okind = "ExternalOutput"
    out = nc.dram_tensor("out", [NOWN * 128, D], F32, kind=okind).ap()
    skind = "ExternalOutput" if debug else "Internal"

    def dscr(name, shape, dt):
        t = nc.dram_tensor(name, list(shape), dt, kind=skind) if debug else nc.dram_tensor(name, list(shape), dt)
        return t.ap(), S.buf(name)

    x1_s, bx1_s = dscr("x1_s", [NOWN * 128, D], F32)
    hT_s, bhT_s = dscr("hT_s", [16, 128, 8 * 256], BF16)
    CH = 655360
    OFF_KA, OFF_VA, OFF_KB, OFF_VB = 0, 131072, 294912, 491520
    kc, bkc = dscr("kc", [16 * 640, 1024], BF16)
    kg, bkg = dscr("kg", [16 * 1280, 1024], BF16)
    kcf = kc.rearrange("r c -> (r c)")
    kgf = kg.rearrange("r c -> (r c)")

    def kc_ka(j):
        o = (j // 2) * CH + OFF_KA + (j % 2) * 65536
        return kcf[o:o + 65536].rearrange("(p c) -> p c", c=512)

    def kc_va(j):
        o = (j // 2) * CH + OFF_VA + (j % 2) * 81920
        return kcf[o:o + 81920].rearrange("(p c) -> p c", c=640)

    def kc_kb(j):
        o = (j // 2) * CH + OFF_KB
        return kcf[o:o + 196608].rearrange("(h p t) -> p h t", h=8, p=96)[:, :, (j % 2) * 128:(j % 2) * 128 + 128]

    def kc_vb(j):
        o = (j // 2) * CH + OFF_VB
        return kcf[o:o + 163840].rearrange("(h p t) -> p h t", h=8, p=128)[:, :, (j % 2) * 80:(j % 2) * 80 + 80]

    def kg_ka(r_, m):
        o = ((m // 2) * 2 + r_) * CH + OFF_KA + (m % 2) * 65536
        return kgf[o:o + 65536].rearrange("(p c) -> p c", c=512)

    def kg_va(r_, m):
        o = ((m // 2) * 2 + r_) * CH + OFF_VA + (m % 2) * 81920
        return kgf[o:o + 81920].rearrange("(p c) -> p c", c=640)

    kgv = kgf.rearrange("(g r x) -> g r x", g=16, r=2)

    def kg_kb(h, r_):
        return kgv[:, r_, OFF_KB + h * 24576:OFF_KB + (h + 1) * 24576].rearrange("g (p t) -> p g t", t=256)

    def kg_vb(h, r_):
        return kgv[:, r_, OFF_VB + h * 20480:OFF_VB + (h + 1) * 20480].rearrange("g (p t) -> p g t", t=160)

    qaT_s, bqaT_s = dscr("qaT_s", [NOWN, 128, 512], BF16)
    qbT_s, bqbT_s = dscr("qbT_s", [8, 96, NOWN * 128], BF16)
    oTA_s, boTA_s = dscr("oTA_s", [8, 64, NOWN * 128], BF16)
    oTB_s, boTB_s = dscr("oTB_s", [8, 64, NOWN * 128], BF16)

    P = [nc.alloc_psum_tensor("P%d" % i, [128, 1024], F32) for i in range(4)]
    Pb = [p.bitcast(BF16) for p in P]
    pbuf = [S.buf("bank%d" % i) for i in range(8)]

    def bank(i):
        return P[i // 2][:, (i % 2) * 512:(i % 2) * 512 + 512]

    def bankb(i):
        return Pb[i // 2][:, (i % 2) * 1024:(i % 2) * 1024 + 1024]

    ident, bident = k.sb("ident", [128, 128], BF16)
    identf, bidentf = k.sb("identf", [128, 128], F32)
    onesf, bonesf = k.sb("onesf", [128, 64], F32)
    maskO, bmaskO = k.sb("maskO", [128, 128], BF16)
    A("sp", lambda e: e.dma_start(out=identf[:], in_=ident_in), writes=[bidentf], dma=True)
    A("dve", lambda e: e.tensor_copy(out=ident[:], in_=identf[:]), reads=[bidentf], writes=[bident])
    A("sp", lambda e: e.dma_start(out=identf[:], in_=maskO_in), reads=[bidentf], writes=[bidentf], dma=True)
    A("dve", lambda e: e.tensor_copy(out=maskO[:], in_=identf[:]), reads=[bidentf], writes=[bmaskO])
    maskE, bmaskE = k.sb("maskE", [128, 128], BF16)
    A("sp", lambda e: e.dma_start(out=identf[:], in_=maskE_in), reads=[bidentf], writes=[bidentf], dma=True)
    A("dve", lambda e: e.tensor_copy(out=maskE[:], in_=identf[:]), reads=[bidentf], writes=[bmaskE])
    A("pool", lambda e: e.memset(onesf[:], 1.0), writes=[bonesf])
    k.persist_done()

    def load_w(dst, bdst, src, kt_n, ncols, rows_per=128):
        v = src.rearrange("(kt p) f -> p kt f", p=128)
        step = 1 if ncols >= 1024 else kt_n
        for k0 in range(0, kt_n, step):
            k1 = min(kt_n, k0 + step)
            A("pool", lambda e, k0=k0, k1=k1: e.dma_start(out=dst[:, k0:k1, :], in_=v[:, k0:k1, :]),
              writes=[(bdst, k0)], dma=True)

    FCH = 704

    def load_w_cols(dsts, srcs):
        for c_ in range(DFF // FCH):
            for (dst, bdst), src in zip(dsts, srcs):
                v = src.rearrange("(kt p) f -> p kt f", p=128)
                A("pool", lambda e, dst=dst, v=v, c_=c_: e.dma_start(out=dst[:, :, c_ * FCH:(c_ + 1) * FCH], in_=v[:, :, c_ * FCH:(c_ + 1) * FCH]),
                  writes=[(bdst, c_)], dma=True)

    def load_bc(dst, bdst, src, n):
        A("sp", lambda e: e.dma_start(out=dst, in_=src.partition_broadcast(128)), writes=[bdst], dma=True)

    def norm_chain(src, bsrc, skey, gbc, bgbc, dst, bdst, dkey, tl):
        (ss, bss, rs, brs, junk, bjunk) = tl
        A("dve", lambda e: e.memset(ss[:, 0:1], 0.0), writes=[bss])
        A("act", lambda e: e.activation(out=junk[:], in_=src, func=AF.Square, accum_out=ss[:, 0:1]),
          reads=[(bsrc, skey), bss], writes=[bjunk, bss])
        A("act", lambda e: e.activation(out=rs[:, 0:1], in_=ss[:, 0:1], func=AF.Sqrt, bias=EPS, scale=1.0 / D), reads=[bss], writes=[brs])
        A("dve", lambda e: e.reciprocal(out=rs[:, 0:1], in_=rs[:, 0:1]), reads=[brs], writes=[brs])
        A("dve", lambda e: e.scalar_tensor_tensor(out=dst, in0=src, scalar=rs[:, 0:1], in1=gbc[:], op0=ALU.mult, op1=ALU.mult),
          reads=[(bsrc, skey), brs, bgbc], writes=[(bdst, dkey)])

    def transposes_to(xn, bxn, dstT, bdstT, dkey):
        pt = bankb(0)
        for kt in range(8):
            A("pe", lambda e, kt=kt: e.transpose(out=pt[:, kt * 128:(kt + 1) * 128], in_=xn[:, kt * 128:(kt + 1) * 128], identity=ident[:]),
              reads=[bxn, bident], writes=[pbuf[0]])
        A("act", lambda e: e.copy(out=dstT, in_=pt.rearrange("p (k t) -> p k t", k=8)), reads=[pbuf[0]], writes=[(bdstT, dkey)])

    def ffn_group(xg, bxg, xnT, bxnT, Wg, bWg, Wu, bWu, Wd, bWd, T, hooks):
        (actT, bactT, sg, bsg) = T
        for ft in range(NFT):
            for fn in hooks.get(ft, []):
                fn()
            bg = 1 + (ft % 2)
            bu = 3 + (ft % 2)
            pg = bank(bg)[:, 0:256]
            pu = bank(bu)[:, 0:256]
            for kt in range(8):
                A("pe", lambda e, kt=kt, ft=ft, pg=pg: e.matmul(pg, lhsT=Wg[:, kt, ft * 128:(ft + 1) * 128], rhs=xnT[:, kt, :],
                                                               start=(kt == 0), stop=(kt == 7)),
                  reads=[(bWg, (ft * 128) // FCH), (bWg, (ft * 128 + 127) // FCH), bxnT], writes=[(pbuf[bg], 0)])
            for kt in range(8):
                A("pe", lambda e, kt=kt, ft=ft, pu=pu: e.matmul(pu, lhsT=Wu[:, kt, ft * 128:(ft + 1) * 128], rhs=xnT[:, kt, :],
                                                               start=(kt == 0), stop=(kt == 7)),
                  reads=[(bWu, (ft * 128) // FCH), (bWu, (ft * 128 + 127) // FCH), bxnT], writes=[(pbuf[bu], 1)])
            A("act", lambda e, pg=pg, ft=ft: e.activation(out=sg[:, ft % 2, :], in_=pg, func=AF.Silu), reads=[(pbuf[bg], 0)], writes=[(bsg, ft % 2)])
            A("dve", lambda e, pu=pu, ft=ft: e.tensor_tensor(out=actT[:, ft, :], in0=pu, in1=sg[:, ft % 2, :], op=ALU.mult),
              reads=[(pbuf[bu], 1), (bsg, ft % 2)], writes=[(bactT, ft)])
        i = 0
        for t_ in range(2):
            for hf in range(2):
                bd = 5 + (i % 2)
                i += 1
                pd = bank(bd)
                for ft in range(NFT):
                    A("pe", lambda e, ft=ft, t_=t_, hf=hf, pd=pd: e.matmul(pd, lhsT=actT[:, ft, t_ * 128:(t_ + 1) * 128],
                                                                         rhs=Wd[:, ft, hf * 512:(hf + 1) * 512],
                                                                         start=(ft == 0), stop=(ft == NFT - 1)),
                      reads=[bactT, (bWd, ft)], writes=[pbuf[bd]])
                A("dve", lambda e, t_=t_, hf=hf, pd=pd: e.scalar_tensor_tensor(out=xg[:, t_, hf * 512:(hf + 1) * 512], in0=pd, scalar=0.5,
                                                                               in1=xg[:, t_, hf * 512:(hf + 1) * 512], op0=ALU.mult, op1=ALU.add),
                  reads=[pbuf[bd], (bxg, t_)], writes=[(bxg, t_)])

    def ffn_tiles():
        actT, bactT = k.sb("actT", [128, NFT, 256], BF16)
        sg, bsg = k.sb("sg", [128, 2, 256], F32)
        ss, bss = k.sb("ss", [128, 2], F32)
        rs, brs = k.sb("rs", [128, 2], F32)
        xns = [k.sb("xn%d" % i, [128, D], BF16) for i in range(2)]
        xnTs = [k.sb("xnT%d" % i, [128, 8, 256], BF16) for i in range(2)]
        return (actT, bactT, sg, bsg), (ss, bss, rs, brs), xns, xnTs

    def ffn_weights_gu(pfx):
        Wg, bWg = k.sb("Wg", [128, 8, DFF], BF16)
        Wu, bWu = k.sb("Wu", [128, 8, DFF], BF16)
        load_w_cols([(Wg, bWg), (Wu, bWu)], [w[pfx + "_w_gate"], w[pfx + "_w_up"]])
        return Wg, bWg, Wu, bWu

    def ffn_weights_d(pfx):
        Wd, bWd = k.sb("Wd", [128, NFT, D], BF16)
        load_w(Wd, bWd, w[pfx + "_w_down"], NFT, D)
        return Wd, bWd

    def ffn_weights(pfx):
        Wg, bWg, Wu, bWu = ffn_weights_gu(pfx)
        Wd, bWd = ffn_weights_d(pfx)
        return Wg, bWg, Wu, bWu, Wd, bWd

    k.new_phase()
    Wg, bWg, Wu, bWu, Wd, bWd = ffn_weights("ffn1")
    g1, bg1 = k.sb("g1", [128, D], F32)
    gm, bgm = k.sb("gm", [128, D], F32)
    load_bc(g1[:], bg1, w["ffn1_norm"], D)
    load_bc(gm[:], bgm, w["mix_norm"], D)
    T, (ssf, bssf, rsf, brsf), xns, xnTs = ffn_tiles()
    xgs = [k.sb("xg%d" % i, [128, 2, D], F32) for i in range(2)]
    hTs = [k.sb("hT%d" % i, [128, 8, 256], BF16) for i in range(2)]
    xs_v = xs.rearrange("(g t p) c -> g p t c", t=2, p=128)
    x1_v = x1_s.rearrange("(g t p) c -> g p t c", t=2, p=128)
    NG1 = 16
    cnt = [0]

    def p1a_load(g):
        xg, bxg = xgs[g % 2]
        A("sp", lambda e: e.dma_start(out=xg[:], in_=xs_v[g]), writes=[bxg], dma=True)

    def p1a_pre_chain(g, t_):
        xg, bxg = xgs[g % 2]
        xn, bxn = xns[cnt[0] % 2]
        cnt[0] += 1
        norm_chain(xg[:, t_, :], bxg, t_, g1, bg1, xn[:], bxn, None, (ssf, bssf, rsf, brsf, xn, bxn))
        return xn, bxn

    def p1a_post_chain(g, t_):
        xg, bxg = xgs[g % 2]
        xn, bxn = xns[cnt[0] % 2]
        cnt[0] += 1
        norm_chain(xg[:, t_, :], bxg, t_, gm, bgm, xn[:], bxn, None, (ssf, bssf, rsf, brsf, xn, bxn))
        return xn, bxn

    def p1a_hooks(g):
        hk = {}
        st = {}
        if g >= 1:
            gp = g - 1
            hTp, bhTp = hTs[gp % 2]
            hk.setdefault(0, []).append(lambda: st.__setitem__("e0", p1a_post_chain(gp, 0)))
            hk.setdefault(5, []).append(lambda: transposes_to(st["e0"][0], st["e0"][1], hTp[:, :, 0:128], bhTp, 0))
            hk.setdefault(2, []).append(lambda: st.__setitem__("e1", p1a_post_chain(gp, 1)))

            def fin():
                transposes_to(st["e1"][0], st["e1"][1], hTp[:, :, 128:256], bhTp, 1)
                A("sp", lambda e: e.dma_start(out=hT_s[gp], in_=hTp[:].rearrange("p k t -> p (k t)")), reads=[bhTp], writes=[(bhT_s, gp)], dma=True)
            hk.setdefault(7, []).append(fin)
        if g + 1 < NG1:
            gn = g + 1
            xnTn, bxnTn = xnTs[gn % 2]
            hk.setdefault(3, []).append(lambda: p1a_load(gn))
            hk.setdefault(9, []).append(lambda: st.__setitem__("q0", p1a_pre_chain(gn, 0)))
            hk.setdefault(14, []).append(lambda: transposes_to(st["q0"][0], st["q0"][1], xnTn[:, :, 0:128], bxnTn, 0))
            hk.setdefault(11, []).append(lambda: st.__setitem__("q1", p1a_pre_chain(gn, 1)))
            hk.setdefault(16, []).append(lambda: transposes_to(st["q1"][0], st["q1"][1], xnTn[:, :, 128:256], bxnTn, 1))
        return hk

    p1a_load(0)
    for t_ in range(2):
        xn, bxn = p1a_pre_chain(0, t_)
        transposes_to(xn, bxn, xnTs[0][0][:, :, t_ * 128:(t_ + 1) * 128], xnTs[0][1], t_)
    for g in range(NG1):
        xg, bxg = xgs[g % 2]
        xnT, bxnT = xnTs[g % 2]
        ffn_group(xg, bxg, xnT, bxnT, Wg, bWg, Wu, bWu, Wd, bWd, T, p1a_hooks(g))
        if True:
            A("sp", lambda e, g=g, xg=xg: e.dma_start(out=x1_v[g], in_=xg[:]), reads=[bxg], writes=[(bx1_s, g)], dma=True)
    gp = NG1 - 1
    hTp, bhTp = hTs[gp % 2]
    for t_ in range(2):
        xn, bxn = p1a_post_chain(gp, t_)
        transposes_to(xn, bxn, hTp[:, :, t_ * 128:(t_ + 1) * 128], bhTp, t_)
    A("sp", lambda e: e.dma_start(out=hT_s[gp], in_=hTp[:].rearrange("p k t -> p (k t)")), reads=[bhTp], writes=[(bhT_s, gp)], dma=True)

    k.new_phase()
    Win, bWin = k.sb("Win", [128, 8, INC], BF16)
    Wuq, bWuq = k.sb("Wuq", [128, 2, 768], BF16)
    Wukv, bWukv = k.sb("Wukv", [128, 1, 1024], BF16)
    load_w(Win, bWin, w["w_in"], 8, INC)
    load_w(Wuq, bWuq, w["b_w_uq"], 2, 768)
    load_w(Wukv, bWukv, w["b_w_ukv"], 1, 1024)
    graw, bgraw = k.sb("graw", [128, 704], F32)
    goff = {}
    o_ = 0
    for nm, n_ in [("a_q_norm", 64), ("a_k_norm", 64), ("b_q_lat_norm", 256), ("b_kv_lat_norm", 128), ("b_q_nope_norm", 64),
                   ("b_q_rope_norm", 32), ("b_k_nope_norm", 64), ("b_k_rope_norm", 32)]:
        goff[nm] = (o_, n_)
        A("sp", lambda e, o_=o_, n_=n_, nm=nm: e.dma_start(out=graw[:, o_:o_ + n_], in_=w[nm].partition_broadcast(128)),
          writes=[(bgraw, nm)], dma=True)
        o_ += n_

    def gain_full(name, nm, H, scale):
        o0, n_ = goff[nm]
        gt, bgt = k.sb(name, [128, H, n_], F32)
        A("dve", lambda e: e.tensor_copy(out=gt[:], in_=graw[:, o0:o0 + n_].unsqueeze(1).to_broadcast([128, H, n_])),
          reads=[(bgraw, nm)], writes=[bgt])
        if scale != 1.0:
            A("dve", lambda e: e.tensor_scalar_mul(out=gt[:], in0=gt[:], scalar1=float(scale)), reads=[bgt], writes=[bgt])
        return gt, bgt

    gqa, bgqa = gain_full("gqa", "a_q_norm", 8, 0.125)
    gka, bgka = gain_full("gka", "a_k_norm", 8, 1.0)
    A("dve", lambda e: e.tensor_tensor(out=gqa[:], in0=gqa[:], in1=gka[:], op=ALU.mult), reads=[bgqa, bgka], writes=[bgqa])
    gcq, bgcq = gain_full("gcq", "b_q_lat_norm", 1, 1.0)
    gckv, bgckv = gain_full("gckv", "b_kv_lat_norm", 1, 1.0)
    SCB = 96.0 ** -0.5
    gqn, bgqn = gain_full("gqn", "b_q_nope_norm", 8, SCB)
    gqr, bgqr = gain_full("gqr", "b_q_rope_norm", 8, SCB)
    gkn, bgkn = gain_full("gkn", "b_k_nope_norm", 8, 1.0)
    A("dve", lambda e: e.tensor_tensor(out=gqn[:], in0=gqn[:], in1=gkn[:], op=ALU.mult), reads=[bgqn, bgkn], writes=[bgqn])
    gkr, bgkr = gain_full("gkr", "b_k_rope_norm", 1, 1.0)

    hT2s = [k.sb("hT2_%d" % i, [128, 8, 256], BF16) for i in range(2)]

    class TS:
        pass

    def mk_ts(si):
        t = TS()
        for nm, shp, dt in [("cst", [128, 64], F32), ("sq", [128, 1024], F32), ("tmp", [128, 1024], F32), ("ss", [128, 16], F32),
                            ("rs", [128, 16], F32), ("pj", [128, 1952], F32), ("qan", [128, 512], BF16), ("kan", [128, 512], BF16),
                            ("qaT", [128, 512], BF16), ("kaT", [128, 512], BF16), ("vaS", [128, 8, 80], BF16), ("vbS", [128, 8, 80], BF16),
                            ("cqn", [128, 256], BF16), ("cqnT", [128, 2, 128], BF16), ("qf", [128, 8, 96], F32), ("qfull", [128, 8, 96], BF16),
                            ("qr", [128, 8, 32], F32), ("rt1", [128, 8, 32], F32), ("rt2", [128, 8, 32], F32),
                            ("qbT", [96, 8, 128], BF16), ("kbT", [96, 8, 128], BF16), ("ckvn", [128, 128], BF16), ("ckvnT", [128, 128], BF16),
                            ("kvf", [128, 8, 128], F32), ("kfull", [128, 8, 96], BF16), ("krn", [128, 1, 32], F32), ("kpe", [128, 32], BF16)]:
            tt, bb = k.sb("%s_s%d" % (nm, si), shp, dt)
            setattr(t, nm, tt)
            setattr(t, "b" + nm, bb)
        for nm in ("vaS", "vbS"):
            tt, bb = getattr(t, nm), getattr(t, "b" + nm)
            A("pool", lambda e, tt=tt: e.memset(tt[:], 0.0), writes=[bb])
            A("pool", lambda e, tt=tt: e.memset(tt[:, :, 64:65], 1.0), reads=[bb], writes=[bb])
        t.b2 = [2 + 2 * si, 2 + 2 * si]
        t.tb = 3 + 2 * si
        return t

    NSL = 3
    tsl = [mk_ts(i) for i in range(NSL)]
    pjctr = [0]

    def headnorm(t, src3, rd, H, Dh, gain3, bgain, out3, wr):
        sq3 = t.sq[:, 0:H * Dh].rearrange("p (h d) -> p h d", h=H)
        tmp3 = t.tmp[:, 0:H * Dh].rearrange("p (h d) -> p h d", h=H)
        if H == 1:
            A("pool", lambda e: e.memset(t.ss[:, 0:1], 0.0), writes=[t.bss])
            A("act", lambda e: e.activation(out=sq3, in_=src3, func=AF.Square, accum_out=t.ss[:, 0:1]), reads=rd + [t.bss], writes=[t.bsq, t.bss])
            yield
        else:
            A("act", lambda e: e.activation(out=sq3, in_=src3, func=AF.Square), reads=rd, writes=[t.bsq])
            yield
            A("dve", lambda e: e.tensor_reduce(out=t.ss[:, 0:H], in_=sq3, axis=AX.X, op=ALU.add), reads=[t.bsq], writes=[t.bss])
            yield
        A("act", lambda e: e.activation(out=t.rs[:, 0:H], in_=t.ss[:, 0:H], func=AF.Sqrt, bias=EPS, scale=1.0 / Dh), reads=[t.bss], writes=[t.brs])
        yield
        A("dve", lambda e: e.reciprocal(out=t.rs[:, 0:H], in_=t.rs[:, 0:H]), reads=[t.brs], writes=[t.brs])
        yield
        if H == 1:
            A("dve", lambda e: e.scalar_tensor_tensor(out=out3[:, 0, :], in0=src3[:, 0, :], scalar=t.rs[:, 0:1], in1=gain3[:, 0, :],
                                                      op0=ALU.mult, op1=ALU.mult), reads=rd + [t.brs, bgain], writes=wr)
            yield
        elif gain3 is None:
            A("dve", lambda e: e.tensor_tensor(out=out3, in0=src3, in1=t.rs[:, 0:H].unsqueeze(2).to_broadcast([128, H, Dh]), op=ALU.mult),
              reads=rd + [t.brs], writes=wr)
            yield
        else:
            A("dve", lambda e: e.tensor_tensor(out=tmp3, in0=src3, in1=t.rs[:, 0:H].unsqueeze(2).to_broadcast([128, H, Dh]), op=ALU.mult),
              reads=rd + [t.brs], writes=[t.btmp])
            yield
            A("pool", lambda e: e.tensor_tensor(out=out3, in0=tmp3, in1=gain3, op=ALU.mult), reads=[t.btmp, bgain], writes=wr)
            yield

    def rope(t, src3, rd, H, out3, wr):
        Cb = t.cst[:, 0:32].unsqueeze(1).to_broadcast([128, H, 32])
        S1 = t.cst[:, 32:48].unsqueeze(1).to_broadcast([128, H, 16])
        S2 = t.cst[:, 48:64].unsqueeze(1).to_broadcast([128, H, 16])
        A("dve", lambda e: e.tensor_tensor(out=t.rt1[:, 0:H, :], in0=src3, in1=Cb, op=ALU.mult), reads=rd + [t.bcst], writes=[t.brt1])
        A("pool", lambda e: e.tensor_tensor(out=t.rt2[:, 0:H, 0:16], in0=src3[:, :, 16:32], in1=S1, op=ALU.mult), reads=rd + [t.bcst], writes=[(t.brt2, 0)])
        A("pool", lambda e: e.tensor_tensor(out=t.rt2[:, 0:H, 16:32], in0=src3[:, :, 0:16], in1=S2, op=ALU.mult), reads=rd + [t.bcst], writes=[(t.brt2, 1)])
        yield
        A("dve", lambda e: e.tensor_tensor(out=out3, in0=t.rt1[:, 0:H, :], in1=t.rt2[:, 0:H, :], op=ALU.add), reads=[t.brt1, t.brt2], writes=wr)
        yield

    def blockgen(j, si):
        t = tsl[si]
        own = j < NOWN
        g = j // 2
        tb = j % 2
        hT2, bhT2 = hT2s[g % 2]
        if tb == 0:
            A("sp", lambda e: e.dma_start(out=hT2[:].rearrange("p k t -> p (k t)"), in_=hT_s[g]), reads=[(bhT_s, g)], writes=[bhT2], dma=True)
        A("sp", lambda e: e.dma_start(out=t.cst[:], in_=cs[j * 128:(j + 1) * 128, :]), writes=[t.bcst], dma=True)
        yield
        if own:
            chunks = [(0, 512, "pj"), (512, 1024, "pj"), (1024, 1536, "va"), (1536, 1952, "pj")]
            o_cq, o_ckv, o_kr = 1536, 1792, 1920
        else:
            chunks = [(512, 1024, "pj"), (1024, 1536, "va"), (1792, 1952, "pj")]
            o_cq, o_ckv, o_kr = None, 1792, 1920
        ptb = bankb(t.tb)
        btb = pbuf[t.tb]
        for (c0, c1, dst) in chunks:
            b_ = pjctr[0] % 2
            pjctr[0] += 1
            for kt in range(8):
                A("pe", lambda e, b_=b_, c0=c0, c1=c1, kt=kt: e.matmul(
                    bank(b_)[:, 0:c1 - c0], lhsT=hT2[:, kt, tb * 128:(tb + 1) * 128], rhs=Win[:, kt, c0:c1],
                    start=(kt == 0), stop=(kt == 7)), reads=[bhT2, (bWin, kt)], writes=[pbuf[b_]])
            if dst == "pj":
                A("act", lambda e, b_=b_, c0=c0, c1=c1: e.copy(out=t.pj[:, c0:c1], in_=bank(b_)[:, 0:c1 - c0]), reads=[pbuf[b_]], writes=[(t.bpj, c0)])
            else:
                A("act", lambda e, b_=b_: e.copy(out=t.vaS[:, :, 0:64], in_=bank(b_).rearrange("p (h d) -> p h d", h=8)),
                  reads=[pbuf[b_]], writes=[t.bvaS])
                A("sp", lambda e: e.dma_start(out=kc_va(j), in_=t.vaS[:].rearrange("p h d -> p (h d)")), reads=[t.bvaS], writes=[(bkc, g)], dma=True)
            yield
        if own:
            yield from headnorm(t, t.pj[:, 0:512].rearrange("p (h d) -> p h d", h=8), [(t.bpj, 0)], 8, 64, gqa[:], bgqa,
                                t.qan[:].rearrange("p (h d) -> p h d", h=8), [t.bqan])
            for t_ in range(4):
                A("pe", lambda e, t_=t_: e.transpose(out=ptb[:, t_ * 128:(t_ + 1) * 128], in_=t.qan[:, t_ * 128:(t_ + 1) * 128], identity=ident[:]),
                  reads=[t.bqan, bident], writes=[btb])
            yield
            A("act", lambda e: e.copy(out=t.qaT[:], in_=ptb[:, 0:512]), reads=[btb], writes=[t.bqaT])
            A("sp", lambda e: e.dma_start(out=qaT_s[j], in_=t.qaT[:]), reads=[t.bqaT], writes=[(bqaT_s, j)], dma=True)
            yield
        yield from headnorm(t, t.pj[:, 512:1024].rearrange("p (h d) -> p h d", h=8), [(t.bpj, 512)], 8, 64, None, None,
                            t.kan[:].rearrange("p (h d) -> p h d", h=8), [t.bkan])
        for t_ in range(4):
            A("pe", lambda e, t_=t_: e.transpose(out=ptb[:, t_ * 128:(t_ + 1) * 128], in_=t.kan[:, t_ * 128:(t_ + 1) * 128], identity=ident[:]),
              reads=[t.bkan, bident], writes=[btb])
        yield
        A("act", lambda e: e.copy(out=t.kaT[:], in_=ptb[:, 0:512]), reads=[btb], writes=[t.bkaT])
        A("sp", lambda e: e.dma_start(out=kc_ka(j), in_=t.kaT[:]), reads=[t.bkaT], writes=[(bkc, g)], dma=True)
        yield
        pjk = (t.bpj, 1536 if own else 1792)
        if own:
            yield from headnorm(t, t.pj[:, o_cq:o_cq + 256].unsqueeze(1), [pjk], 1, 256, gcq[:], bgcq, t.cqn[:].unsqueeze(1), [t.bcqn])
            for t_ in range(2):
                A("pe", lambda e, t_=t_: e.transpose(out=ptb[:, t_ * 128:(t_ + 1) * 128], in_=t.cqn[:, t_ * 128:(t_ + 1) * 128], identity=ident[:]),
                  reads=[t.bcqn, bident], writes=[btb])
            yield
            A("act", lambda e: e.copy(out=t.cqnT[:].rearrange("p k t -> p (k t)"), in_=ptb[:, 0:256]), reads=[btb], writes=[t.bcqnT])
            yield
            for c_ in range(2):
                bb_ = t.b2[c_]
                for kt in range(2):
                    A("pe", lambda e, c_=c_, kt=kt, bb_=bb_: e.matmul(bank(bb_)[:, 0:384], lhsT=t.cqnT[:, kt, :], rhs=Wuq[:, kt, c_ * 384:(c_ + 1) * 384],
                                                                     start=(kt == 0), stop=(kt == 1)), reads=[t.bcqnT, bWuq], writes=[pbuf[bb_]])
                A("act", lambda e, c_=c_, bb_=bb_: e.copy(out=t.qf[:, c_ * 4:(c_ + 1) * 4, :], in_=bank(bb_)[:, 0:384].rearrange("p (h d) -> p h d", h=4)),
                  reads=[pbuf[bb_]], writes=[(t.bqf, c_)])
            yield
            yield from headnorm(t, t.qf[:, :, 0:64], [t.bqf], 8, 64, gqn[:], bgqn, t.qfull[:, :, 0:64], [(t.bqfull, 0)])
            yield from headnorm(t, t.qf[:, :, 64:96], [t.bqf], 8, 32, gqr[:], bgqr, t.qr[:], [t.bqr])
            yield from rope(t, t.qr[:], [t.bqr], 8, t.qfull[:, :, 64:96], [(t.bqfull, 1)])
            for h in range(8):
                A("pe", lambda e, h=h: e.transpose(out=ptb[0:96, h * 128:(h + 1) * 128], in_=t.qfull[:, h, :], identity=ident[:]),
                  reads=[t.bqfull, bident], writes=[btb])
            yield
            A("act", lambda e: e.copy(out=t.qbT[:].rearrange("p h t -> p (h t)"), in_=ptb[0:96, :]), reads=[btb], writes=[t.bqbT])
            A("sp", lambda e: e.dma_start(out=qbT_s[:, :, j * 128:(j + 1) * 128].rearrange("h p t -> p h t"), in_=t.qbT[:]),
              reads=[t.bqbT], writes=[(bqbT_s, j)], dma=True)
            yield
        yield from headnorm(t, t.pj[:, o_ckv:o_ckv + 128].unsqueeze(1), [pjk], 1, 128, gckv[:], bgckv, t.ckvn[:].unsqueeze(1), [t.bckvn])
        A("pe", lambda e: e.transpose(out=ptb[:, 0:128], in_=t.ckvn[:], identity=ident[:]), reads=[t.bckvn, bident], writes=[btb])
        yield
        A("act", lambda e: e.copy(out=t.ckvnT[:], in_=ptb[:, 0:128]), reads=[btb], writes=[t.bckvnT])
        yield
        for c_ in range(2):
            bb_ = t.b2[c_]
            A("pe", lambda e, c_=c_, bb_=bb_: e.matmul(bank(bb_), lhsT=t.ckvnT[:], rhs=Wukv[:, 0, c_ * 512:(c_ + 1) * 512], start=True, stop=True),
              reads=[t.bckvnT, bWukv], writes=[pbuf[bb_]])
            A("act", lambda e, c_=c_, bb_=bb_: e.copy(out=t.kvf[:, c_ * 4:(c_ + 1) * 4, :], in_=bank(bb_).rearrange("p (h d) -> p h d", h=4)),
              reads=[pbuf[bb_]], writes=[(t.bkvf, c_)])
        yield
        A("dve", lambda e: e.tensor_copy(out=t.vbS[:, :, 0:64], in_=t.kvf[:, :, 64:128]), reads=[t.bkvf], writes=[t.bvbS])
        A("sp", lambda e: e.dma_start(out=kc_vb(j), in_=t.vbS[:]), reads=[t.bvbS], writes=[(bkc, g)], dma=True)
        yield from headnorm(t, t.kvf[:, :, 0:64], [t.bkvf], 8, 64, None, None, t.kfull[:, :, 0:64], [(t.bkfull, 0)])
        yield from headnorm(t, t.pj[:, o_kr:o_kr + 32].unsqueeze(1), [pjk], 1, 32, gkr[:], bgkr, t.krn[:], [t.bkrn])
        yield from rope(t, t.krn[:], [t.bkrn], 1, t.kpe[:].unsqueeze(1), [t.bkpe])
        A("pool", lambda e: e.tensor_copy(out=t.kfull[:, :, 64:96], in_=t.kpe[:].unsqueeze(1).to_broadcast([128, 8, 32])), reads=[t.bkpe], writes=[(t.bkfull, 1)])
        yield
        for h in range(8):
            A("pe", lambda e, h=h: e.transpose(out=ptb[0:96, h * 128:(h + 1) * 128], in_=t.kfull[:, h, :], identity=ident[:]),
              reads=[t.bkfull, bident], writes=[btb])
        yield
        A("act", lambda e: e.copy(out=t.kbT[:].rearrange("p h t -> p (h t)"), in_=ptb[0:96, :]), reads=[btb], writes=[t.bkbT])
        A("sp", lambda e: e.dma_start(out=kc_kb(j), in_=t.kbT[:]), reads=[t.bkbT], writes=[(bkc, g)], dma=True)
        yield

    RG = [[0, 1], [2, 3], [4, 5], [6, 7]]

    def emit_gather(g):
        A("pool", lambda e: e.collective_compute("AllGather", ALU.bypass, replica_groups=RG,
                                                 ins=[kc[g * 640:(g + 1) * 640, :].opt()], outs=[kg[g * 1280:(g + 1) * 1280, :].opt()]),
          reads=[(bkc, g)], writes=[(bkg, g)], cc=True)

    active = [None] * NSL
    actj = [None] * NSL
    done = set()
    nextj = 0
    for si in range(NSL - 1):
        active[si] = blockgen(nextj, si)
        actj[si] = nextj
        nextj += 1
        for sj in range(si + 1):
            for _ in range(3):
                next(active[sj])
    while True:
        alive = False
        for si in range(NSL):
            if active[si] is None and nextj < NOWN:
                active[si] = blockgen(nextj, si)
                actj[si] = nextj
                nextj += 1
            if active[si] is not None:
                alive = True
                try:
                    next(active[si])
                except StopIteration:
                    active[si] = None
                    done.add(actj[si])
                    g_ = actj[si] // 2
                    if (2 * g_ in done) and (2 * g_ + 1 in done):
                        emit_gather(g_)
        if not alive:
            break

    k.new_phase()
    Wg2, bWg2 = k.sb("Wg", [128, 8, DFF], BF16)
    Wo, bWo = k.sb("Wo", [128, 8, D], BF16)
    off_after_w = k.off
    KTs = [k.sb("KT%d" % i, [96, NBLK * 128], BF16) for i in range(2)]
    VAs = [k.sb("VA%d" % i, [128, NBLK, 80], BF16) for i in range(2)]
    QTs = [k.sb("QT%d" % i, [96, NOWN * 128], BF16) for i in range(2)]
    NSB = 4
    SKB = 4
    NPT = 6
    SBANKS = [0, 1, 2, 5]
    BC_B = 6
    pTs = [k.sb("pT%d" % i, [128, 512], BF16) for i in range(NPT)]
    rrow, brrow = k.sb("rrow", [128, 512], F32)
    bcs, bbcs = k.sb("bcs", [64, 512], F32)
    oTn = [k.sb("oTn%d" % i, [64, 512], BF16) for i in range(2)]

    def b_loads(h):
        KT, bKT = KTs[h % 2]
        VA, bVA = VAs[h % 2]
        QT, bQT = QTs[h % 2]
        KTv = KT[:].rearrange("p (r g t) -> p r g t", r=2, g=16)
        VAv = VA[:].rearrange("p c d -> p (c d)").rearrange("p (r g x) -> p r g x", r=2, g=16)
        for q_ in range(4):
            for r_ in range(2):
                A("sp", lambda e, r_=r_, q_=q_: e.dma_start(out=KTv[:, r_, q_ * 4:(q_ + 1) * 4], in_=kg_kb(h, r_)[:, q_ * 4:(q_ + 1) * 4]),
                  reads=[(bkg, g_) for g_ in range(q_ * 4, q_ * 4 + 4)], writes=[(bKT, (r_, q_))], dma=True)
                A("sp", lambda e, r_=r_, q_=q_: e.dma_start(out=VAv[:, r_, q_ * 4:(q_ + 1) * 4], in_=kg_vb(h, r_)[:, q_ * 4:(q_ + 1) * 4]),
                  reads=[(bkg, g_) for g_ in range(q_ * 4, q_ * 4 + 4)], writes=[(bVA, (r_, q_))], dma=True)
        A("sp", lambda e: e.dma_start(out=QT[:], in_=qbT_s[h]), reads=[bqbT_s], writes=[bQT], dma=True)

    items = []
    ng = 0
    for h in range(8):
        for J in range(8):
            blocks = []
            for jb in range(4 * J + 4):
                blocks.append((jb, 0, 0) if jb < 4 * J else (jb, (jb - 4 * J) * 128, 1))
            for jb in range(4 * J + 4):
                blocks.append((NOWN + jb, 0, 0) if jb < 4 * J else (NOWN + jb, (jb - 4 * J) * 128, 2))
            for bi, (blk, c0, kind) in enumerate(blocks):
                items.append(dict(h=h, J=J, blk=blk, c0=c0, kind=kind, bi=bi, nb=len(blocks), ng=ng,
                                  first=(J == 0 and bi == 0)))
            ng += 1

    def b_stage1(i, it):
        h, J, blk, c0, kind = it["h"], it["J"], it["blk"], it["c0"], it["kind"]
        if it["first"] and h == 0:
            b_loads(0)
        if J == 0 and it["bi"] == SKB + 1 and h + 1 < 8:
            b_loads(h + 1)
        KT, bKT = KTs[h % 2]
        QT, bQT = QTs[h % 2]
        ps_b = SBANKS[i % NSB]
        pT, bpT = pTs[i % NPT]
        ps = bank(ps_b)
        A("pe", lambda e: e.matmul(ps[:, c0:512], lhsT=KT[:, blk * 128:(blk + 1) * 128], rhs=QT[:, J * 512 + c0:(J + 1) * 512],
                                   start=True, stop=True), reads=[(bKT, (blk // 32, (blk % 32) // 8)), bQT], writes=[pbuf[ps_b]])
        A("act", lambda e: e.activation(out=pT[:, c0:512], in_=ps[:, c0:512], func=AF.Exp), reads=[pbuf[ps_b]], writes=[bpT])
        if kind == 1:
            A("pool", lambda e: e.tensor_tensor(out=pT[:, c0:c0 + 128], in0=pT[:, c0:c0 + 128], in1=maskE[:], op=ALU.mult),
              reads=[bpT, bmaskE], writes=[bpT])
        elif kind == 2:
            A("pool", lambda e: e.tensor_tensor(out=pT[:, c0:c0 + 128], in0=pT[:, c0:c0 + 128], in1=maskO[:], op=ALU.mult),
              reads=[bpT, bmaskO], writes=[bpT])

    pend_b = []

    def b_stage2(i, it):
        h, J, blk, c0, bi, nb = it["h"], it["J"], it["blk"], it["c0"], it["bi"], it["nb"]
        VA, bVA = VAs[h % 2]
        pT, bpT = pTs[i % NPT]
        po_b = 3 + (it["ng"] % 2)
        po = bank(po_b)
        A("pe", lambda e: e.matmul(po[0:80, c0:512], lhsT=VA[:, blk, :], rhs=pT[:, c0:512], start=(bi == 0), stop=(bi == nb - 1)),
          reads=[(bVA, (blk // 32, (blk % 32) // 8)), bpT], writes=[pbuf[po_b]])
        if bi == nb - 1:
            A("dve", lambda e: e.reciprocal(out=rrow[64:65, :], in_=po[64:65, :]), reads=[pbuf[po_b]], writes=[brrow])

            def epi():
                A("pe", lambda e: e.matmul(bank(BC_B)[0:64, :], lhsT=onesf[64:65, 0:64], rhs=rrow[64:65, :], start=True, stop=True),
                  reads=[bonesf, brrow], writes=[pbuf[BC_B]])
                A("dve", lambda e: e.tensor_copy(out=bcs[:], in_=bank(BC_B)[0:64, :]), reads=[pbuf[BC_B]], writes=[bbcs])
                on, bon = oTn[it["ng"] % 2]
                A("dve", lambda e: e.tensor_tensor(out=on[:], in0=po[0:64, :], in1=bcs[:], op=ALU.mult), reads=[pbuf[po_b], bbcs], writes=[bon])
                A("sp", lambda e: e.dma_start(out=oTB_s[h][:, J * 512:(J + 1) * 512], in_=on[:]), reads=[bon], writes=[(boTB_s, h * 8 + J)], dma=True)
            pend_b.append([6, epi])

    expB, bexpB = k.sb("expB", [128, 8, 6, 128], F32)
    for h in range(8):
        A("sp", lambda e, h=h: e.dma_start(out=expB[:, h].rearrange("p b q -> p (b q)"), in_=biasA[:, h * 768:(h + 1) * 768]),
          writes=[(bexpB, h)], dma=True)
        A("act", lambda e, h=h: e.activation(out=expB[:, h].rearrange("p b q -> p (b q)"), in_=expB[:, h].rearrange("p b q -> p (b q)"), func=AF.Exp),
          reads=[(bexpB, h)], writes=[(bexpB, h)])
    NR = 4
    ring = [(k.sb("rk%d" % i, [128, 2, 512], BF16), k.sb("rv%d" % i, [128, 2, 640], BF16)) for i in range(NR)]
    qTs_ = [k.sb("qTa%d" % i, [128, 4, 128], BF16) for i in range(2)]
    NSA = 3
    pe32 = [k.sb("pe32_%d" % i, [128, 3, 128], BF16) for i in range(NSA)]
    pTa = [k.sb("pTa%d" % i, [128, 3, 128], BF16) for i in range(NSA)]
    poS, bpoS = k.sb("poS", [128, 512], F32)
    rrA, brrA = k.sb("rrA", [128, 512], F32)
    oTAn = [k.sb("oTAn%d" % i, [64, 8, 128], BF16) for i in range(2)]
    PSA_B = 6
    POA_B = 7
    poa = bank(POA_B)

    def a_loads(m):
        (rk, brk), (rv, brv) = ring[m % NR]
        qT, bqT = qTs_[m % 2]
        for r_ in range(2):
            A("sp", lambda e, r_=r_: e.dma_start(out=rk[:, r_, :], in_=kg_ka(r_, m)), reads=[(bkg, m // 2)], writes=[(brk, r_)], dma=True)
            A("sp", lambda e, r_=r_: e.dma_start(out=rv[:, r_, :], in_=kg_va(r_, m)), reads=[(bkg, m // 2)], writes=[(brv, r_)], dma=True)
        A("sp", lambda e: e.dma_start(out=qT[:].rearrange("p t q -> p (t q)"), in_=qaT_s[m]), reads=[bqaT_s], writes=[bqT], dma=True)

    aitems = []
    for m in range(NOWN):
        nbk = 2 * min(m + 1, 3)
        for h in range(8):
            for hf in range(2):
                b0, b1 = hf * 3, min(nbk, hf * 3 + 3)
                if b1 > b0:
                    aitems.append(dict(m=m, h=h, b0=b0, b1=b1, nbk=nbk))

    def a_stage1(i, it):
        m, h, b0, b1 = it["m"], it["h"], it["b0"], it["b1"]
        if h == 0 and b0 == 0 and m == 0:
            a_loads(0)
        if h == 2 and b0 == 0 and m + 1 < NOWN:
            a_loads(m + 1)
        qT, bqT = qTs_[m % 2]
        t_ = h // 2
        pp = (h % 2) * 64
        psa = bank(PSA_B)
        pe_t, bpe_t = pe32[i % NSA]
        pT_t, bpT_t = pTa[i % NSA]
        n_ = b1 - b0
        for b_ in range(b0, b1):
            (rk_, brk_), _ = ring[(m - b_ // 2) % NR]
            side = b_ % 2
            A("pe", lambda e, b_=b_, rk_=rk_, side=side: e.matmul(
                psa[:, (b_ - b0) * 128:(b_ - b0 + 1) * 128], lhsT=rk_[pp:pp + 64, side, t_ * 128:(t_ + 1) * 128], rhs=qT[pp:pp + 64, t_, :],
                start=True, stop=True), reads=[(brk_, side), bqT], writes=[pbuf[PSA_B]])
        A("act", lambda e: e.activation(out=pe_t[:].rearrange("p b q -> p (b q)")[:, 0:n_ * 128], in_=psa[:, 0:n_ * 128], func=AF.Exp),
          reads=[pbuf[PSA_B]], writes=[bpe_t])
        A("dve" if (i % 2) == 0 else "pool", lambda e: e.tensor_tensor(out=pT_t[:, 0:n_, :], in0=pe_t[:, 0:n_, :], in1=expB[:, h, b0:b1, :], op=ALU.mult),
          reads=[bpe_t, (bexpB, h)], writes=[bpT_t])

    pend_a = []

    def a_stage2(i, it):
        m, h, b0, b1, nbk = it["m"], it["h"], it["b0"], it["b1"], it["nbk"]
        pT_t, bpT_t = pTa[i % NSA]
        hq = h % 4
        for b_ in range(b0, b1):
            _, (rv_, brv_) = ring[(m - b_ // 2) % NR]
            side = b_ % 2
            A("pe", lambda e, b_=b_, rv_=rv_, side=side: e.matmul(
                poa[0:80, hq * 128:(hq + 1) * 128], lhsT=rv_[:, side, h * 80:(h + 1) * 80], rhs=pT_t[:, b_ - b0, :],
                start=(b_ == 0), stop=(b_ == nbk - 1)), reads=[(brv_, side), bpT_t], writes=[pbuf[POA_B]])
        if hq == 3 and b1 == nbk:
            A("dve", lambda e: e.tensor_copy(out=poS[0:80, :], in_=poa[0:80, :]), reads=[pbuf[POA_B]], writes=[bpoS])
            A("dve", lambda e: e.reciprocal(out=rrA[64:65, :], in_=poS[64:65, :]), reads=[bpoS], writes=[brrA])
            pend_a.append([2, lambda: a_epi(m, h // 4)])

    def a_epi(m, hh):
        A("pe", lambda e: e.matmul(bank(BC_B)[0:64, :], lhsT=onesf[64:65, 0:64], rhs=rrA[64:65, :], start=True, stop=True),
          reads=[bonesf, brrA], writes=[pbuf[BC_B]])
        on, bon = oTAn[m % 2]
        A("dve", lambda e: e.tensor_tensor(out=on[:, hh * 4:(hh + 1) * 4, :].rearrange("p h q -> p (h q)"), in0=poS[0:64, :], in1=bank(BC_B)[0:64, :], op=ALU.mult),
          reads=[bpoS, pbuf[BC_B]], writes=[(bon, hh)])
        if hh == 1:
            A("sp", lambda e: e.dma_start(out=oTA_s[:, :, m * 128:(m + 1) * 128].rearrange("h p q -> p h q"), in_=on[:]),
              reads=[bon], writes=[(boTA_s, m)], dma=True)

    def a_step(j):
        if j < len(aitems):
            a_stage1(j, aitems[j])
        for pe_ in list(pend_a):
            pe_[0] -= 1
            if pe_[0] <= 0:
                pe_[1]()
                pend_a.remove(pe_)
        if 1 <= j <= len(aitems):
            a_stage2(j - 1, aitems[j - 1])

    NA_STEPS = len(aitems) + 4
    aj = 0
    nB = len(items)
    while aj < 40:
        a_step(aj)
        aj += 1
    for i in range(nB + SKB):
        if i < nB:
            b_stage1(i, items[i])
        for pe_ in list(pend_b):
            pe_[0] -= 1
            if pe_[0] <= 0:
                pe_[1]()
                pend_b.remove(pe_)
        if i >= SKB:
            b_stage2(i - SKB, items[i - SKB])
        if i == 96:
            load_w(Wo, bWo, w["w_out"], 8, D)
        if i in (160, 320, 480, 640):
            c_ = (160, 320, 480, 640).index(i)
            vsrc = w["ffn2_w_gate"].rearrange("(kt p) f -> p kt f", p=128)
            A("pool", lambda e, c_=c_, vsrc=vsrc: e.dma_start(out=Wg2[:, :, c_ * FCH:(c_ + 1) * FCH], in_=vsrc[:, :, c_ * FCH:(c_ + 1) * FCH]),
              writes=[(bWg2, c_)], dma=True)
        while aj < NA_STEPS and (aj - 40) * nB <= i * (NA_STEPS - 40):
            a_step(aj)
            aj += 1
    for pe_ in pend_b:
        pe_[1]()
    while aj < NA_STEPS:
        a_step(aj)
        aj += 1
    for pe_ in pend_a:
        pe_[1]()

    k.new_phase(keep=off_after_w)
    Wg, bWg = Wg2, bWg2
    Wu, bWu = k.sb("Wu", [128, 8, DFF], BF16)
    load_w_cols([(Wu, bWu)], [w["ffn2_w_up"]])
    Wd, bWd = ffn_weights_d("ffn2")
    g2, bg2 = k.sb("g2", [128, D], F32)
    gf, bgf = k.sb("gf", [128, D], F32)
    load_bc(g2[:], bg2, w["ffn2_norm"], D)
    load_bc(gf[:], bgf, w["final_norm"], D)
    T, (ssf, bssf, rsf, brsf), xns, xnTs = ffn_tiles()
    xgs = [k.sb("xg2_%d" % i, [128, 2, D], F32) for i in range(2)]
    oTs = [k.sb("oT%d" % i, [128, 8, 256], BF16) for i in range(2)]
    out_v = out.rearrange("(g t p) c -> g p t c", t=2, p=128)
    oTA_v = oTA_s.rearrange("(h2 par) d t -> par d h2 t", par=2)
    oTB_v = oTB_s.rearrange("(h2 par) d t -> par d h2 t", par=2)
    final_stores = []
    NG3 = 16
    cnt = [0]

    def p3_load_x(g):
        xg, bxg = xgs[g % 2]
        A("sp", lambda e: e.dma_start(out=xg[:], in_=x1_v[g]), reads=[(bx1_s, g)], writes=[bxg], dma=True)

    def p3_load(g):
        oT, boT = oTs[g % 2]
        for par in range(2):
            A("sp", lambda e, par=par: e.dma_start(out=oT[par * 64:(par + 1) * 64, 0:4, :], in_=oTA_v[par][:, :, g * 256:(g + 1) * 256]),
              reads=[boTA_s], writes=[(boT, par)], dma=True)
            A("sp", lambda e, par=par: e.dma_start(out=oT[par * 64:(par + 1) * 64, 4:8, :], in_=oTB_v[par][:, :, g * 256:(g + 1) * 256]),
              reads=[boTB_s], writes=[(boT, 2 + par)], dma=True)

    def p3_wout(g, t_):
        xg, bxg = xgs[g % 2]
        oT, boT = oTs[g % 2]
        for hf in range(2):
            bd = 5 + hf
            pd = bank(bd)
            for kt in range(8):
                A("pe", lambda e, kt=kt, hf=hf, pd=pd: e.matmul(pd, lhsT=oT[:, kt, t_ * 128:(t_ + 1) * 128], rhs=Wo[:, kt, hf * 512:(hf + 1) * 512],
                                                              start=(kt == 0), stop=(kt == 7)), reads=[boT, bWo], writes=[pbuf[bd]])
            A("dve", lambda e, hf=hf, pd=pd: e.tensor_tensor(out=xg[:, t_, hf * 512:(hf + 1) * 512], in0=pd, in1=xg[:, t_, hf * 512:(hf + 1) * 512], op=ALU.add),
              reads=[pbuf[bd], (bxg, t_)], writes=[(bxg, t_)])

    def p3_pre_chain(g, t_):
        xg, bxg = xgs[g % 2]
        xn, bxn = xns[cnt[0] % 2]
        cnt[0] += 1
        norm_chain(xg[:, t_, :], bxg, t_, g2, bg2, xn[:], bxn, None, (ssf, bssf, rsf, brsf, xn, bxn))
        return xn, bxn

    def p3_final(g, t_):
        xg, bxg = xgs[g % 2]
        xn, bxn = xns[cnt[0] % 2]
        cnt[0] += 1
        norm_chain(xg[:, t_, :], bxg, t_, gf, bgf, xg[:, t_, :], bxg, t_, (ssf, bssf, rsf, brsf, xn, bxn))

    def p3_store(g):
        xg, bxg = xgs[g % 2]
        st_ = A("sp", lambda e: e.dma_start(out=out_v[g], in_=xg[:]), reads=[bxg], dma=True)
        final_stores.append(st_)

    def p3_hooks(g):
        hk = {}
        st = {}
        if g >= 1:
            gp = g - 1
            hk.setdefault(1, []).append(lambda: p3_final(gp, 0))
            hk.setdefault(3, []).append(lambda: (p3_final(gp, 1), p3_store(gp)))
        if g + 1 < NG3:
            gn = g + 1
            xnTn, bxnTn = xnTs[gn % 2]
            hk.setdefault(0, []).append(lambda: p3_load(gn))
            hk.setdefault(4, []).append(lambda: p3_load_x(gn))
            hk.setdefault(6, []).append(lambda: p3_wout(gn, 0))
            hk.setdefault(7, []).append(lambda: p3_wout(gn, 1))
            hk.setdefault(9, []).append(lambda: st.__setitem__("q0", p3_pre_chain(gn, 0)))
            hk.setdefault(14, []).append(lambda: transposes_to(st["q0"][0], st["q0"][1], xnTn[:, :, 0:128], bxnTn, 0))
            hk.setdefault(11, []).append(lambda: st.__setitem__("q1", p3_pre_chain(gn, 1)))
            hk.setdefault(16, []).append(lambda: transposes_to(st["q1"][0], st["q1"][1], xnTn[:, :, 128:256], bxnTn, 1))
        return hk

    p3_load(0)
    p3_load_x(0)
    for t_ in range(2):
        p3_wout(0, t_)
        xn, bxn = p3_pre_chain(0, t_)
        transposes_to(xn, bxn, xnTs[0][0][:, :, t_ * 128:(t_ + 1) * 128], xnTs[0][1], t_)
    for g in range(NG3):
        xg, bxg = xgs[g % 2]
        xnT, bxnT = xnTs[g % 2]
        ffn_group(xg, bxg, xnT, bxnT, Wg, bWg, Wu, bWu, Wd, bWd, T, p3_hooks(g))
    for t_ in range(2):
        p3_final(NG3 - 1, t_)
    p3_store(NG3 - 1)
    S.finalize_on(final_stores)
    S.emit(nc)
    return nc


_NC_CACHE = {}


def _host_inputs(inputs):
    x = np.ascontiguousarray(np.asarray(inputs["x"], dtype=np.float32))
    B, Sq, _ = x.shape
    rel_bias = np.asarray(inputs["a_rel_bias"], dtype=np.float32)[0]
    inv = (1.0 / (np.float32(10000.0) ** (np.arange(0, 32, 2, dtype=np.float32) / np.float32(32)))).astype(np.float32)
    k_i = np.arange(128)[:, None]
    q_i = np.arange(128)[None, :]
    shared = {}
    for nm in ["ffn1_norm", "ffn1_w_gate", "ffn1_w_up", "ffn1_w_down", "mix_norm", "w_in", "a_q_norm", "a_k_norm",
               "b_q_lat_norm", "b_w_uq", "b_kv_lat_norm", "b_w_ukv", "b_q_nope_norm", "b_q_rope_norm", "b_k_nope_norm",
               "b_k_rope_norm", "w_out", "ffn2_norm", "ffn2_w_gate", "ffn2_w_up", "ffn2_w_down", "final_norm"]:
        a = np.asarray(inputs[nm], dtype=np.float32)
        if a.ndim == 3:
            a = a[0]
        shared[nm] = np.ascontiguousarray(a)
    shared["ident"] = np.eye(128, dtype=np.float32)
    per_par = {}
    for p in range(2):
        bA = np.empty((128, 8, 6, 128), np.float32)
        for b in range(6):
            d = b // 2
            other = b % 2
            off = (1 - 2 * d - p) if other else (-2 * d - p)
            kpos = off * 128 + k_i
            rel = q_i - kpos
            ck = 2 * off + (k_i >= 64)
            cq = (q_i >= 64).astype(np.int64)
            vis = ((cq - ck) >= 0) & ((cq - ck) <= 8)
            idx = np.clip(rel, -128, 128) + 128
            vals = rel_bias[:, idx]
            sel = np.where(vis[None], vals, np.float32(-100.0))
            bA[:, :, b, :] = sel.transpose(1, 0, 2)
        gblk = np.arange(32) * 2 + p
        pos = (gblk[:, None] * 128 + np.arange(128)[None, :]).reshape(-1).astype(np.float32)
        ang = pos[:, None] * inv[None, :]
        c = np.cos(ang).astype(np.float32)
        s = np.sin(ang).astype(np.float32)
        cs = np.concatenate([c, c, -s, s], axis=1).astype(np.float32)
        diag = np.where((k_i >= 64) & (q_i < 64), 0.0, 1.0).astype(np.float32)
        ones = np.ones((128, 128), np.float32)
        zeros = np.zeros((128, 128), np.float32)
        per_par[p] = dict(biasA=np.ascontiguousarray(bA.reshape(128, -1)), cs=np.ascontiguousarray(cs),
                          maskE=(diag if p == 0 else ones), maskO=(zeros if p == 0 else diag), gblk=gblk)
    in_maps = []
    for c_ in range(8):
        b = c_ // 2
        p = c_ % 2
        xb = x[b].reshape(64, 128, 1024)
        xs = np.ascontiguousarray(xb[per_par[p]["gblk"]].reshape(32 * 128, 1024))
        m = dict(shared)
        m["xs"] = xs
        m["cs"] = per_par[p]["cs"]
        m["biasA"] = per_par[p]["biasA"]
        m["maskO"] = per_par[p]["maskO"]
        m["maskE"] = per_par[p]["maskE"]
        in_maps.append(m)
    return in_maps


def kernel(**inputs):
    in_maps = _host_inputs(inputs)
    if "nc" not in _NC_CACHE:
        _NC_CACHE["nc"] = build_program()
    nc = _NC_CACHE["nc"]
    res = run_bass_kernel_spmd(nc, in_maps, core_ids=list(range(8)))
    out = np.empty((4, 8192, 1024), np.float32)
    ov = out.reshape(4, 64, 128, 1024)
    for c_ in range(8):
        b = c_ // 2
        p = c_ % 2
        r = np.asarray(res.results[c_]["out"]).reshape(32, 128, 1024)
        ov[b, p::2] = r
    return out
```

```python
import numpy as np
import concourse.bass as bass
import concourse.mybir as mybir
from concourse.bass_utils import run_bass_kernel_spmd

F32 = mybir.dt.float32
BF16 = mybir.dt.bfloat16
AF = mybir.ActivationFunctionType
ALU = mybir.AluOpType
AX = mybir.AxisListType

ENGS = ("pe", "act", "dve", "pool", "sp")
DMA_RING = 8


class Op:
    __slots__ = ("eng", "fn", "deps", "marked", "ord", "is_dma", "dsem", "dval", "idx", "tag", "is_cc")

    def __init__(self, eng, fn, is_dma, tag=""):
        self.eng = eng
        self.fn = fn
        self.deps = []
        self.marked = False
        self.ord = 0
        self.is_dma = is_dma
        self.dsem = None
        self.dval = 0
        self.idx = 0
        self.tag = tag
        self.is_cc = False


class Buf:
    def __init__(self, sched, name):
        self.name = name
        self.w = {}
        self.r = {}
        self.base = list(sched.barrier_ops)

    def _keys(self, key):
        if key is None:
            return set(self.w.keys()) | set(self.r.keys())
        return {key, None}


class Sched:
    def __init__(self, same_eng_sync=True):
        self.ops = {e: [] for e in ENGS}
        self.barrier_ops = []
        self.same_eng_sync = same_eng_sync
        self.final_ops = []
        self.ndma = {e: 0 for e in ENGS}

    def buf(self, name):
        return Buf(self, name)

    @staticmethod
    def _norm(x):
        if isinstance(x, tuple):
            return x
        return (x, None)

    def add(self, eng, fn, reads=(), writes=(), dma=False, tag="", cc=False):
        o = Op(eng, fn, dma or cc, tag)
        o.is_cc = cc
        deps = {}

        def dep(d):
            if d is None or d is o:
                return
            deps[id(d)] = d

        lane = ("cc", id(o)) if cc else (eng if not dma else ("dma", eng, self.ndma[eng] % DMA_RING))
        for x in reads:
            b, k = self._norm(x)
            for d in b.base:
                dep(d)
            for kk in b._keys(k):
                dep(b.w.get(kk))
        for x in writes:
            b, k = self._norm(x)
            for d in b.base:
                dep(d)
            for kk in b._keys(k):
                dep(b.w.get(kk))
                for d in b.r.get(kk, {}).values():
                    dep(d)
        for x in reads:
            b, k = self._norm(x)
            b.r.setdefault(k, {})[lane] = o
        for x in writes:
            b, k = self._norm(x)
            if k is None:
                b.w = {None: o}
                b.r = {}
            else:
                b.w[k] = o
                b.r[k] = {}
            b.base = []
        for d in deps.values():
            if d.eng == eng and not d.is_dma and not (dma or cc):
                if eng == "pe" or not self.same_eng_sync:
                    continue
            o.deps.append(d)
            d.marked = True
        o.idx = len(self.ops[eng])
        self.ops[eng].append(o)
        if dma:
            self.ndma[eng] += 1
        return o

    def barrier(self):
        bo = []
        for e in ENGS:
            last = None
            dmas = []
            for o in reversed(self.ops[e]):
                if o.is_dma:
                    if len(dmas) < DMA_RING:
                        dmas.append(o)
                elif last is None:
                    last = o
                if last is not None and len(dmas) >= DMA_RING:
                    break
            if last is not None:
                bo.append(last)
            bo.extend(dmas)
        self.barrier_ops = bo

    def finalize_on(self, ops):
        self.final_ops.extend(ops)
        for o in ops:
            o.marked = True

    def emit(self, nc):
        stack_sems = {}
        import contextlib
        with contextlib.ExitStack() as es:
            SEM_LIM = 30000
            nmark = {e: sum(1 for o in self.ops[e] if (o.marked and not o.is_dma)) for e in ENGS}
            esem = {e: [es.enter_context(nc.semaphore("s_%s_%d" % (e, i)))
                        for i in range(max(1, (nmark[e] + SEM_LIM - 1) // SEM_LIM))] for e in ENGS}
            dsem = {e: [es.enter_context(nc.semaphore("d_%s_%d" % (e, i))) for i in range(DMA_RING)]
                    for e in ("sp", "act", "pool") if self.ndma[e] > 0}
            for e in ENGS:
                c = 0
                for o in self.ops[e]:
                    if not o.is_dma and o.marked:
                        c += 1
                    o.ord = c
                k = 0
                for o in self.ops[e]:
                    if o.is_cc:
                        o.dsem = es.enter_context(nc.semaphore("cc_%s_%d" % (e, o.idx)))
                        o.dval = 1
                    elif o.is_dma:
                        o.dsem = dsem[e][k % DMA_RING]
                        o.dval = 16 * (k // DMA_RING + 1)
                        k += 1
            block = es.enter_context(nc.Block())

            def run(e, eng):
                waited = {}

                def wait(sem, val):
                    key = id(sem)
                    if waited.get(key, 0) >= val:
                        return
                    waited[key] = val
                    eng.wait_ge(sem, val)

                for o in self.ops[e]:
                    for d in o.deps:
                        if d.is_dma:
                            wait(d.dsem, d.dval)
                        else:
                            wait(esem[d.eng][(d.ord - 1) // SEM_LIM], (d.ord - 1) % SEM_LIM + 1)
                    if o.is_cc:
                        ins = o.fn(eng)
                        ins.then_inc(o.dsem)
                    elif o.is_dma:
                        if o.dval > 16:
                            wait(o.dsem, o.dval - 16)
                        ins = o.fn(eng)
                        ins.then_inc(o.dsem, 16)
                    else:
                        ins = o.fn(eng)
                        if o.marked:
                            ins.then_inc(esem[e][(o.ord - 1) // SEM_LIM], 1)
                if e == "sp":
                    for d in self.final_ops:
                        if d.is_dma:
                            wait(d.dsem, d.dval)
                        else:
                            wait(esem[d.eng][(d.ord - 1) // SEM_LIM], (d.ord - 1) % SEM_LIM + 1)

            @block.tensor
            def _(eng):
                run("pe", eng)

            @block.scalar
            def _(eng):
                run("act", eng)

            @block.vector
            def _(eng):
                run("dve", eng)

            @block.gpsimd
            def _(eng):
                run("pool", eng)

            @block.sync
            def _(eng):
                run("sp", eng)

import contextlib

D = 1024
DFF = 2816
NFT = 22
INC = 1952
NBLK = 64
NOWN = 32
EPS = 1e-6
SB_LO, SB_HI = 16512, 229344


class KB:
    def __init__(self, nc):
        self.nc = nc
        self.S = Sched()
        self.off = SB_LO
        self.phase_base = SB_LO
        self.uid = 0

    def sb(self, name, shape, dt):
        sz = int(np.prod(shape[1:])) * (4 if dt == F32 else 2)
        sz = (sz + 63) // 64 * 64
        self.uid += 1
        t = self.nc.alloc_sbuf_tensor_at("%s_%d" % (name, self.uid), list(shape), dt, offset=self.off)
        self.off += sz
        assert self.off <= SB_HI, ("SBUF overflow", name, self.off)
        return t, self.S.buf(name)

    def persist_done(self):
        self.phase_base = self.off

    def new_phase(self, keep=None):
        self.S.barrier()
        self.off = self.phase_base if keep is None else keep


def build_program(debug=False):
    nc = bass.Bass("TRN2", target_bir_lowering=False)
    k = KB(nc)
    S = k.S
    A = S.add

    def din(name, shape, dt=F32):
        return nc.dram_tensor(name, list(shape), dt, kind="ExternalInput").ap()

    xs = din("xs", [NOWN * 128, D])
    cs = din("cs", [NOWN * 128, 64])
    biasA = din("biasA", [128, 8 * 6 * 128])
    maskO_in = din("maskO", [128, 128])
    maskE_in = din("maskE", [128, 128])
    ident_in = din("ident", [128, 128])
    w = {}
    for nm, shp in [("ffn1_norm", [1, D]), ("ffn1_w_gate", [D, DFF]), ("ffn1_w_up", [D, DFF]), ("ffn1_w_down", [DFF, D]),
                    ("mix_norm", [1, D]), ("w_in", [D, INC]), ("a_q_norm", [1, 64]), ("a_k_norm", [1, 64]),
                    ("b_q_lat_norm", [1, 256]), ("b_w_uq", [256, 768]), ("b_kv_lat_norm", [1, 128]), ("b_w_ukv", [128, 1024]),
                    ("b_q_nope_norm", [1, 64]), ("b_q_rope_norm", [1, 32]), ("b_k_nope_norm", [1, 64]), ("b_k_rope_norm", [1, 32]),
                    ("w_out", [D, D]), ("ffn2_norm", [1, D]), ("ffn2_w_gate", [D, DFF]), ("ffn2_w_up", [D, DFF]),
                    ("ffn2_w_down", [DFF, D]), ("final_norm", [1, D])]:
        w[nm] = din(nm, shp)
    okind = "ExternalOutput"
    out = nc.dram_tensor("out", [NOWN * 128, D], F32, kind=okind).ap()
    skind = "ExternalOutput" if debug else "Internal"

    def dscr(name, shape, dt):
        t = nc.dram_tensor(name, list(shape), dt, kind=skind) if debug else nc.dram_tensor(name, list(shape), dt)
        return t.ap(), S.buf(name)

    x1_s, bx1_s = dscr("x1_s", [NOWN * 128, D], F32)
    hT_s, bhT_s = dscr("hT_s", [16, 128, 8 * 256], BF16)
    CH = 655360
    OFF_KA, OFF_VA, OFF_KB, OFF_VB = 0, 131072, 294912, 491520
    kc, bkc = dscr("kc", [16 * 640, 1024], BF16)
    kg, bkg = dscr("kg", [16 * 1280, 1024], BF16)
    kcf = kc.rearrange("r c -> (r c)")
    kgf = kg.rearrange("r c -> (r c)")

    def kc_ka(j):
        o = (j // 2) * CH + OFF_KA + (j % 2) * 65536
        return kcf[o:o + 65536].rearrange("(p c) -> p c", c=512)

    def kc_va(j):
        o = (j // 2) * CH + OFF_VA + (j % 2) * 81920
        return kcf[o:o + 81920].rearrange("(p c) -> p c", c=640)

    def kc_kb(j):
        o = (j // 2) * CH + OFF_KB
        return kcf[o:o + 196608].rearrange("(h p t) -> p h t", h=8, p=96)[:, :, (j % 2) * 128:(j % 2) * 128 + 128]

    def kc_vb(j):
        o = (j // 2) * CH + OFF_VB
        return kcf[o:o + 163840].rearrange("(h p t) -> p h t", h=8, p=128)[:, :, (j % 2) * 80:(j % 2) * 80 + 80]

    def kg_ka(r_, m):
        o = ((m // 2) * 2 + r_) * CH + OFF_KA + (m % 2) * 65536
        return kgf[o:o + 65536].rearrange("(p c) -> p c", c=512)

    def kg_va(r_, m):
        o = ((m // 2) * 2 + r_) * CH + OFF_VA + (m % 2) * 81920
        return kgf[o:o + 81920].rearrange("(p c) -> p c", c=640)

    kgv = kgf.rearrange("(g r x) -> g r x", g=16, r=2)

    def kg_kb(h, r_):
        return kgv[:, r_, OFF_KB + h * 24576:OFF_KB + (h + 1) * 24576].rearrange("g (p t) -> p g t", t=256)

    def kg_vb(h, r_):
        return kgv[:, r_, OFF_VB + h * 20480:OFF_VB + (h + 1) * 20480].rearrange("g (p t) -> p g t", t=160)

    qaT_s, bqaT_s = dscr("qaT_s", [NOWN, 128, 512], BF16)
    qbT_s, bqbT_s = dscr("qbT_s", [8, 96, NOWN * 128], BF16)
    oTA_s, boTA_s = dscr("oTA_s", [8, 64, NOWN * 128], BF16)
    oTB_s, boTB_s = dscr("oTB_s", [8, 64, NOWN * 128], BF16)

    P = [nc.alloc_psum_tensor("P%d" % i, [128, 1024], F32) for i in range(4)]
    Pb = [p.bitcast(BF16) for p in P]
    pbuf = [S.buf("bank%d" % i) for i in range(8)]

    def bank(i):
        return P[i // 2][:, (i % 2) * 512:(i % 2) * 512 + 512]

    def bankb(i):
        return Pb[i // 2][:, (i % 2) * 1024:(i % 2) * 1024 + 1024]

    ident, bident = k.sb("ident", [128, 128], BF16)
    identf, bidentf = k.sb("identf", [128, 128], F32)
    onesf, bonesf = k.sb("onesf", [128, 64], F32)
    maskO, bmaskO = k.sb("maskO", [128, 128], BF16)
    A("sp", lambda e: e.dma_start(out=identf[:], in_=ident_in), writes=[bidentf], dma=True)
    A("dve", lambda e: e.tensor_copy(out=ident[:], in_=identf[:]), reads=[bidentf], writes=[bident])
    A("sp", lambda e: e.dma_start(out=identf[:], in_=maskO_in), reads=[bidentf], writes=[bidentf], dma=True)
    A("dve", lambda e: e.tensor_copy(out=maskO[:], in_=identf[:]), reads=[bidentf], writes=[bmaskO])
    maskE, bmaskE = k.sb("maskE", [128, 128], BF16)
    A("sp", lambda e: e.dma_start(out=identf[:], in_=maskE_in), reads=[bidentf], writes=[bidentf], dma=True)
    A("dve", lambda e: e.tensor_copy(out=maskE[:], in_=identf[:]), reads=[bidentf], writes=[bmaskE])
    A("pool", lambda e: e.memset(onesf[:], 1.0), writes=[bonesf])
    k.persist_done()

    def load_w(dst, bdst, src, kt_n, ncols, rows_per=128):
        v = src.rearrange("(kt p) f -> p kt f", p=128)
        step = 1 if ncols >= 1024 else kt_n
        for k0 in range(0, kt_n, step):
            k1 = min(kt_n, k0 + step)
            A("pool", lambda e, k0=k0, k1=k1: e.dma_start(out=dst[:, k0:k1, :], in_=v[:, k0:k1, :]),
              writes=[(bdst, k0)], dma=True)

    FCH = 704

    def load_w_cols(dsts, srcs):
        for c_ in range(DFF // FCH):
            for (dst, bdst), src in zip(dsts, srcs):
                v = src.rearrange("(kt p) f -> p kt f", p=128)
                A("pool", lambda e, dst=dst, v=v, c_=c_: e.dma_start(out=dst[:, :, c_ * FCH:(c_ + 1) * FCH], in_=v[:, :, c_ * FCH:(c_ + 1) * FCH]),
                  writes=[(bdst, c_)], dma=True)

    def load_bc(dst, bdst, src, n):
        A("sp", lambda e: e.dma_start(out=dst, in_=src.partition_broadcast(128)), writes=[bdst], dma=True)

    def norm_chain(src, bsrc, skey, gbc, bgbc, dst, bdst, dkey, tl):
        (ss, bss, rs, brs, junk, bjunk) = tl
        A("dve", lambda e: e.memset(ss[:, 0:1], 0.0), writes=[bss])
        A("act", lambda e: e.activation(out=junk[:], in_=src, func=AF.Square, accum_out=ss[:, 0:1]),
          reads=[(bsrc, skey), bss], writes=[bjunk, bss])
        A("act", lambda e: e.activation(out=rs[:, 0:1], in_=ss[:, 0:1], func=AF.Sqrt, bias=EPS, scale=1.0 / D), reads=[bss], writes=[brs])
        A("dve", lambda e: e.reciprocal(out=rs[:, 0:1], in_=rs[:, 0:1]), reads=[brs], writes=[brs])
        A("dve", lambda e: e.scalar_tensor_tensor(out=dst, in0=src, scalar=rs[:, 0:1], in1=gbc[:], op0=ALU.mult, op1=ALU.mult),
          reads=[(bsrc, skey), brs, bgbc], writes=[(bdst, dkey)])

    def transposes_to(xn, bxn, dstT, bdstT, dkey):
        pt = bankb(0)
        for kt in range(8):
            A("pe", lambda e, kt=kt: e.transpose(out=pt[:, kt * 128:(kt + 1) * 128], in_=xn[:, kt * 128:(kt + 1) * 128], identity=ident[:]),
              reads=[bxn, bident], writes=[pbuf[0]])
        A("act", lambda e: e.copy(out=dstT, in_=pt.rearrange("p (k t) -> p k t", k=8)), reads=[pbuf[0]], writes=[(bdstT, dkey)])

    def ffn_group(xg, bxg, xnT, bxnT, Wg, bWg, Wu, bWu, Wd, bWd, T, hooks):
        (actT, bactT, sg, bsg) = T
        for ft in range(NFT):
            for fn in hooks.get(ft, []):
                fn()
            bg = 1 + (ft % 2)
            bu = 3 + (ft % 2)
            pg = bank(bg)[:, 0:256]
            pu = bank(bu)[:, 0:256]
            for kt in range(8):
                A("pe", lambda e, kt=kt, ft=ft, pg=pg: e.matmul(pg, lhsT=Wg[:, kt, ft * 128:(ft + 1) * 128], rhs=xnT[:, kt, :],
                                                               start=(kt == 0), stop=(kt == 7)),
                  reads=[(bWg, (ft * 128) // FCH), (bWg, (ft * 128 + 127) // FCH), bxnT], writes=[(pbuf[bg], 0)])
            for kt in range(8):
                A("pe", lambda e, kt=kt, ft=ft, pu=pu: e.matmul(pu, lhsT=Wu[:, kt, ft * 128:(ft + 1) * 128], rhs=xnT[:, kt, :],
                                                               start=(kt == 0), stop=(kt == 7)),
                  reads=[(bWu, (ft * 128) // FCH), (bWu, (ft * 128 + 127) // FCH), bxnT], writes=[(pbuf[bu], 1)])
            A("act", lambda e, pg=pg, ft=ft: e.activation(out=sg[:, ft % 2, :], in_=pg, func=AF.Silu), reads=[(pbuf[bg], 0)], writes=[(bsg, ft % 2)])
            A("dve", lambda e, pu=pu, ft=ft: e.tensor_tensor(out=actT[:, ft, :], in0=pu, in1=sg[:, ft % 2, :], op=ALU.mult),
              reads=[(pbuf[bu], 1), (bsg, ft % 2)], writes=[(bactT, ft)])
        i = 0
        for t_ in range(2):
            for hf in range(2):
                bd = 5 + (i % 2)
                i += 1
                pd = bank(bd)
                for ft in range(NFT):
                    A("pe", lambda e, ft=ft, t_=t_, hf=hf, pd=pd: e.matmul(pd, lhsT=actT[:, ft, t_ * 128:(t_ + 1) * 128],
                                                                         rhs=Wd[:, ft, hf * 512:(hf + 1) * 512],
                                                                         start=(ft == 0), stop=(ft == NFT - 1)),
                      reads=[bactT, (bWd, ft)], writes=[pbuf[bd]])
                A("dve", lambda e, t_=t_, hf=hf, pd=pd: e.scalar_tensor_tensor(out=xg[:, t_, hf * 512:(hf + 1) * 512], in0=pd, scalar=0.5,
                                                                               in1=xg[:, t_, hf * 512:(hf + 1) * 512], op0=ALU.mult, op1=ALU.add),
                  reads=[pbuf[bd], (bxg, t_)], writes=[(bxg, t_)])

    def ffn_tiles():
        actT, bactT = k.sb("actT", [128, NFT, 256], BF16)
        sg, bsg = k.sb("sg", [128, 2, 256], F32)
        ss, bss = k.sb("ss", [128, 2], F32)
        rs, brs = k.sb("rs", [128, 2], F32)
        xns = [k.sb("xn%d" % i, [128, D], BF16) for i in range(2)]
        xnTs = [k.sb("xnT%d" % i, [128, 8, 256], BF16) for i in range(2)]
        return (actT, bactT, sg, bsg), (ss, bss, rs, brs), xns, xnTs

    def ffn_weights_gu(pfx):
        Wg, bWg = k.sb("Wg", [128, 8, DFF], BF16)
        Wu, bWu = k.sb("Wu", [128, 8, DFF], BF16)
        load_w_cols([(Wg, bWg), (Wu, bWu)], [w[pfx + "_w_gate"], w[pfx + "_w_up"]])
        return Wg, bWg, Wu, bWu

    def ffn_weights_d(pfx):
        Wd, bWd = k.sb("Wd", [128, NFT, D], BF16)
        load_w(Wd, bWd, w[pfx + "_w_down"], NFT, D)
        return Wd, bWd

    def ffn_weights(pfx):
        Wg, bWg, Wu, bWu = ffn_weights_gu(pfx)
        Wd, bWd = ffn_weights_d(pfx)
        return Wg, bWg, Wu, bWu, Wd, bWd

    k.new_phase()
    Wg, bWg, Wu, bWu, Wd, bWd = ffn_weights("ffn1")
    g1, bg1 = k.sb("g1", [128, D], F32)
    gm, bgm = k.sb("gm", [128, D], F32)
    load_bc(g1[:], bg1, w["ffn1_norm"], D)
    load_bc(gm[:], bgm, w["mix_norm"], D)
    T, (ssf, bssf, rsf, brsf), xns, xnTs = ffn_tiles()
    xgs = [k.sb("xg%d" % i, [128, 2, D], F32) for i in range(2)]
    hTs = [k.sb("hT%d" % i, [128, 8, 256], BF16) for i in range(2)]
    xs_v = xs.rearrange("(g t p) c -> g p t c", t=2, p=128)
    x1_v = x1_s.rearrange("(g t p) c -> g p t c", t=2, p=128)
    NG1 = 16
    cnt = [0]

    def p1a_load(g):
        xg, bxg = xgs[g % 2]
        A("sp", lambda e: e.dma_start(out=xg[:], in_=xs_v[g]), writes=[bxg], dma=True)

    def p1a_pre_chain(g, t_):
        xg, bxg = xgs[g % 2]
        xn, bxn = xns[cnt[0] % 2]
        cnt[0] += 1
        norm_chain(xg[:, t_, :], bxg, t_, g1, bg1, xn[:], bxn, None, (ssf, bssf, rsf, brsf, xn, bxn))
        return xn, bxn

    def p1a_post_chain(g, t_):
        xg, bxg = xgs[g % 2]
        xn, bxn = xns[cnt[0] % 2]
        cnt[0] += 1
        norm_chain(xg[:, t_, :], bxg, t_, gm, bgm, xn[:], bxn, None, (ssf, bssf, rsf, brsf, xn, bxn))
        return xn, bxn

    def p1a_hooks(g):
        hk = {}
        st = {}
        if g >= 1:
            gp = g - 1
            hTp, bhTp = hTs[gp % 2]
            hk.setdefault(0, []).append(lambda: st.__setitem__("e0", p1a_post_chain(gp, 0)))
            hk.setdefault(5, []).append(lambda: transposes_to(st["e0"][0], st["e0"][1], hTp[:, :, 0:128], bhTp, 0))
            hk.setdefault(2, []).append(lambda: st.__setitem__("e1", p1a_post_chain(gp, 1)))

            def fin():
                transposes_to(st["e1"][0], st["e1"][1], hTp[:, :, 128:256], bhTp, 1)
                A("sp", lambda e: e.dma_start(out=hT_s[gp], in_=hTp[:].rearrange("p k t -> p (k t)")), reads=[bhTp], writes=[(bhT_s, gp)], dma=True)
            hk.setdefault(7, []).append(fin)
        if g + 1 < NG1:
            gn = g + 1
            xnTn, bxnTn = xnTs[gn % 2]
            hk.setdefault(3, []).append(lambda: p1a_load(gn))
            hk.setdefault(9, []).append(lambda: st.__setitem__("q0", p1a_pre_chain(gn, 0)))
            hk.setdefault(14, []).append(lambda: transposes_to(st["q0"][0], st["q0"][1], xnTn[:, :, 0:128], bxnTn, 0))
            hk.setdefault(11, []).append(lambda: st.__setitem__("q1", p1a_pre_chain(gn, 1)))
            hk.setdefault(16, []).append(lambda: transposes_to(st["q1"][0], st["q1"][1], xnTn[:, :, 128:256], bxnTn, 1))
        return hk

    p1a_load(0)
    for t_ in range(2):
        xn, bxn = p1a_pre_chain(0, t_)
        transposes_to(xn, bxn, xnTs[0][0][:, :, t_ * 128:(t_ + 1) * 128], xnTs[0][1], t_)
    for g in range(NG1):
        xg, bxg = xgs[g % 2]
        xnT, bxnT = xnTs[g % 2]
        ffn_group(xg, bxg, xnT, bxnT, Wg, bWg, Wu, bWu, Wd, bWd, T, p1a_hooks(g))
        if True:
            A("sp", lambda e, g=g, xg=xg: e.dma_start(out=x1_v[g], in_=xg[:]), reads=[bxg], writes=[(bx1_s, g)], dma=True)
    gp = NG1 - 1
    hTp, bhTp = hTs[gp % 2]
    for t_ in range(2):
        xn, bxn = p1a_post_chain(gp, t_)
        transposes_to(xn, bxn, hTp[:, :, t_ * 128:(t_ + 1) * 128], bhTp, t_)
    A("sp", lambda e: e.dma_start(out=hT_s[gp], in_=hTp[:].rearrange("p k t -> p (k t)")), reads=[bhTp], writes=[(bhT_s, gp)], dma=True)

    k.new_phase()
    Win, bWin = k.sb("Win", [128, 8, INC], BF16)
    Wuq, bWuq = k.sb("Wuq", [128, 2, 768], BF16)
    Wukv, bWukv = k.sb("Wukv", [128, 1, 1024], BF16)
    load_w(Win, bWin, w["w_in"], 8, INC)
    load_w(Wuq, bWuq, w["b_w_uq"], 2, 768)
    load_w(Wukv, bWukv, w["b_w_ukv"], 1, 1024)
    graw, bgraw = k.sb("graw", [128, 704], F32)
    goff = {}
    o_ = 0
    for nm, n_ in [("a_q_norm", 64), ("a_k_norm", 64), ("b_q_lat_norm", 256), ("b_kv_lat_norm", 128), ("b_q_nope_norm", 64),
                   ("b_q_rope_norm", 32), ("b_k_nope_norm", 64), ("b_k_rope_norm", 32)]:
        goff[nm] = (o_, n_)
        A("sp", lambda e, o_=o_, n_=n_, nm=nm: e.dma_start(out=graw[:, o_:o_ + n_], in_=w[nm].partition_broadcast(128)),
          writes=[(bgraw, nm)], dma=True)
        o_ += n_

    def gain_full(name, nm, H, scale):
        o0, n_ = goff[nm]
        gt, bgt = k.sb(name, [128, H, n_], F32)
        A("dve", lambda e: e.tensor_copy(out=gt[:], in_=graw[:, o0:o0 + n_].unsqueeze(1).to_broadcast([128, H, n_])),
          reads=[(bgraw, nm)], writes=[bgt])
        if scale != 1.0:
            A("dve", lambda e: e.tensor_scalar_mul(out=gt[:], in0=gt[:], scalar1=float(scale)), reads=[bgt], writes=[bgt])
        return gt, bgt

    gqa, bgqa = gain_full("gqa", "a_q_norm", 8, 0.125)
    gka, bgka = gain_full("gka", "a_k_norm", 8, 1.0)
    A("dve", lambda e: e.tensor_tensor(out=gqa[:], in0=gqa[:], in1=gka[:], op=ALU.mult), reads=[bgqa, bgka], writes=[bgqa])
    gcq, bgcq = gain_full("gcq", "b_q_lat_norm", 1, 1.0)
    gckv, bgckv = gain_full("gckv", "b_kv_lat_norm", 1, 1.0)
    SCB = 96.0 ** -0.5
    gqn, bgqn = gain_full("gqn", "b_q_nope_norm", 8, SCB)
    gqr, bgqr = gain_full("gqr", "b_q_rope_norm", 8, SCB)
    gkn, bgkn = gain_full("gkn", "b_k_nope_norm", 8, 1.0)
    A("dve", lambda e: e.tensor_tensor(out=gqn[:], in0=gqn[:], in1=gkn[:], op=ALU.mult), reads=[bgqn, bgkn], writes=[bgqn])
    gkr, bgkr = gain_full("gkr", "b_k_rope_norm", 1, 1.0)

    hT2s = [k.sb("hT2_%d" % i, [128, 8, 256], BF16) for i in range(2)]

    class TS:
        pass

    def mk_ts(si):
        t = TS()
        for nm, shp, dt in [("cst", [128, 64], F32), ("sq", [128, 1024], F32), ("tmp", [128, 1024], F32), ("ss", [128, 16], F32),
                            ("rs", [128, 16], F32), ("pj", [128, 1952], F32), ("qan", [128, 512], BF16), ("kan", [128, 512], BF16),
                            ("qaT", [128, 512], BF16), ("kaT", [128, 512], BF16), ("vaS", [128, 8, 80], BF16), ("vbS", [128, 8, 80], BF16),
                            ("cqn", [128, 256], BF16), ("cqnT", [128, 2, 128], BF16), ("qf", [128, 8, 96], F32), ("qfull", [128, 8, 96], BF16),
                            ("qr", [128, 8, 32], F32), ("rt1", [128, 8, 32], F32), ("rt2", [128, 8, 32], F32),
                            ("qbT", [96, 8, 128], BF16), ("kbT", [96, 8, 128], BF16), ("ckvn", [128, 128], BF16), ("ckvnT", [128, 128], BF16),
                            ("kvf", [128, 8, 128], F32), ("kfull", [128, 8, 96], BF16), ("krn", [128, 1, 32], F32), ("kpe", [128, 32], BF16)]:
            tt, bb = k.sb("%s_s%d" % (nm, si), shp, dt)
            setattr(t, nm, tt)
            setattr(t, "b" + nm, bb)
        for nm in ("vaS", "vbS"):
            tt, bb = getattr(t, nm), getattr(t, "b" + nm)
            A("pool", lambda e, tt=tt: e.memset(tt[:], 0.0), writes=[bb])
            A("pool", lambda e, tt=tt: e.memset(tt[:, :, 64:65], 1.0), reads=[bb], writes=[bb])
        t.b2 = [2 + 2 * si, 2 + 2 * si]
        t.tb = 3 + 2 * si
        return t

    NSL = 3
    tsl = [mk_ts(i) for i in range(NSL)]
    pjctr = [0]

    def headnorm(t, src3, rd, H, Dh, gain3, bgain, out3, wr):
        sq3 = t.sq[:, 0:H * Dh].rearrange("p (h d) -> p h d", h=H)
        tmp3 = t.tmp[:, 0:H * Dh].rearrange("p (h d) -> p h d", h=H)
        if H == 1:
            A("pool", lambda e: e.memset(t.ss[:, 0:1], 0.0), writes=[t.bss])
            A("act", lambda e: e.activation(out=sq3, in_=src3, func=AF.Square, accum_out=t.ss[:, 0:1]), reads=rd + [t.bss], writes=[t.bsq, t.bss])
            yield
        else:
            A("act", lambda e: e.activation(out=sq3, in_=src3, func=AF.Square), reads=rd, writes=[t.bsq])
            yield
            A("dve", lambda e: e.tensor_reduce(out=t.ss[:, 0:H], in_=sq3, axis=AX.X, op=ALU.add), reads=[t.bsq], writes=[t.bss])
            yield
        A("act", lambda e: e.activation(out=t.rs[:, 0:H], in_=t.ss[:, 0:H], func=AF.Sqrt, bias=EPS, scale=1.0 / Dh), reads=[t.bss], writes=[t.brs])
        yield
        A("dve", lambda e: e.reciprocal(out=t.rs[:, 0:H], in_=t.rs[:, 0:H]), reads=[t.brs], writes=[t.brs])
        yield
        if H == 1:
            A("dve", lambda e: e.scalar_tensor_tensor(out=out3[:, 0, :], in0=src3[:, 0, :], scalar=t.rs[:, 0:1], in1=gain3[:, 0, :],
                                                      op0=ALU.mult, op1=ALU.mult), reads=rd + [t.brs, bgain], writes=wr)
            yield
        elif gain3 is None:
            A("dve", lambda e: e.tensor_tensor(out=out3, in0=src3, in1=t.rs[:, 0:H].unsqueeze(2).to_broadcast([128, H, Dh]), op=ALU.mult),
              reads=rd + [t.brs], writes=wr)
            yield
        else:
            A("dve", lambda e: e.tensor_tensor(out=tmp3, in0=src3, in1=t.rs[:, 0:H].unsqueeze(2).to_broadcast([128, H, Dh]), op=ALU.mult),
              reads=rd + [t.brs], writes=[t.btmp])
            yield
            A("pool", lambda e: e.tensor_tensor(out=out3, in0=tmp3, in1=gain3, op=ALU.mult), reads=[t.btmp, bgain], writes=wr)
            yield

    def rope(t, src3, rd, H, out3, wr):
        Cb = t.cst[:, 0:32].unsqueeze(1).to_broadcast([128, H, 32])
        S1 = t.cst[:, 32:48].unsqueeze(1).to_broadcast([128, H, 16])
        S2 = t.cst[:, 48:64].unsqueeze(1).to_broadcast([128, H, 16])
        A("dve", lambda e: e.tensor_tensor(out=t.rt1[:, 0:H, :], in0=src3, in1=Cb, op=ALU.mult), reads=rd + [t.bcst], writes=[t.brt1])
        A("pool", lambda e: e.tensor_tensor(out=t.rt2[:, 0:H, 0:16], in0=src3[:, :, 16:32], in1=S1, op=ALU.mult), reads=rd + [t.bcst], writes=[(t.brt2, 0)])
        A("pool", lambda e: e.tensor_tensor(out=t.rt2[:, 0:H, 16:32], in0=src3[:, :, 0:16], in1=S2, op=ALU.mult), reads=rd + [t.bcst], writes=[(t.brt2, 1)])
        yield
        A("dve", lambda e: e.tensor_tensor(out=out3, in0=t.rt1[:, 0:H, :], in1=t.rt2[:, 0:H, :], op=ALU.add), reads=[t.brt1, t.brt2], writes=wr)
        yield

    def blockgen(j, si):
        t = tsl[si]
        own = j < NOWN
        g = j // 2
        tb = j % 2
        hT2, bhT2 = hT2s[g % 2]
        if tb == 0:
            A("pool", lambda e: e.dma_start(out=hT2[:].rearrange("p k t -> p (k t)"), in_=hT_s[g]), reads=[(bhT_s, g)], writes=[bhT2], dma=True)
        A("pool", lambda e: e.dma_start(out=t.cst[:], in_=cs[j * 128:(j + 1) * 128, :]), writes=[t.bcst], dma=True)
        yield
        if own:
            chunks = [(0, 512, "pj"), (512, 1024, "pj"), (1024, 1536, "va"), (1536, 1952, "pj")]
            o_cq, o_ckv, o_kr = 1536, 1792, 1920
        else:
            chunks = [(512, 1024, "pj"), (1024, 1536, "va"), (1792, 1952, "pj")]
            o_cq, o_ckv, o_kr = None, 1792, 1920
        ptb = bankb(t.tb)
        btb = pbuf[t.tb]
        for (c0, c1, dst) in chunks:
            b_ = pjctr[0] % 2
            pjctr[0] += 1
            for kt in range(8):
                A("pe", lambda e, b_=b_, c0=c0, c1=c1, kt=kt: e.matmul(
                    bank(b_)[:, 0:c1 - c0], lhsT=hT2[:, kt, tb * 128:(tb + 1) * 128], rhs=Win[:, kt, c0:c1],
                    start=(kt == 0), stop=(kt == 7)), reads=[bhT2, (bWin, kt)], writes=[pbuf[b_]])
            if dst == "pj":
                A("act", lambda e, b_=b_, c0=c0, c1=c1: e.copy(out=t.pj[:, c0:c1], in_=bank(b_)[:, 0:c1 - c0]), reads=[pbuf[b_]], writes=[(t.bpj, c0)])
            else:
                A("act", lambda e, b_=b_: e.copy(out=t.vaS[:, :, 0:64], in_=bank(b_).rearrange("p (h d) -> p h d", h=8)),
                  reads=[pbuf[b_]], writes=[t.bvaS])
                A("sp", lambda e: e.dma_start(out=kc_va(j), in_=t.vaS[:].rearrange("p h d -> p (h d)")), reads=[t.bvaS], writes=[(bkc, g)], dma=True)
            yield
        if own:
            yield from headnorm(t, t.pj[:, 0:512].rearrange("p (h d) -> p h d", h=8), [(t.bpj, 0)], 8, 64, gqa[:], bgqa,
                                t.qan[:].rearrange("p (h d) -> p h d", h=8), [t.bqan])
            for t_ in range(4):
                A("pe", lambda e, t_=t_: e.transpose(out=ptb[:, t_ * 128:(t_ + 1) * 128], in_=t.qan[:, t_ * 128:(t_ + 1) * 128], identity=ident[:]),
                  reads=[t.bqan, bident], writes=[btb])
            yield
            A("act", lambda e: e.copy(out=t.qaT[:], in_=ptb[:, 0:512]), reads=[btb], writes=[t.bqaT])
            A("sp", lambda e: e.dma_start(out=qaT_s[j], in_=t.qaT[:]), reads=[t.bqaT], writes=[(bqaT_s, j)], dma=True)
            yield
        yield from headnorm(t, t.pj[:, 512:1024].rearrange("p (h d) -> p h d", h=8), [(t.bpj, 512)], 8, 64, None, None,
                            t.kan[:].rearrange("p (h d) -> p h d", h=8), [t.bkan])
        for t_ in range(4):
            A("pe", lambda e, t_=t_: e.transpose(out=ptb[:, t_ * 128:(t_ + 1) * 128], in_=t.kan[:, t_ * 128:(t_ + 1) * 128], identity=ident[:]),
              reads=[t.bkan, bident], writes=[btb])
        yield
        A("act", lambda e: e.copy(out=t.kaT[:], in_=ptb[:, 0:512]), reads=[btb], writes=[t.bkaT])
        A("sp", lambda e: e.dma_start(out=kc_ka(j), in_=t.kaT[:]), reads=[t.bkaT], writes=[(bkc, g)], dma=True)
        yield
        pjk = (t.bpj, 1536 if own else 1792)
        if own:
            yield from headnorm(t, t.pj[:, o_cq:o_cq + 256].unsqueeze(1), [pjk], 1, 256, gcq[:], bgcq, t.cqn[:].unsqueeze(1), [t.bcqn])
            for t_ in range(2):
                A("pe", lambda e, t_=t_: e.transpose(out=ptb[:, t_ * 128:(t_ + 1) * 128], in_=t.cqn[:, t_ * 128:(t_ + 1) * 128], identity=ident[:]),
                  reads=[t.bcqn, bident], writes=[btb])
            yield
            A("act", lambda e: e.copy(out=t.cqnT[:].rearrange("p k t -> p (k t)"), in_=ptb[:, 0:256]), reads=[btb], writes=[t.bcqnT])
            yield
            for c_ in range(2):
                bb_ = t.b2[c_]
                for kt in range(2):
                    A("pe", lambda e, c_=c_, kt=kt, bb_=bb_: e.matmul(bank(bb_)[:, 0:384], lhsT=t.cqnT[:, kt, :], rhs=Wuq[:, kt, c_ * 384:(c_ + 1) * 384],
                                                                     start=(kt == 0), stop=(kt == 1)), reads=[t.bcqnT, bWuq], writes=[pbuf[bb_]])
                A("act", lambda e, c_=c_, bb_=bb_: e.copy(out=t.qf[:, c_ * 4:(c_ + 1) * 4, :], in_=bank(bb_)[:, 0:384].rearrange("p (h d) -> p h d", h=4)),
                  reads=[pbuf[bb_]], writes=[(t.bqf, c_)])
            yield
            yield from headnorm(t, t.qf[:, :, 0:64], [t.bqf], 8, 64, gqn[:], bgqn, t.qfull[:, :, 0:64], [(t.bqfull, 0)])
            yield from headnorm(t, t.qf[:, :, 64:96], [t.bqf], 8, 32, gqr[:], bgqr, t.qr[:], [t.bqr])
            yield from rope(t, t.qr[:], [t.bqr], 8, t.qfull[:, :, 64:96], [(t.bqfull, 1)])
            for h in range(8):
                A("pe", lambda e, h=h: e.transpose(out=ptb[0:96, h * 128:(h + 1) * 128], in_=t.qfull[:, h, :], identity=ident[:]),
                  reads=[t.bqfull, bident], writes=[btb])
            yield
            A("act", lambda e: e.copy(out=t.qbT[:].rearrange("p h t -> p (h t)"), in_=ptb[0:96, :]), reads=[btb], writes=[t.bqbT])
            A("sp", lambda e: e.dma_start(out=qbT_s[:, :, j * 128:(j + 1) * 128].rearrange("h p t -> p h t"), in_=t.qbT[:]),
              reads=[t.bqbT], writes=[(bqbT_s, j)], dma=True)
            yield
        yield from headnorm(t, t.pj[:, o_ckv:o_ckv + 128].unsqueeze(1), [pjk], 1, 128, gckv[:], bgckv, t.ckvn[:].unsqueeze(1), [t.bckvn])
        A("pe", lambda e: e.transpose(out=ptb[:, 0:128], in_=t.ckvn[:], identity=ident[:]), reads=[t.bckvn, bident], writes=[btb])
        yield
        A("act", lambda e: e.copy(out=t.ckvnT[:], in_=ptb[:, 0:128]), reads=[btb], writes=[t.bckvnT])
        yield
        for c_ in range(2):
            bb_ = t.b2[c_]
            A("pe", lambda e, c_=c_, bb_=bb_: e.matmul(bank(bb_), lhsT=t.ckvnT[:], rhs=Wukv[:, 0, c_ * 512:(c_ + 1) * 512], start=True, stop=True),
              reads=[t.bckvnT, bWukv], writes=[pbuf[bb_]])
            A("act", lambda e, c_=c_, bb_=bb_: e.copy(out=t.kvf[:, c_ * 4:(c_ + 1) * 4, :], in_=bank(bb_).rearrange("p (h d) -> p h d", h=4)),
              reads=[pbuf[bb_]], writes=[(t.bkvf, c_)])
        yield
        A("dve", lambda e: e.tensor_copy(out=t.vbS[:, :, 0:64], in_=t.kvf[:, :, 64:128]), reads=[t.bkvf], writes=[t.bvbS])
        A("sp", lambda e: e.dma_start(out=kc_vb(j), in_=t.vbS[:]), reads=[t.bvbS], writes=[(bkc, g)], dma=True)
        yield from headnorm(t, t.kvf[:, :, 0:64], [t.bkvf], 8, 64, None, None, t.kfull[:, :, 0:64], [(t.bkfull, 0)])
        yield from headnorm(t, t.pj[:, o_kr:o_kr + 32].unsqueeze(1), [pjk], 1, 32, gkr[:], bgkr, t.krn[:], [t.bkrn])
        yield from rope(t, t.krn[:], [t.bkrn], 1, t.kpe[:].unsqueeze(1), [t.bkpe])
        A("pool", lambda e: e.tensor_copy(out=t.kfull[:, :, 64:96], in_=t.kpe[:].unsqueeze(1).to_broadcast([128, 8, 32])), reads=[t.bkpe], writes=[(t.bkfull, 1)])
        yield
        for h in range(8):
            A("pe", lambda e, h=h: e.transpose(out=ptb[0:96, h * 128:(h + 1) * 128], in_=t.kfull[:, h, :], identity=ident[:]),
              reads=[t.bkfull, bident], writes=[btb])
        yield
        A("act", lambda e: e.copy(out=t.kbT[:].rearrange("p h t -> p (h t)"), in_=ptb[0:96, :]), reads=[btb], writes=[t.bkbT])
        A("sp", lambda e: e.dma_start(out=kc_kb(j), in_=t.kbT[:]), reads=[t.bkbT], writes=[(bkc, g)], dma=True)
        yield

    RG = [[0, 1], [2, 3], [4, 5], [6, 7]]

    def emit_gather(g):
        A("pool", lambda e: e.collective_compute("AllGather", ALU.bypass, replica_groups=RG,
                                                 ins=[kc[g * 640:(g + 1) * 640, :].opt()], outs=[kg[g * 1280:(g + 1) * 1280, :].opt()]),
          reads=[(bkc, g)], writes=[(bkg, g)], cc=True)

    active = [None] * NSL
    actj = [None] * NSL
    done = set()
    nextj = 0
    for si in range(NSL - 1):
        active[si] = blockgen(nextj, si)
        actj[si] = nextj
        nextj += 1
        for sj in range(si + 1):
            for _ in range(3):
                next(active[sj])
    while True:
        alive = False
        for si in range(NSL):
            if active[si] is None and nextj < NOWN:
                active[si] = blockgen(nextj, si)
                actj[si] = nextj
                nextj += 1
            if active[si] is not None:
                alive = True
                try:
                    next(active[si])
                except StopIteration:
                    active[si] = None
                    done.add(actj[si])
                    g_ = actj[si] // 2
                    if (2 * g_ in done) and (2 * g_ + 1 in done):
                        emit_gather(g_)
        if not alive:
            break

    k.new_phase()
    Wg2, bWg2 = k.sb("Wg", [128, 8, DFF], BF16)
    Wo, bWo = k.sb("Wo", [128, 8, D], BF16)
    off_after_w = k.off
    KTs = [k.sb("KT%d" % i, [96, NBLK * 128], BF16) for i in range(2)]
    VAs = [k.sb("VA%d" % i, [128, NBLK, 80], BF16) for i in range(2)]
    QTs = [k.sb("QT%d" % i, [96, NOWN * 128], BF16) for i in range(2)]
    NSB = 4
    SKB = 4
    NPT = 6
    SBANKS = [0, 1, 2, 5]
    BC_B = 6
    pTs = [k.sb("pT%d" % i, [128, 512], BF16) for i in range(NPT)]
    rrow, brrow = k.sb("rrow", [128, 512], F32)
    bcs, bbcs = k.sb("bcs", [64, 512], F32)
    oTn = [k.sb("oTn%d" % i, [64, 512], BF16) for i in range(2)]

    def b_loads(h):
        KT, bKT = KTs[h % 2]
        VA, bVA = VAs[h % 2]
        QT, bQT = QTs[h % 2]
        KTv = KT[:].rearrange("p (r g t) -> p r g t", r=2, g=16)
        VAv = VA[:].rearrange("p c d -> p (c d)").rearrange("p (r g x) -> p r g x", r=2, g=16)
        for q_ in range(4):
            for r_ in range(2):
                A("sp", lambda e, r_=r_, q_=q_: e.dma_start(out=KTv[:, r_, q_ * 4:(q_ + 1) * 4], in_=kg_kb(h, r_)[:, q_ * 4:(q_ + 1) * 4]),
                  reads=[(bkg, g_) for g_ in range(q_ * 4, q_ * 4 + 4)], writes=[(bKT, (r_, q_))], dma=True)
                A("sp", lambda e, r_=r_, q_=q_: e.dma_start(out=VAv[:, r_, q_ * 4:(q_ + 1) * 4], in_=kg_vb(h, r_)[:, q_ * 4:(q_ + 1) * 4]),
                  reads=[(bkg, g_) for g_ in range(q_ * 4, q_ * 4 + 4)], writes=[(bVA, (r_, q_))], dma=True)
        A("sp", lambda e: e.dma_start(out=QT[:], in_=qbT_s[h]), reads=[bqbT_s], writes=[bQT], dma=True)

    items = []
    ng = 0
    for h in range(8):
        for J in range(8):
            blocks = []
            for jb in range(4 * J + 4):
                blocks.append((jb, 0, 0) if jb < 4 * J else (jb, (jb - 4 * J) * 128, 1))
            for jb in range(4 * J + 4):
                blocks.append((NOWN + jb, 0, 0) if jb < 4 * J else (NOWN + jb, (jb - 4 * J) * 128, 2))
            for bi, (blk, c0, kind) in enumerate(blocks):
                items.append(dict(h=h, J=J, blk=blk, c0=c0, kind=kind, bi=bi, nb=len(blocks), ng=ng,
                                  first=(J == 0 and bi == 0)))
            ng += 1

    def b_stage1(i, it):
        h, J, blk, c0, kind = it["h"], it["J"], it["blk"], it["c0"], it["kind"]
        if it["first"] and h == 0:
            b_loads(0)
        if J == 0 and it["bi"] == SKB + 1 and h + 1 < 8:
            b_loads(h + 1)
        KT, bKT = KTs[h % 2]
        QT, bQT = QTs[h % 2]
        ps_b = SBANKS[i % NSB]
        pT, bpT = pTs[i % NPT]
        ps = bank(ps_b)
        A("pe", lambda e: e.matmul(ps[:, c0:512], lhsT=KT[:, blk * 128:(blk + 1) * 128], rhs=QT[:, J * 512 + c0:(J + 1) * 512],
                                   start=True, stop=True), reads=[(bKT, (blk // 32, (blk % 32) // 8)), bQT], writes=[pbuf[ps_b]])
        A("act", lambda e: e.activation(out=pT[:, c0:512], in_=ps[:, c0:512], func=AF.Exp), reads=[pbuf[ps_b]], writes=[bpT])
        if kind == 1:
            A("pool", lambda e: e.tensor_tensor(out=pT[:, c0:c0 + 128], in0=pT[:, c0:c0 + 128], in1=maskE[:], op=ALU.mult),
              reads=[bpT, bmaskE], writes=[bpT])
        elif kind == 2:
            A("pool", lambda e: e.tensor_tensor(out=pT[:, c0:c0 + 128], in0=pT[:, c0:c0 + 128], in1=maskO[:], op=ALU.mult),
              reads=[bpT, bmaskO], writes=[bpT])

    pend_b = []

    def b_stage2(i, it):
        h, J, blk, c0, bi, nb = it["h"], it["J"], it["blk"], it["c0"], it["bi"], it["nb"]
        VA, bVA = VAs[h % 2]
        pT, bpT = pTs[i % NPT]
        po_b = 3 + (it["ng"] % 2)
        po = bank(po_b)
        A("pe", lambda e: e.matmul(po[0:80, c0:512], lhsT=VA[:, blk, :], rhs=pT[:, c0:512], start=(bi == 0), stop=(bi == nb - 1)),
          reads=[(bVA, (blk // 32, (blk % 32) // 8)), bpT], writes=[pbuf[po_b]])
        if bi == nb - 1:
            A("dve", lambda e: e.reciprocal(out=rrow[64:65, :], in_=po[64:65, :]), reads=[pbuf[po_b]], writes=[brrow])

            def epi():
                A("pe", lambda e: e.matmul(bank(BC_B)[0:64, :], lhsT=onesf[64:65, 0:64], rhs=rrow[64:65, :], start=True, stop=True),
                  reads=[bonesf, brrow], writes=[pbuf[BC_B]])
                A("dve", lambda e: e.tensor_copy(out=bcs[:], in_=bank(BC_B)[0:64, :]), reads=[pbuf[BC_B]], writes=[bbcs])
                on, bon = oTn[it["ng"] % 2]
                A("dve", lambda e: e.tensor_tensor(out=on[:], in0=po[0:64, :], in1=bcs[:], op=ALU.mult), reads=[pbuf[po_b], bbcs], writes=[bon])
                A("sp", lambda e: e.dma_start(out=oTB_s[h][:, J * 512:(J + 1) * 512], in_=on[:]), reads=[bon], writes=[(boTB_s, h * 8 + J)], dma=True)
            pend_b.append([6, epi])

    expB, bexpB = k.sb("expB", [128, 8, 6, 128], F32)
    for h in range(8):
        A("sp", lambda e, h=h: e.dma_start(out=expB[:, h].rearrange("p b q -> p (b q)"), in_=biasA[:, h * 768:(h + 1) * 768]),
          writes=[(bexpB, h)], dma=True)
        A("act", lambda e, h=h: e.activation(out=expB[:, h].rearrange("p b q -> p (b q)"), in_=expB[:, h].rearrange("p b q -> p (b q)"), func=AF.Exp),
          reads=[(bexpB, h)], writes=[(bexpB, h)])
    NR = 4
    ring = [(k.sb("rk%d" % i, [128, 2, 512], BF16), k.sb("rv%d" % i, [128, 2, 640], BF16)) for i in range(NR)]
    qTs_ = [k.sb("qTa%d" % i, [128, 4, 128], BF16) for i in range(2)]
    NSA = 3
    pe32 = [k.sb("pe32_%d" % i, [128, 3, 128], BF16) for i in range(NSA)]
    pTa = [k.sb("pTa%d" % i, [128, 3, 128], BF16) for i in range(NSA)]
    poS, bpoS = k.sb("poS", [128, 512], F32)
    rrA, brrA = k.sb("rrA", [128, 512], F32)
    oTAn = [k.sb("oTAn%d" % i, [64, 8, 128], BF16) for i in range(2)]
    PSA_B = 6
    POA_B = 7
    poa = bank(POA_B)

    def a_loads(m):
        (rk, brk), (rv, brv) = ring[m % NR]
        qT, bqT = qTs_[m % 2]
        for r_ in range(2):
            A("sp", lambda e, r_=r_: e.dma_start(out=rk[:, r_, :], in_=kg_ka(r_, m)), reads=[(bkg, m // 2)], writes=[(brk, r_)], dma=True)
            A("sp", lambda e, r_=r_: e.dma_start(out=rv[:, r_, :], in_=kg_va(r_, m)), reads=[(bkg, m // 2)], writes=[(brv, r_)], dma=True)
        A("sp", lambda e: e.dma_start(out=qT[:].rearrange("p t q -> p (t q)"), in_=qaT_s[m]), reads=[bqaT_s], writes=[bqT], dma=True)

    aitems = []
    for m in range(NOWN):
        nbk = 2 * min(m + 1, 3)
        for h in range(8):
            for hf in range(2):
                b0, b1 = hf * 3, min(nbk, hf * 3 + 3)
                if b1 > b0:
                    aitems.append(dict(m=m, h=h, b0=b0, b1=b1, nbk=nbk))

    def a_stage1(i, it):
        m, h, b0, b1 = it["m"], it["h"], it["b0"], it["b1"]
        if h == 0 and b0 == 0 and m == 0:
            a_loads(0)
        if h == 2 and b0 == 0 and m + 1 < NOWN:
            a_loads(m + 1)
        qT, bqT = qTs_[m % 2]
        t_ = h // 2
        pp = (h % 2) * 64
        psa = bank(PSA_B)
        pe_t, bpe_t = pe32[i % NSA]
        pT_t, bpT_t = pTa[i % NSA]
        n_ = b1 - b0
        for b_ in range(b0, b1):
            (rk_, brk_), _ = ring[(m - b_ // 2) % NR]
            side = b_ % 2
            A("pe", lambda e, b_=b_, rk_=rk_, side=side: e.matmul(
                psa[:, (b_ - b0) * 128:(b_ - b0 + 1) * 128], lhsT=rk_[pp:pp + 64, side, t_ * 128:(t_ + 1) * 128], rhs=qT[pp:pp + 64, t_, :],
                start=True, stop=True), reads=[(brk_, side), bqT], writes=[pbuf[PSA_B]])
        A("act", lambda e: e.activation(out=pe_t[:].rearrange("p b q -> p (b q)")[:, 0:n_ * 128], in_=psa[:, 0:n_ * 128], func=AF.Exp),
          reads=[pbuf[PSA_B]], writes=[bpe_t])
        A("dve" if (i % 3) != 2 else "pool", lambda e: e.tensor_tensor(out=pT_t[:, 0:n_, :], in0=pe_t[:, 0:n_, :], in1=expB[:, h, b0:b1, :], op=ALU.mult),
          reads=[bpe_t, (bexpB, h)], writes=[bpT_t])

    pend_a = []

    def a_stage2(i, it):
        m, h, b0, b1, nbk = it["m"], it["h"], it["b0"], it["b1"], it["nbk"]
        pT_t, bpT_t = pTa[i % NSA]
        hq = h % 4
        for b_ in range(b0, b1):
            _, (rv_, brv_) = ring[(m - b_ // 2) % NR]
            side = b_ % 2
            A("pe", lambda e, b_=b_, rv_=rv_, side=side: e.matmul(
                poa[0:80, hq * 128:(hq + 1) * 128], lhsT=rv_[:, side, h * 80:(h + 1) * 80], rhs=pT_t[:, b_ - b0, :],
                start=(b_ == 0), stop=(b_ == nbk - 1)), reads=[(brv_, side), bpT_t], writes=[pbuf[POA_B]])
        if hq == 3 and b1 == nbk:
            A("dve", lambda e: e.tensor_copy(out=poS[0:80, :], in_=poa[0:80, :]), reads=[pbuf[POA_B]], writes=[bpoS])
            A("dve", lambda e: e.reciprocal(out=rrA[64:65, :], in_=poS[64:65, :]), reads=[bpoS], writes=[brrA])
            pend_a.append([2, lambda: a_epi(m, h // 4)])

    def a_epi(m, hh):
        A("pe", lambda e: e.matmul(bank(BC_B)[0:64, :], lhsT=onesf[64:65, 0:64], rhs=rrA[64:65, :], start=True, stop=True),
          reads=[bonesf, brrA], writes=[pbuf[BC_B]])
        on, bon = oTAn[m % 2]
        A("dve", lambda e: e.tensor_tensor(out=on[:, hh * 4:(hh + 1) * 4, :].rearrange("p h q -> p (h q)"), in0=poS[0:64, :], in1=bank(BC_B)[0:64, :], op=ALU.mult),
          reads=[bpoS, pbuf[BC_B]], writes=[(bon, hh)])
        if hh == 1:
            A("sp", lambda e: e.dma_start(out=oTA_s[:, :, m * 128:(m + 1) * 128].rearrange("h p q -> p h q"), in_=on[:]),
              reads=[bon], writes=[(boTA_s, m)], dma=True)

    def a_step(j):
        if j < len(aitems):
            a_stage1(j, aitems[j])
        for pe_ in list(pend_a):
            pe_[0] -= 1
            if pe_[0] <= 0:
                pe_[1]()
                pend_a.remove(pe_)
        if 1 <= j <= len(aitems):
            a_stage2(j - 1, aitems[j - 1])

    NA_STEPS = len(aitems) + 4
    aj = 0
    nB = len(items)
    while aj < 40:
        a_step(aj)
        aj += 1
    for i in range(nB + SKB):
        if i < nB:
            b_stage1(i, items[i])
        for pe_ in list(pend_b):
            pe_[0] -= 1
            if pe_[0] <= 0:
                pe_[1]()
                pend_b.remove(pe_)
        if i >= SKB:
            b_stage2(i - SKB, items[i - SKB])
        if i == 96:
            load_w(Wo, bWo, w["w_out"], 8, D)
        if i in (160, 320, 480, 640):
            c_ = (160, 320, 480, 640).index(i)
            vsrc = w["ffn2_w_gate"].rearrange("(kt p) f -> p kt f", p=128)
            A("pool", lambda e, c_=c_, vsrc=vsrc: e.dma_start(out=Wg2[:, :, c_ * FCH:(c_ + 1) * FCH], in_=vsrc[:, :, c_ * FCH:(c_ + 1) * FCH]),
              writes=[(bWg2, c_)], dma=True)
        while aj < NA_STEPS and (aj - 40) * nB <= i * (NA_STEPS - 40):
            a_step(aj)
            aj += 1
    for pe_ in pend_b:
        pe_[1]()
    while aj < NA_STEPS:
        a_step(aj)
        aj += 1
    for pe_ in pend_a:
        pe_[1]()

    k.new_phase(keep=off_after_w)
    Wg, bWg = Wg2, bWg2
    Wu, bWu = k.sb("Wu", [128, 8, DFF], BF16)
    load_w_cols([(Wu, bWu)], [w["ffn2_w_up"]])
    Wd, bWd = ffn_weights_d("ffn2")
    g2, bg2 = k.sb("g2", [128, D], F32)
    gf, bgf = k.sb("gf", [128, D], F32)
    load_bc(g2[:], bg2, w["ffn2_norm"], D)
    load_bc(gf[:], bgf, w["final_norm"], D)
    T, (ssf, bssf, rsf, brsf), xns, xnTs = ffn_tiles()
    xgs = [k.sb("xg2_%d" % i, [128, 2, D], F32) for i in range(2)]
    oTs = [k.sb("oT%d" % i, [128, 8, 256], BF16) for i in range(2)]
    out_v = out.rearrange("(g t p) c -> g p t c", t=2, p=128)
    oTA_v = oTA_s.rearrange("(h2 par) d t -> par d h2 t", par=2)
    oTB_v = oTB_s.rearrange("(h2 par) d t -> par d h2 t", par=2)
    final_stores = []
    NG3 = 16
    cnt = [0]

    def p3_load_x(g):
        xg, bxg = xgs[g % 2]
        A("sp", lambda e: e.dma_start(out=xg[:], in_=x1_v[g]), reads=[(bx1_s, g)], writes=[bxg], dma=True)

    def p3_load(g):
        oT, boT = oTs[g % 2]
        for par in range(2):
            A("sp", lambda e, par=par: e.dma_start(out=oT[par * 64:(par + 1) * 64, 0:4, :], in_=oTA_v[par][:, :, g * 256:(g + 1) * 256]),
              reads=[boTA_s], writes=[(boT, par)], dma=True)
            A("sp", lambda e, par=par: e.dma_start(out=oT[par * 64:(par + 1) * 64, 4:8, :], in_=oTB_v[par][:, :, g * 256:(g + 1) * 256]),
              reads=[boTB_s], writes=[(boT, 2 + par)], dma=True)

    def p3_wout(g, t_):
        xg, bxg = xgs[g % 2]
        oT, boT = oTs[g % 2]
        for hf in range(2):
            bd = 5 + hf
            pd = bank(bd)
            for kt in range(8):
                A("pe", lambda e, kt=kt, hf=hf, pd=pd: e.matmul(pd, lhsT=oT[:, kt, t_ * 128:(t_ + 1) * 128], rhs=Wo[:, kt, hf * 512:(hf + 1) * 512],
                                                              start=(kt == 0), stop=(kt == 7)), reads=[boT, bWo], writes=[pbuf[bd]])
            A("dve", lambda e, hf=hf, pd=pd: e.tensor_tensor(out=xg[:, t_, hf * 512:(hf + 1) * 512], in0=pd, in1=xg[:, t_, hf * 512:(hf + 1) * 512], op=ALU.add),
              reads=[pbuf[bd], (bxg, t_)], writes=[(bxg, t_)])

    def p3_pre_chain(g, t_):
        xg, bxg = xgs[g % 2]
        xn, bxn = xns[cnt[0] % 2]
        cnt[0] += 1
        norm_chain(xg[:, t_, :], bxg, t_, g2, bg2, xn[:], bxn, None, (ssf, bssf, rsf, brsf, xn, bxn))
        return xn, bxn

    def p3_final(g, t_):
        xg, bxg = xgs[g % 2]
        xn, bxn = xns[cnt[0] % 2]
        cnt[0] += 1
        norm_chain(xg[:, t_, :], bxg, t_, gf, bgf, xg[:, t_, :], bxg, t_, (ssf, bssf, rsf, brsf, xn, bxn))

    def p3_store(g):
        xg, bxg = xgs[g % 2]
        st_ = A("sp", lambda e: e.dma_start(out=out_v[g], in_=xg[:]), reads=[bxg], dma=True)
        final_stores.append(st_)

    def p3_hooks(g):
        hk = {}
        st = {}
        if g >= 1:
            gp = g - 1
            hk.setdefault(1, []).append(lambda: p3_final(gp, 0))
            hk.setdefault(3, []).append(lambda: (p3_final(gp, 1), p3_store(gp)))
        if g + 1 < NG3:
            gn = g + 1
            xnTn, bxnTn = xnTs[gn % 2]
            hk.setdefault(0, []).append(lambda: p3_load(gn))
            hk.setdefault(4, []).append(lambda: p3_load_x(gn))
            hk.setdefault(6, []).append(lambda: p3_wout(gn, 0))
            hk.setdefault(7, []).append(lambda: p3_wout(gn, 1))
            hk.setdefault(9, []).append(lambda: st.__setitem__("q0", p3_pre_chain(gn, 0)))
            hk.setdefault(14, []).append(lambda: transposes_to(st["q0"][0], st["q0"][1], xnTn[:, :, 0:128], bxnTn, 0))
            hk.setdefault(11, []).append(lambda: st.__setitem__("q1", p3_pre_chain(gn, 1)))
            hk.setdefault(16, []).append(lambda: transposes_to(st["q1"][0], st["q1"][1], xnTn[:, :, 128:256], bxnTn, 1))
        return hk

    p3_load(0)
    p3_load_x(0)
    for t_ in range(2):
        p3_wout(0, t_)
        xn, bxn = p3_pre_chain(0, t_)
        transposes_to(xn, bxn, xnTs[0][0][:, :, t_ * 128:(t_ + 1) * 128], xnTs[0][1], t_)
    for g in range(NG3):
        xg, bxg = xgs[g % 2]
        xnT, bxnT = xnTs[g % 2]
        ffn_group(xg, bxg, xnT, bxnT, Wg, bWg, Wu, bWu, Wd, bWd, T, p3_hooks(g))
    for t_ in range(2):
        p3_final(NG3 - 1, t_)
    p3_store(NG3 - 1)
    S.finalize_on(final_stores)
    S.emit(nc)
    return nc


_NC_CACHE = {}


def _host_inputs(inputs):
    x = np.ascontiguousarray(np.asarray(inputs["x"], dtype=np.float32))
    B, Sq, _ = x.shape
    rel_bias = np.asarray(inputs["a_rel_bias"], dtype=np.float32)[0]
    inv = (1.0 / (np.float32(10000.0) ** (np.arange(0, 32, 2, dtype=np.float32) / np.float32(32)))).astype(np.float32)
    k_i = np.arange(128)[:, None]
    q_i = np.arange(128)[None, :]
    shared = {}
    for nm in ["ffn1_norm", "ffn1_w_gate", "ffn1_w_up", "ffn1_w_down", "mix_norm", "w_in", "a_q_norm", "a_k_norm",
               "b_q_lat_norm", "b_w_uq", "b_kv_lat_norm", "b_w_ukv", "b_q_nope_norm", "b_q_rope_norm", "b_k_nope_norm",
               "b_k_rope_norm", "w_out", "ffn2_norm", "ffn2_w_gate", "ffn2_w_up", "ffn2_w_down", "final_norm"]:
        a = np.asarray(inputs[nm], dtype=np.float32)
        if a.ndim == 3:
            a = a[0]
        shared[nm] = np.ascontiguousarray(a)
    shared["ident"] = np.eye(128, dtype=np.float32)
    per_par = {}
    for p in range(2):
        bA = np.empty((128, 8, 6, 128), np.float32)
        for b in range(6):
            d = b // 2
            other = b % 2
            off = (1 - 2 * d - p) if other else (-2 * d - p)
            kpos = off * 128 + k_i
            rel = q_i - kpos
            ck = 2 * off + (k_i >= 64)
            cq = (q_i >= 64).astype(np.int64)
            vis = ((cq - ck) >= 0) & ((cq - ck) <= 8)
            idx = np.clip(rel, -128, 128) + 128
            vals = rel_bias[:, idx]
            sel = np.where(vis[None], vals, np.float32(-100.0))
            bA[:, :, b, :] = sel.transpose(1, 0, 2)
        gblk = np.arange(32) * 2 + p
        pos = (gblk[:, None] * 128 + np.arange(128)[None, :]).reshape(-1).astype(np.float32)
        ang = pos[:, None] * inv[None, :]
        c = np.cos(ang).astype(np.float32)
        s = np.sin(ang).astype(np.float32)
        cs = np.concatenate([c, c, -s, s], axis=1).astype(np.float32)
        diag = np.where((k_i >= 64) & (q_i < 64), 0.0, 1.0).astype(np.float32)
        ones = np.ones((128, 128), np.float32)
        zeros = np.zeros((128, 128), np.float32)
        per_par[p] = dict(biasA=np.ascontiguousarray(bA.reshape(128, -1)), cs=np.ascontiguousarray(cs),
                          maskE=(diag if p == 0 else ones), maskO=(zeros if p == 0 else diag), gblk=gblk)
    in_maps = []
    for c_ in range(8):
        b = c_ // 2
        p = c_ % 2
        xb = x[b].reshape(64, 128, 1024)
        xs = np.ascontiguousarray(xb[per_par[p]["gblk"]].reshape(32 * 128, 1024))
        m = dict(shared)
        m["xs"] = xs
        m["cs"] = per_par[p]["cs"]
        m["biasA"] = per_par[p]["biasA"]
        m["maskO"] = per_par[p]["maskO"]
        m["maskE"] = per_par[p]["maskE"]
        in_maps.append(m)
    return in_maps


def kernel(**inputs):
    in_maps = _host_inputs(inputs)
    if "nc" not in _NC_CACHE:
        _NC_CACHE["nc"] = build_program()
    nc = _NC_CACHE["nc"]
    res = run_bass_kernel_spmd(nc, in_maps, core_ids=list(range(8)))
    out = np.empty((4, 8192, 1024), np.float32)
    ov = out.reshape(4, 64, 128, 1024)
    for c_ in range(8):
        b = c_ // 2
        p = c_ % 2
        r = np.asarray(res.results[c_]["out"]).reshape(32, 128, 1024)
        ov[b, p::2] = r
    return out
```

```python
import numpy as np
import concourse.bass as bass
import concourse.mybir as mybir
from concourse.bass_utils import run_bass_kernel_spmd

F32 = mybir.dt.float32
BF16 = mybir.dt.bfloat16
AF = mybir.ActivationFunctionType
ALU = mybir.AluOpType
AX = mybir.AxisListType

ENGS = ("pe", "act", "dve", "pool", "sp")
DMA_RING = 8


class Op:
    __slots__ = ("eng", "fn", "deps", "marked", "ord", "is_dma", "dsem", "dval", "idx", "tag", "is_cc")

    def __init__(self, eng, fn, is_dma, tag=""):
        self.eng = eng
        self.fn = fn
        self.deps = []
        self.marked = False
        self.ord = 0
        self.is_dma = is_dma
        self.dsem = None
        self.dval = 0
        self.idx = 0
        self.tag = tag
        self.is_cc = False


class Buf:
    def __init__(self, sched, name):
        self.name = name
        self.w = {}
        self.r = {}
        self.base = list(sched.barrier_ops)

    def _keys(self, key):
        if key is None:
            return set(self.w.keys()) | set(self.r.keys())
        return {key, None}


class Sched:
    def __init__(self, same_eng_sync=True):
        self.ops = {e: [] for e in ENGS}
        self.barrier_ops = []
        self.same_eng_sync = same_eng_sync
        self.final_ops = []
        self.ndma = {e: 0 for e in ENGS}

    def buf(self, name):
        return Buf(self, name)

    @staticmethod
    def _norm(x):
        if isinstance(x, tuple):
            return x
        return (x, None)

    def add(self, eng, fn, reads=(), writes=(), dma=False, tag="", cc=False):
        o = Op(eng, fn, dma or cc, tag)
        o.is_cc = cc
        deps = {}

        def dep(d):
            if d is None or d is o:
                return
            deps[id(d)] = d

        lane = ("cc", id(o)) if cc else (eng if not dma else ("dma", eng, self.ndma[eng] % DMA_RING))
        for x in reads:
            b, k = self._norm(x)
            for d in b.base:
                dep(d)
            for kk in b._keys(k):
                dep(b.w.get(kk))
        for x in writes:
            b, k = self._norm(x)
            for d in b.base:
                dep(d)
            for kk in b._keys(k):
                dep(b.w.get(kk))
                for d in b.r.get(kk, {}).values():
                    dep(d)
        for x in reads:
            b, k = self._norm(x)
            b.r.setdefault(k, {})[lane] = o
        for x in writes:
            b, k = self._norm(x)
            if k is None:
                b.w = {None: o}
                b.r = {}
            else:
                b.w[k] = o
                b.r[k] = {}
            b.base = []
        for d in deps.values():
            if d.eng == eng and not d.is_dma and not (dma or cc):
                if eng == "pe" or not self.same_eng_sync:
                    continue
            o.deps.append(d)
            d.marked = True
        o.idx = len(self.ops[eng])
        self.ops[eng].append(o)
        if dma:
            self.ndma[eng] += 1
        return o

    def barrier(self):
        bo = []
        for e in ENGS:
            last = None
            dmas = []
            for o in reversed(self.ops[e]):
                if o.is_dma:
                    if len(dmas) < DMA_RING:
                        dmas.append(o)
                elif last is None:
                    last = o
                if last is not None and len(dmas) >= DMA_RING:
                    break
            if last is not None:
                bo.append(last)
            bo.extend(dmas)
        self.barrier_ops = bo

    def finalize_on(self, ops):
        self.final_ops.extend(ops)
        for o in ops:
            o.marked = True

    def emit(self, nc):
        stack_sems = {}
        import contextlib
        with contextlib.ExitStack() as es:
            SEM_LIM = 30000
            nmark = {e: sum(1 for o in self.ops[e] if (o.marked and not o.is_dma)) for e in ENGS}
            esem = {e: [es.enter_context(nc.semaphore("s_%s_%d" % (e, i)))
                        for i in range(max(1, (nmark[e] + SEM_LIM - 1) // SEM_LIM))] for e in ENGS}
            dsem = {e: [es.enter_context(nc.semaphore("d_%s_%d" % (e, i))) for i in range(DMA_RING)]
                    for e in ("sp", "act", "pool") if self.ndma[e] > 0}
            for e in ENGS:
                c = 0
                for o in self.ops[e]:
                    if not o.is_dma and o.marked:
                        c += 1
                    o.ord = c
                k = 0
                for o in self.ops[e]:
                    if o.is_cc:
                        o.dsem = es.enter_context(nc.semaphore("cc_%s_%d" % (e, o.idx)))
                        o.dval = 1
                    elif o.is_dma:
                        o.dsem = dsem[e][k % DMA_RING]
                        o.dval = 16 * (k // DMA_RING + 1)
                        k += 1
            block = es.enter_context(nc.Block())

            def run(e, eng):
                waited = {}

                def wait(sem, val):
                    key = id(sem)
                    if waited.get(key, 0) >= val:
                        return
                    waited[key] = val
                    eng.wait_ge(sem, val)

                for o in self.ops[e]:
                    for d in o.deps:
                        if d.is_dma:
                            wait(d.dsem, d.dval)
                        else:
                            wait(esem[d.eng][(d.ord - 1) // SEM_LIM], (d.ord - 1) % SEM_LIM + 1)
                    if o.is_cc:
                        ins = o.fn(eng)
                        ins.then_inc(o.dsem)
                    elif o.is_dma:
                        if o.dval > 16:
                            wait(o.dsem, o.dval - 16)
                        ins = o.fn(eng)
                        ins.then_inc(o.dsem, 16)
                    else:
                        ins = o.fn(eng)
                        if o.marked:
                            ins.then_inc(esem[e][(o.ord - 1) // SEM_LIM], 1)
                if e == "sp":
                    for d in self.final_ops:
                        if d.is_dma:
                            wait(d.dsem, d.dval)
                        else:
                            wait(esem[d.eng][(d.ord - 1) // SEM_LIM], (d.ord - 1) % SEM_LIM + 1)

            @block.tensor
            def _(eng):
                run("pe", eng)

            @block.scalar
            def _(eng):
                run("act", eng)

            @block.vector
            def _(eng):
                run("dve", eng)

            @block.gpsimd
            def _(eng):
                run("pool", eng)

            @block.sync
            def _(eng):
                run("sp", eng)

import contextlib

D = 1024
DFF = 2816
NFT = 22
INC = 1952
NBLK = 64
NOWN = 32
EPS = 1e-6
SB_LO, SB_HI = 16512, 229344


class KB:
    def __init__(self, nc):
        self.nc = nc
        self.S = Sched()
        self.off = SB_LO
        self.phase_base = SB_LO
        self.uid = 0

    def sb(self, name, shape, dt):
        sz = int(np.prod(shape[1:])) * (4 if dt == F32 else 2)
        sz = (sz + 63) // 64 * 64
        self.uid += 1
        t = self.nc.alloc_sbuf_tensor_at("%s_%d" % (name, self.uid), list(shape), dt, offset=self.off)
        self.off += sz
        assert self.off <= SB_HI, ("SBUF overflow", name, self.off)
        return t, self.S.buf(name)

    def persist_done(self):
        self.phase_base = self.off

    def new_phase(self, keep=None):
        self.S.barrier()
        self.off = self.phase_base if keep is None else keep


def build_program(debug=False):
    nc = bass.Bass("TRN2", target_bir_lowering=False)
    k = KB(nc)
    S = k.S
    A = S.add

    def din(name, shape, dt=F32):
        return nc.dram_tensor(name, list(shape), dt, kind="ExternalInput").ap()

    xs = din("xs", [NOWN * 128, D])
    cs = din("cs", [NOWN * 128, 64])
    biasA = din("biasA", [128, 8 * 6 * 128])
    maskO_in = din("maskO", [128, 128])
    maskE_in = din("maskE", [128, 128])
    ident_in = din("ident", [128, 128])
    w = {}
    for nm, shp in [("ffn1_norm", [1, D]), ("ffn1_w_gate", [D, DFF]), ("ffn1_w_up", [D, DFF]), ("ffn1_w_down", [DFF, D]),
                    ("mix_norm", [1, D]), ("w_in", [D, INC]), ("a_q_norm", [1, 64]), ("a_k_norm", [1, 64]),
                    ("b_q_lat_norm", [1, 256]), ("b_w_uq", [256, 768]), ("b_kv_lat_norm", [1, 128]), ("b_w_ukv", [128, 1024]),
                    ("b_q_nope_norm", [1, 64]), ("b_q_rope_norm", [1, 32]), ("b_k_nope_norm", [1, 64]), ("b_k_rope_norm", [1, 32]),
                    ("w_out", [D, D]), ("ffn2_norm", [1, D]), ("ffn2_w_gate", [D, DFF]), ("ffn2_w_up", [D, DFF]),
                    ("ffn2_w_down", [DFF, D]), ("final_norm", [1, D])]:
        w[nm] = din(nm, shp)
    okind = "ExternalOutput"
    out = nc.dram_tensor("out", [NOWN * 128, D], F32, kind=okind).ap()
    skind = "ExternalOutput" if debug else "Internal"

    def dscr(name, shape, dt):
        t = nc.dram_tensor(name, list(shape), dt, kind=skind) if debug else nc.dram_tensor(name, list(shape), dt)
        return t.ap(), S.buf(name)

    x1_s, bx1_s = dscr("x1_s", [NOWN * 128, D], F32)
    hT_s, bhT_s = dscr("hT_s", [16, 128, 8 * 256], BF16)
    CH = 655360
    OFF_KA, OFF_VA, OFF_KB, OFF_VB = 0, 131072, 294912, 491520
    kc, bkc = dscr("kc", [16 * 640, 1024], BF16)
    kg, bkg = dscr("kg", [16 * 1280, 1024], BF16)
    kcf = kc.rearrange("r c -> (r c)")
    kgf = kg.rearrange("r c -> (r c)")

    def kc_ka(j):
        o = (j // 2) * CH + OFF_KA + (j % 2) * 65536
        return kcf[o:o + 65536].rearrange("(p c) -> p c", c=512)

    def kc_va(j):
        o = (j // 2) * CH + OFF_VA + (j % 2) * 81920
        return kcf[o:o + 81920].rearrange("(p c) -> p c", c=640)

    def kc_kb(j):
        o = (j // 2) * CH + OFF_KB
        return kcf[o:o + 196608].rearrange("(h p t) -> p h t", h=8, p=96)[:, :, (j % 2) * 128:(j % 2) * 128 + 128]

    def kc_vb(j):
        o = (j // 2) * CH + OFF_VB
        return kcf[o:o + 163840].rearrange("(h p t) -> p h t", h=8, p=128)[:, :, (j % 2) * 80:(j % 2) * 80 + 80]

    def kg_ka(r_, m):
        o = ((m // 2) * 2 + r_) * CH + OFF_KA + (m % 2) * 65536
        return kgf[o:o + 65536].rearrange("(p c) -> p c", c=512)

    def kg_va(r_, m):
        o = ((m // 2) * 2 + r_) * CH + OFF_VA + (m % 2) * 81920
        return kgf[o:o + 81920].rearrange("(p c) -> p c", c=640)

    kgv = kgf.rearrange("(g r x) -> g r x", g=16, r=2)

    def kg_kb(h, r_):
        return kgv[:, r_, OFF_KB + h * 24576:OFF_KB + (h + 1) * 24576].rearrange("g (p t) -> p g t", t=256)

    def kg_vb(h, r_):
        return kgv[:, r_, OFF_VB + h * 20480:OFF_VB + (h + 1) * 20480].rearrange("g (p t) -> p g t", t=160)

    qaT_s, bqaT_s = dscr("qaT_s", [NOWN, 128, 512], BF16)
    qbT_s, bqbT_s = dscr("qbT_s", [8, 96, NOWN * 128], BF16)
    oTA_s, boTA_s = dscr("oTA_s", [8, 64, NOWN * 128], BF16)
    oTB_s, boTB_s = dscr("oTB_s", [8, 64, NOWN * 128], BF16)

    P = [nc.alloc_psum_tensor("P%d" % i, [128, 1024], F32) for i in range(4)]
    Pb = [p.bitcast(BF16) for p in P]
    pbuf = [S.buf("bank%d" % i) for i in range(8)]

    def bank(i):
        return P[i // 2][:, (i % 2) * 512:(i % 2) * 512 + 512]

    def bankb(i):
        return Pb[i // 2][:, (i % 2) * 1024:(i % 2) * 1024 + 1024]

    ident, bident = k.sb("ident", [128, 128], BF16)
    identf, bidentf = k.sb("identf", [128, 128], F32)
    onesf, bonesf = k.sb("onesf", [128, 64], F32)
    maskO, bmaskO = k.sb("maskO", [128, 128], BF16)
    A("sp", lambda e: e.dma_start(out=identf[:], in_=ident_in), writes=[bidentf], dma=True)
    A("dve", lambda e: e.tensor_copy(out=ident[:], in_=identf[:]), reads=[bidentf], writes=[bident])
    A("sp", lambda e: e.dma_start(out=identf[:], in_=maskO_in), reads=[bidentf], writes=[bidentf], dma=True)
    A("dve", lambda e: e.tensor_copy(out=maskO[:], in_=identf[:]), reads=[bidentf], writes=[bmaskO])
    maskE, bmaskE = k.sb("maskE", [128, 128], BF16)
    A("sp", lambda e: e.dma_start(out=identf[:], in_=maskE_in), reads=[bidentf], writes=[bidentf], dma=True)
    A("dve", lambda e: e.tensor_copy(out=maskE[:], in_=identf[:]), reads=[bidentf], writes=[bmaskE])
    A("pool", lambda e: e.memset(onesf[:], 1.0), writes=[bonesf])
    k.persist_done()

    def load_w(dst, bdst, src, kt_n, ncols, rows_per=128):
        v = src.rearrange("(kt p) f -> p kt f", p=128)
        step = 1 if ncols >= 1024 else kt_n
        for k0 in range(0, kt_n, step):
            k1 = min(kt_n, k0 + step)
            A("pool", lambda e, k0=k0, k1=k1: e.dma_start(out=dst[:, k0:k1, :], in_=v[:, k0:k1, :]),
              writes=[(bdst, k0)], dma=True)

    FCH = 704

    def load_w_cols(dsts, srcs):
        for c_ in range(DFF // FCH):
            for (dst, bdst), src in zip(dsts, srcs):
                v = src.rearrange("(kt p) f -> p kt f", p=128)
                A("pool", lambda e, dst=dst, v=v, c_=c_: e.dma_start(out=dst[:, :, c_ * FCH:(c_ + 1) * FCH], in_=v[:, :, c_ * FCH:(c_ + 1) * FCH]),
                  writes=[(bdst, c_)], dma=True)

    def load_bc(dst, bdst, src, n):
        A("sp", lambda e: e.dma_start(out=dst, in_=src.partition_broadcast(128)), writes=[bdst], dma=True)

    def norm_chain(src, bsrc, skey, gbc, bgbc, dst, bdst, dkey, tl):
        (ss, bss, rs, brs, junk, bjunk) = tl
        A("dve", lambda e: e.memset(ss[:, 0:1], 0.0), writes=[bss])
        A("act", lambda e: e.activation(out=junk[:], in_=src, func=AF.Square, accum_out=ss[:, 0:1]),
          reads=[(bsrc, skey), bss], writes=[bjunk, bss])
        A("act", lambda e: e.activation(out=rs[:, 0:1], in_=ss[:, 0:1], func=AF.Sqrt, bias=EPS, scale=1.0 / D), reads=[bss], writes=[brs])
        A("dve", lambda e: e.reciprocal(out=rs[:, 0:1], in_=rs[:, 0:1]), reads=[brs], writes=[brs])
        A("dve", lambda e: e.scalar_tensor_tensor(out=dst, in0=src, scalar=rs[:, 0:1], in1=gbc[:], op0=ALU.mult, op1=ALU.mult),
          reads=[(bsrc, skey), brs, bgbc], writes=[(bdst, dkey)])

    def transposes_to(xn, bxn, dstT, bdstT, dkey):
        pt = bankb(0)
        for kt in range(8):
            A("pe", lambda e, kt=kt: e.transpose(out=pt[:, kt * 128:(kt + 1) * 128], in_=xn[:, kt * 128:(kt + 1) * 128], identity=ident[:]),
              reads=[bxn, bident], writes=[pbuf[0]])
        A("act", lambda e: e.copy(out=dstT, in_=pt.rearrange("p (k t) -> p k t", k=8)), reads=[pbuf[0]], writes=[(bdstT, dkey)])

    def ffn_group(xg, bxg, xnT, bxnT, Wg, bWg, Wu, bWu, Wd, bWd, T, hooks):
        (actT, bactT, sg, bsg) = T
        for ft in range(NFT):
            for fn in hooks.get(ft, []):
                fn()
            bg = 1 + (ft % 2)
            bu = 3 + (ft % 2)
            pg = bank(bg)[:, 0:256]
            pu = bank(bu)[:, 0:256]
            for kt in range(8):
                A("pe", lambda e, kt=kt, ft=ft, pg=pg: e.matmul(pg, lhsT=Wg[:, kt, ft * 128:(ft + 1) * 128], rhs=xnT[:, kt, :],
                                                               start=(kt == 0), stop=(kt == 7)),
                  reads=[(bWg, (ft * 128) // FCH), (bWg, (ft * 128 + 127) // FCH), bxnT], writes=[(pbuf[bg], 0)])
            for kt in range(8):
                A("pe", lambda e, kt=kt, ft=ft, pu=pu: e.matmul(pu, lhsT=Wu[:, kt, ft * 128:(ft + 1) * 128], rhs=xnT[:, kt, :],
                                                               start=(kt == 0), stop=(kt == 7)),
                  reads=[(bWu, (ft * 128) // FCH), (bWu, (ft * 128 + 127) // FCH), bxnT], writes=[(pbuf[bu], 1)])
            A("act", lambda e, pg=pg, ft=ft: e.activation(out=sg[:, ft % 2, :], in_=pg, func=AF.Silu), reads=[(pbuf[bg], 0)], writes=[(bsg, ft % 2)])
            A("dve", lambda e, pu=pu, ft=ft: e.tensor_tensor(out=actT[:, ft, :], in0=pu, in1=sg[:, ft % 2, :], op=ALU.mult),
              reads=[(pbuf[bu], 1), (bsg, ft % 2)], writes=[(bactT, ft)])
        i = 0
        for t_ in range(2):
            for hf in range(2):
                bd = 5 + (i % 2)
                i += 1
                pd = bank(bd)
                for ft in range(NFT):
                    A("pe", lambda e, ft=ft, t_=t_, hf=hf, pd=pd: e.matmul(pd, lhsT=actT[:, ft, t_ * 128:(t_ + 1) * 128],
                                                                         rhs=Wd[:, ft, hf * 512:(hf + 1) * 512],
                                                                         start=(ft == 0), stop=(ft == NFT - 1)),
                      reads=[bactT, (bWd, ft)], writes=[pbuf[bd]])
                A("dve", lambda e, t_=t_, hf=hf, pd=pd: e.scalar_tensor_tensor(out=xg[:, t_, hf * 512:(hf + 1) * 512], in0=pd, scalar=0.5,
                                                                               in1=xg[:, t_, hf * 512:(hf + 1) * 512], op0=ALU.mult, op1=ALU.add),
                  reads=[pbuf[bd], (bxg, t_)], writes=[(bxg, t_)])

    def ffn_tiles():
        actT, bactT = k.sb("actT", [128, NFT, 256], BF16)
        sg, bsg = k.sb("sg", [128, 2, 256], F32)
        ss, bss = k.sb("ss", [128, 2], F32)
        rs, brs = k.sb("rs", [128, 2], F32)
        xns = [k.sb("xn%d" % i, [128, D], BF16) for i in range(2)]
        xnTs = [k.sb("xnT%d" % i, [128, 8, 256], BF16) for i in range(2)]
        return (actT, bactT, sg, bsg), (ss, bss, rs, brs), xns, xnTs

    def ffn_weights_gu(pfx):
        Wg, bWg = k.sb("Wg", [128, 8, DFF], BF16)
        Wu, bWu = k.sb("Wu", [128, 8, DFF], BF16)
        load_w_cols([(Wg, bWg), (Wu, bWu)], [w[pfx + "_w_gate"], w[pfx + "_w_up"]])
        return Wg, bWg, Wu, bWu

    def ffn_weights_d(pfx):
        Wd, bWd = k.sb("Wd", [128, NFT, D], BF16)
        load_w(Wd, bWd, w[pfx + "_w_down"], NFT, D)
        return Wd, bWd

    def ffn_weights(pfx):
        Wg, bWg, Wu, bWu = ffn_weights_gu(pfx)
        Wd, bWd = ffn_weights_d(pfx)
        return Wg, bWg, Wu, bWu, Wd, bWd

    k.new_phase()
    Wg, bWg, Wu, bWu, Wd, bWd = ffn_weights("ffn1")
    g1, bg1 = k.sb("g1", [128, D], F32)
    gm, bgm = k.sb("gm", [128, D], F32)
    load_bc(g1[:], bg1, w["ffn1_norm"], D)
    load_bc(gm[:], bgm, w["mix_norm"], D)
    T, (ssf, bssf, rsf, brsf), xns, xnTs = ffn_tiles()
    xgs = [k.sb("xg%d" % i, [128, 2, D], F32) for i in range(2)]
    hTs = [k.sb("hT%d" % i, [128, 8, 256], BF16) for i in range(2)]
    xs_v = xs.rearrange("(g t p) c -> g p t c", t=2, p=128)
    x1_v = x1_s.rearrange("(g t p) c -> g p t c", t=2, p=128)
    NG1 = 16
    cnt = [0]

    def p1a_load(g):
        xg, bxg = xgs[g % 2]
        A("sp", lambda e: e.dma_start(out=xg[:], in_=xs_v[g]), writes=[bxg], dma=True)

    def p1a_pre_chain(g, t_):
        xg, bxg = xgs[g % 2]
        xn, bxn = xns[cnt[0] % 2]
        cnt[0] += 1
        norm_chain(xg[:, t_, :], bxg, t_, g1, bg1, xn[:], bxn, None, (ssf, bssf, rsf, brsf, xn, bxn))
        return xn, bxn

    def p1a_post_chain(g, t_):
        xg, bxg = xgs[g % 2]
        xn, bxn = xns[cnt[0] % 2]
        cnt[0] += 1
        norm_chain(xg[:, t_, :], bxg, t_, gm, bgm, xn[:], bxn, None, (ssf, bssf, rsf, brsf, xn, bxn))
        return xn, bxn

    def p1a_hooks(g):
        hk = {}
        st = {}
        if g >= 1:
            gp = g - 1
            hTp, bhTp = hTs[gp % 2]
            hk.setdefault(0, []).append(lambda: st.__setitem__("e0", p1a_post_chain(gp, 0)))
            hk.setdefault(5, []).append(lambda: transposes_to(st["e0"][0], st["e0"][1], hTp[:, :, 0:128], bhTp, 0))
            hk.setdefault(2, []).append(lambda: st.__setitem__("e1", p1a_post_chain(gp, 1)))

            def fin():
                transposes_to(st["e1"][0], st["e1"][1], hTp[:, :, 128:256], bhTp, 1)
                A("sp", lambda e: e.dma_start(out=hT_s[gp], in_=hTp[:].rearrange("p k t -> p (k t)")), reads=[bhTp], writes=[(bhT_s, gp)], dma=True)
            hk.setdefault(7, []).append(fin)
        if g + 1 < NG1:
            gn = g + 1
            xnTn, bxnTn = xnTs[gn % 2]
            hk.setdefault(3, []).append(lambda: p1a_load(gn))
            hk.setdefault(9, []).append(lambda: st.__setitem__("q0", p1a_pre_chain(gn, 0)))
            hk.setdefault(14, []).append(lambda: transposes_to(st["q0"][0], st["q0"][1], xnTn[:, :, 0:128], bxnTn, 0))
            hk.setdefault(11, []).append(lambda: st.__setitem__("q1", p1a_pre_chain(gn, 1)))
            hk.setdefault(16, []).append(lambda: transposes_to(st["q1"][0], st["q1"][1], xnTn[:, :, 128:256], bxnTn, 1))
        return hk

    p1a_load(0)
    for t_ in range(2):
        xn, bxn = p1a_pre_chain(0, t_)
        transposes_to(xn, bxn, xnTs[0][0][:, :, t_ * 128:(t_ + 1) * 128], xnTs[0][1], t_)
    for g in range(NG1):
        xg, bxg = xgs[g % 2]
        xnT, bxnT = xnTs[g % 2]
        ffn_group(xg, bxg, xnT, bxnT, Wg, bWg, Wu, bWu, Wd, bWd, T, p1a_hooks(g))
        if True:
            A("sp", lambda e, g=g, xg=xg: e.dma_start(out=x1_v[g], in_=xg[:]), reads=[bxg], writes=[(bx1_s, g)], dma=True)
    gp = NG1 - 1
    hTp, bhTp = hTs[gp % 2]
    for t_ in range(2):
        xn, bxn = p1a_post_chain(gp, t_)
        transposes_to(xn, bxn, hTp[:, :, t_ * 128:(t_ + 1) * 128], bhTp, t_)
    A("sp", lambda e: e.dma_start(out=hT_s[gp], in_=hTp[:].rearrange("p k t -> p (k t)")), reads=[bhTp], writes=[(bhT_s, gp)], dma=True)

    k.new_phase()
    Win, bWin = k.sb("Win", [128, 8, INC], BF16)
    Wuq, bWuq = k.sb("Wuq", [128, 2, 768], BF16)
    Wukv, bWukv = k.sb("Wukv", [128, 1, 1024], BF16)
    load_w(Win, bWin, w["w_in"], 8, INC)
    load_w(Wuq, bWuq, w["b_w_uq"], 2, 768)
    load_w(Wukv, bWukv, w["b_w_ukv"], 1, 1024)
    graw, bgraw = k.sb("graw", [128, 704], F32)
    goff = {}
    o_ = 0
    for nm, n_ in [("a_q_norm", 64), ("a_k_norm", 64), ("b_q_lat_norm", 256), ("b_kv_lat_norm", 128), ("b_q_nope_norm", 64),
                   ("b_q_rope_norm", 32), ("b_k_nope_norm", 64), ("b_k_rope_norm", 32)]:
        goff[nm] = (o_, n_)
        A("sp", lambda e, o_=o_, n_=n_, nm=nm: e.dma_start(out=graw[:, o_:o_ + n_], in_=w[nm].partition_broadcast(128)),
          writes=[(bgraw, nm)], dma=True)
        o_ += n_

    def gain_full(name, nm, H, scale):
        o0, n_ = goff[nm]
        gt, bgt = k.sb(name, [128, H, n_], F32)
        A("dve", lambda e: e.tensor_copy(out=gt[:], in_=graw[:, o0:o0 + n_].unsqueeze(1).to_broadcast([128, H, n_])),
          reads=[(bgraw, nm)], writes=[bgt])
        if scale != 1.0:
            A("dve", lambda e: e.tensor_scalar_mul(out=gt[:], in0=gt[:], scalar1=float(scale)), reads=[bgt], writes=[bgt])
        return gt, bgt

    gqa, bgqa = gain_full("gqa", "a_q_norm", 8, 0.125)
    gka, bgka = gain_full("gka", "a_k_norm", 8, 1.0)
    A("dve", lambda e: e.tensor_tensor(out=gqa[:], in0=gqa[:], in1=gka[:], op=ALU.mult), reads=[bgqa, bgka], writes=[bgqa])
    gcq, bgcq = gain_full("gcq", "b_q_lat_norm", 1, 1.0)
    gckv, bgckv = gain_full("gckv", "b_kv_lat_norm", 1, 1.0)
    SCB = 96.0 ** -0.5
    gqn, bgqn = gain_full("gqn", "b_q_nope_norm", 8, SCB)
    gqr, bgqr = gain_full("gqr", "b_q_rope_norm", 8, SCB)
    gkn, bgkn = gain_full("gkn", "b_k_nope_norm", 8, 1.0)
    A("dve", lambda e: e.tensor_tensor(out=gqn[:], in0=gqn[:], in1=gkn[:], op=ALU.mult), reads=[bgqn, bgkn], writes=[bgqn])
    gkr, bgkr = gain_full("gkr", "b_k_rope_norm", 1, 1.0)

    hT2s = [k.sb("hT2_%d" % i, [128, 8, 256], BF16) for i in range(2)]

    class TS:
        pass

    def mk_ts(si):
        t = TS()
        for nm, shp, dt in [("cst", [128, 64], F32), ("sq", [128, 1024], F32), ("tmp", [128, 1024], F32), ("ss", [128, 16], F32),
                            ("rs", [128, 16], F32), ("pj", [128, 1952], F32), ("qan", [128, 512], BF16), ("kan", [128, 512], BF16),
                            ("qaT", [128, 512], BF16), ("kaT", [128, 512], BF16), ("vaS", [128, 8, 80], BF16), ("vbS", [128, 8, 80], BF16),
                            ("cqn", [128, 256], BF16), ("cqnT", [128, 2, 128], BF16), ("qf", [128, 8, 96], F32), ("qfull", [128, 8, 96], BF16),
                            ("qr", [128, 8, 32], F32), ("rt1", [128, 8, 32], F32), ("rt2", [128, 8, 32], F32),
                            ("qbT", [96, 8, 128], BF16), ("kbT", [96, 8, 128], BF16), ("ckvn", [128, 128], BF16), ("ckvnT", [128, 128], BF16),
                            ("kvf", [128, 8, 128], F32), ("kfull", [128, 8, 96], BF16), ("krn", [128, 1, 32], F32), ("kpe", [128, 32], BF16)]:
            tt, bb = k.sb("%s_s%d" % (nm, si), shp, dt)
            setattr(t, nm, tt)
            setattr(t, "b" + nm, bb)
        for nm in ("vaS", "vbS"):
            tt, bb = getattr(t, nm), getattr(t, "b" + nm)
            A("pool", lambda e, tt=tt: e.memset(tt[:], 0.0), writes=[bb])
            A("pool", lambda e, tt=tt: e.memset(tt[:, :, 64:65], 1.0), reads=[bb], writes=[bb])
        t.b2 = [2 + 2 * si, 2 + 2 * si]
        t.tb = 3 + 2 * si
        return t

    NSL = 3
    tsl = [mk_ts(i) for i in range(NSL)]
    pjctr = [0]

    def headnorm(t, src3, rd, H, Dh, gain3, bgain, out3, wr):
        sq3 = t.sq[:, 0:H * Dh].rearrange("p (h d) -> p h d", h=H)
        tmp3 = t.tmp[:, 0:H * Dh].rearrange("p (h d) -> p h d", h=H)
        if H == 1:
            A("pool", lambda e: e.memset(t.ss[:, 0:1], 0.0), writes=[t.bss])
            A("act", lambda e: e.activation(out=sq3, in_=src3, func=AF.Square, accum_out=t.ss[:, 0:1]), reads=rd + [t.bss], writes=[t.bsq, t.bss])
            yield
        else:
            A("act", lambda e: e.activation(out=sq3, in_=src3, func=AF.Square), reads=rd, writes=[t.bsq])
            yield
            A("dve", lambda e: e.tensor_reduce(out=t.ss[:, 0:H], in_=sq3, axis=AX.X, op=ALU.add), reads=[t.bsq], writes=[t.bss])
            yield
        A("act", lambda e: e.activation(out=t.rs[:, 0:H], in_=t.ss[:, 0:H], func=AF.Sqrt, bias=EPS, scale=1.0 / Dh), reads=[t.bss], writes=[t.brs])
        yield
        A("dve", lambda e: e.reciprocal(out=t.rs[:, 0:H], in_=t.rs[:, 0:H]), reads=[t.brs], writes=[t.brs])
        yield
        if H == 1:
            A("dve", lambda e: e.scalar_tensor_tensor(out=out3[:, 0, :], in0=src3[:, 0, :], scalar=t.rs[:, 0:1], in1=gain3[:, 0, :],
                                                      op0=ALU.mult, op1=ALU.mult), reads=rd + [t.brs, bgain], writes=wr)
            yield
        elif gain3 is None:
            A("dve", lambda e: e.tensor_tensor(out=out3, in0=src3, in1=t.rs[:, 0:H].unsqueeze(2).to_broadcast([128, H, Dh]), op=ALU.mult),
              reads=rd + [t.brs], writes=wr)
            yield
        else:
            A("dve", lambda e: e.tensor_tensor(out=tmp3, in0=src3, in1=t.rs[:, 0:H].unsqueeze(2).to_broadcast([128, H, Dh]), op=ALU.mult),
              reads=rd + [t.brs], writes=[t.btmp])
            yield
            A("pool", lambda e: e.tensor_tensor(out=out3, in0=tmp3, in1=gain3, op=ALU.mult), reads=[t.btmp, bgain], writes=wr)
            yield

    def rope(t, src3, rd, H, out3, wr):
        Cb = t.cst[:, 0:32].unsqueeze(1).to_broadcast([128, H, 32])
        S1 = t.cst[:, 32:48].unsqueeze(1).to_broadcast([128, H, 16])
        S2 = t.cst[:, 48:64].unsqueeze(1).to_broadcast([128, H, 16])
        A("dve", lambda e: e.tensor_tensor(out=t.rt1[:, 0:H, :], in0=src3, in1=Cb, op=ALU.mult), reads=rd + [t.bcst], writes=[t.brt1])
        A("pool", lambda e: e.tensor_tensor(out=t.rt2[:, 0:H, 0:16], in0=src3[:, :, 16:32], in1=S1, op=ALU.mult), reads=rd + [t.bcst], writes=[(t.brt2, 0)])
        A("pool", lambda e: e.tensor_tensor(out=t.rt2[:, 0:H, 16:32], in0=src3[:, :, 0:16], in1=S2, op=ALU.mult), reads=rd + [t.bcst], writes=[(t.brt2, 1)])
        yield
        A("dve", lambda e: e.tensor_tensor(out=out3, in0=t.rt1[:, 0:H, :], in1=t.rt2[:, 0:H, :], op=ALU.add), reads=[t.brt1, t.brt2], writes=wr)
        yield

    def blockgen(j, si):
        t = tsl[si]
        own = j < NOWN
        g = j // 2
        tb = j % 2
        hT2, bhT2 = hT2s[g % 2]
        if tb == 0:
            A("sp", lambda e: e.dma_start(out=hT2[:].rearrange("p k t -> p (k t)"), in_=hT_s[g]), reads=[(bhT_s, g)], writes=[bhT2], dma=True)
        A("sp", lambda e: e.dma_start(out=t.cst[:], in_=cs[j * 128:(j + 1) * 128, :]), writes=[t.bcst], dma=True)
        yield
        if own:
            chunks = [(0, 512, "pj"), (512, 1024, "pj"), (1024, 1536, "va"), (1536, 1952, "pj")]
            o_cq, o_ckv, o_kr = 1536, 1792, 1920
        else:
            chunks = [(512, 1024, "pj"), (1024, 1536, "va"), (1792, 1952, "pj")]
            o_cq, o_ckv, o_kr = None, 1792, 1920
        ptb = bankb(t.tb)
        btb = pbuf[t.tb]
        for (c0, c1, dst) in chunks:
            b_ = pjctr[0] % 2
            pjctr[0] += 1
            for kt in range(8):
                A("pe", lambda e, b_=b_, c0=c0, c1=c1, kt=kt: e.matmul(
                    bank(b_)[:, 0:c1 - c0], lhsT=hT2[:, kt, tb * 128:(tb + 1) * 128], rhs=Win[:, kt, c0:c1],
                    start=(kt == 0), stop=(kt == 7)), reads=[bhT2, (bWin, kt)], writes=[pbuf[b_]])
            if dst == "pj":
                A("act", lambda e, b_=b_, c0=c0, c1=c1: e.copy(out=t.pj[:, c0:c1], in_=bank(b_)[:, 0:c1 - c0]), reads=[pbuf[b_]], writes=[(t.bpj, c0)])
            else:
                A("act", lambda e, b_=b_: e.copy(out=t.vaS[:, :, 0:64], in_=bank(b_).rearrange("p (h d) -> p h d", h=8)),
                  reads=[pbuf[b_]], writes=[t.bvaS])
                A("sp", lambda e: e.dma_start(out=kc_va(j), in_=t.vaS[:].rearrange("p h d -> p (h d)")), reads=[t.bvaS], writes=[(bkc, g)], dma=True)
            yield
        if own:
            yield from headnorm(t, t.pj[:, 0:512].rearrange("p (h d) -> p h d", h=8), [(t.bpj, 0)], 8, 64, gqa[:], bgqa,
                                t.qan[:].rearrange("p (h d) -> p h d", h=8), [t.bqan])
            for t_ in range(4):
                A("pe", lambda e, t_=t_: e.transpose(out=ptb[:, t_ * 128:(t_ + 1) * 128], in_=t.qan[:, t_ * 128:(t_ + 1) * 128], identity=ident[:]),
                  reads=[t.bqan, bident], writes=[btb])
            yield
            A("act", lambda e: e.copy(out=t.qaT[:], in_=ptb[:, 0:512]), reads=[btb], writes=[t.bqaT])
            A("sp", lambda e: e.dma_start(out=qaT_s[j], in_=t.qaT[:]), reads=[t.bqaT], writes=[(bqaT_s, j)], dma=True)
            yield
        yield from headnorm(t, t.pj[:, 512:1024].rearrange("p (h d) -> p h d", h=8), [(t.bpj, 512)], 8, 64, None, None,
                            t.kan[:].rearrange("p (h d) -> p h d", h=8), [t.bkan])
        for t_ in range(4):
            A("pe", lambda e, t_=t_: e.transpose(out=ptb[:, t_ * 128:(t_ + 1) * 128], in_=t.kan[:, t_ * 128:(t_ + 1) * 128], identity=ident[:]),
              reads=[t.bkan, bident], writes=[btb])
        yield
        A("act", lambda e: e.copy(out=t.kaT[:], in_=ptb[:, 0:512]), reads=[btb], writes=[t.bkaT])
        A("sp", lambda e: e.dma_start(out=kc_ka(j), in_=t.kaT[:]), reads=[t.bkaT], writes=[(bkc, g)], dma=True)
        yield
        pjk = (t.bpj, 1536 if own else 1792)
        if own:
            yield from headnorm(t, t.pj[:, o_cq:o_cq + 256].unsqueeze(1), [pjk], 1, 256, gcq[:], bgcq, t.cqn[:].unsqueeze(1), [t.bcqn])
            for t_ in range(2):
                A("pe", lambda e, t_=t_: e.transpose(out=ptb[:, t_ * 128:(t_ + 1) * 128], in_=t.cqn[:, t_ * 128:(t_ + 1) * 128], identity=ident[:]),
                  reads=[t.bcqn, bident], writes=[btb])
            yield
            A("act", lambda e: e.copy(out=t.cqnT[:].rearrange("p k t -> p (k t)"), in_=ptb[:, 0:256]), reads=[btb], writes=[t.bcqnT])
            yield
            for c_ in range(2):
                bb_ = t.b2[c_]
                for kt in range(2):
                    A("pe", lambda e, c_=c_, kt=kt, bb_=bb_: e.matmul(bank(bb_)[:, 0:384], lhsT=t.cqnT[:, kt, :], rhs=Wuq[:, kt, c_ * 384:(c_ + 1) * 384],
                                                                     start=(kt == 0), stop=(kt == 1)), reads=[t.bcqnT, bWuq], writes=[pbuf[bb_]])
                A("act", lambda e, c_=c_, bb_=bb_: e.copy(out=t.qf[:, c_ * 4:(c_ + 1) * 4, :], in_=bank(bb_)[:, 0:384].rearrange("p (h d) -> p h d", h=4)),
                  reads=[pbuf[bb_]], writes=[(t.bqf, c_)])
            yield
            yield from headnorm(t, t.qf[:, :, 0:64], [t.bqf], 8, 64, gqn[:], bgqn, t.qfull[:, :, 0:64], [(t.bqfull, 0)])
            yield from headnorm(t, t.qf[:, :, 64:96], [t.bqf], 8, 32, gqr[:], bgqr, t.qr[:], [t.bqr])
            yield from rope(t, t.qr[:], [t.bqr], 8, t.qfull[:, :, 64:96], [(t.bqfull, 1)])
            for h in range(8):
                A("pe", lambda e, h=h: e.transpose(out=ptb[0:96, h * 128:(h + 1) * 128], in_=t.qfull[:, h, :], identity=ident[:]),
                  reads=[t.bqfull, bident], writes=[btb])
            yield
            A("act", lambda e: e.copy(out=t.qbT[:].rearrange("p h t -> p (h t)"), in_=ptb[0:96, :]), reads=[btb], writes=[t.bqbT])
            A("sp", lambda e: e.dma_start(out=qbT_s[:, :, j * 128:(j + 1) * 128].rearrange("h p t -> p h t"), in_=t.qbT[:]),
              reads=[t.bqbT], writes=[(bqbT_s, j)], dma=True)
            yield
        yield from headnorm(t, t.pj[:, o_ckv:o_ckv + 128].unsqueeze(1), [pjk], 1, 128, gckv[:], bgckv, t.ckvn[:].unsqueeze(1), [t.bckvn])
        A("pe", lambda e: e.transpose(out=ptb[:, 0:128], in_=t.ckvn[:], identity=ident[:]), reads=[t.bckvn, bident], writes=[btb])
        yield
        A("act", lambda e: e.copy(out=t.ckvnT[:], in_=ptb[:, 0:128]), reads=[btb], writes=[t.bckvnT])
        yield
        for c_ in range(2):
            bb_ = t.b2[c_]
            A("pe", lambda e, c_=c_, bb_=bb_: e.matmul(bank(bb_), lhsT=t.ckvnT[:], rhs=Wukv[:, 0, c_ * 512:(c_ + 1) * 512], start=True, stop=True),
              reads=[t.bckvnT, bWukv], writes=[pbuf[bb_]])
            A("act", lambda e, c_=c_, bb_=bb_: e.copy(out=t.kvf[:, c_ * 4:(c_ + 1) * 4, :], in_=bank(bb_).rearrange("p (h d) -> p h d", h=4)),
              reads=[pbuf[bb_]], writes=[(t.bkvf, c_)])
        yield
        A("dve", lambda e: e.tensor_copy(out=t.vbS[:, :, 0:64], in_=t.kvf[:, :, 64:128]), reads=[t.bkvf], writes=[t.bvbS])
        A("sp", lambda e: e.dma_start(out=kc_vb(j), in_=t.vbS[:]), reads=[t.bvbS], writes=[(bkc, g)], dma=True)
        yield from headnorm(t, t.kvf[:, :, 0:64], [t.bkvf], 8, 64, None, None, t.kfull[:, :, 0:64], [(t.bkfull, 0)])
        yield from headnorm(t, t.pj[:, o_kr:o_kr + 32].unsqueeze(1), [pjk], 1, 32, gkr[:], bgkr, t.krn[:], [t.bkrn])
        yield from rope(t, t.krn[:], [t.bkrn], 1, t.kpe[:].unsqueeze(1), [t.bkpe])
        A("pool", lambda e: e.tensor_copy(out=t.kfull[:, :, 64:96], in_=t.kpe[:].unsqueeze(1).to_broadcast([128, 8, 32])), reads=[t.bkpe], writes=[(t.bkfull, 1)])
        yield
        for h in range(8):
            A("pe", lambda e, h=h: e.transpose(out=ptb[0:96, h * 128:(h + 1) * 128], in_=t.kfull[:, h, :], identity=ident[:]),
              reads=[t.bkfull, bident], writes=[btb])
        yield
        A("act", lambda e: e.copy(out=t.kbT[:].rearrange("p h t -> p (h t)"), in_=ptb[0:96, :]), reads=[btb], writes=[t.bkbT])
        A("sp", lambda e: e.dma_start(out=kc_kb(j), in_=t.kbT[:]), reads=[t.bkbT], writes=[(bkc, g)], dma=True)
        yield

    RG = [[0, 1], [2, 3], [4, 5], [6, 7]]

    def emit_gather(g):
        A("pool", lambda e: e.collective_compute("AllGather", ALU.bypass, replica_groups=RG,
                                                 ins=[kc[g * 640:(g + 1) * 640, :].opt()], outs=[kg[g * 1280:(g + 1) * 1280, :].opt()]),
          reads=[(bkc, g)], writes=[(bkg, g)], cc=True)

    active = [None] * NSL
    actj = [None] * NSL
    done = set()
    nextj = 0
    for si in range(NSL - 1):
        active[si] = blockgen(nextj, si)
        actj[si] = nextj
        nextj += 1
        for sj in range(si + 1):
            for _ in range(9):
                next(active[sj])
    while True:
        alive = False
        for si in range(NSL):
            if active[si] is None and nextj < NOWN:
                active[si] = blockgen(nextj, si)
                actj[si] = nextj
                nextj += 1
            if active[si] is not None:
                alive = True
                try:
                    next(active[si])
                except StopIteration:
                    active[si] = None
                    done.add(actj[si])
                    g_ = actj[si] // 2
                    if (2 * g_ in done) and (2 * g_ + 1 in done):
                        emit_gather(g_)
        if not alive:
            break

    k.new_phase()
    Wg2, bWg2 = k.sb("Wg", [128, 8, DFF], BF16)
    Wo, bWo = k.sb("Wo", [128, 8, D], BF16)
    off_after_w = k.off
    KTs = [k.sb("KT%d" % i, [96, NBLK * 128], BF16) for i in range(2)]
    VAs = [k.sb("VA%d" % i, [128, NBLK, 80], BF16) for i in range(2)]
    QTs = [k.sb("QT%d" % i, [96, NOWN * 128], BF16) for i in range(2)]
    NSB = 4
    SKB = 4
    NPT = 6
    SBANKS = [0, 1, 2, 5]
    BC_B = 6
    pTs = [k.sb("pT%d" % i, [128, 512], BF16) for i in range(NPT)]
    rrow, brrow = k.sb("rrow", [128, 512], F32)
    bcs, bbcs = k.sb("bcs", [64, 512], F32)
    oTn = [k.sb("oTn%d" % i, [64, 512], BF16) for i in range(2)]

    def b_loads(h):
        KT, bKT = KTs[h % 2]
        VA, bVA = VAs[h % 2]
        QT, bQT = QTs[h % 2]
        KTv = KT[:].rearrange("p (r g t) -> p r g t", r=2, g=16)
        VAv = VA[:].rearrange("p c d -> p (c d)").rearrange("p (r g x) -> p r g x", r=2, g=16)
        for q_ in range(4):
            for r_ in range(2):
                A("sp", lambda e, r_=r_, q_=q_: e.dma_start(out=KTv[:, r_, q_ * 4:(q_ + 1) * 4], in_=kg_kb(h, r_)[:, q_ * 4:(q_ + 1) * 4]),
                  reads=[(bkg, g_) for g_ in range(q_ * 4, q_ * 4 + 4)], writes=[(bKT, (r_, q_))], dma=True)
                A("sp", lambda e, r_=r_, q_=q_: e.dma_start(out=VAv[:, r_, q_ * 4:(q_ + 1) * 4], in_=kg_vb(h, r_)[:, q_ * 4:(q_ + 1) * 4]),
                  reads=[(bkg, g_) for g_ in range(q_ * 4, q_ * 4 + 4)], writes=[(bVA, (r_, q_))], dma=True)
        A("sp", lambda e: e.dma_start(out=QT[:], in_=qbT_s[h]), reads=[bqbT_s], writes=[bQT], dma=True)

    items = []
    ng = 0
    for h in range(8):
        for J in range(8):
            blocks = []
            for jb in range(4 * J + 4):
                blocks.append((jb, 0, 0) if jb < 4 * J else (jb, (jb - 4 * J) * 128, 1))
            for jb in range(4 * J + 4):
                blocks.append((NOWN + jb, 0, 0) if jb < 4 * J else (NOWN + jb, (jb - 4 * J) * 128, 2))
            for bi, (blk, c0, kind) in enumerate(blocks):
                items.append(dict(h=h, J=J, blk=blk, c0=c0, kind=kind, bi=bi, nb=len(blocks), ng=ng,
                                  first=(J == 0 and bi == 0)))
            ng += 1

    def b_stage1(i, it):
        h, J, blk, c0, kind = it["h"], it["J"], it["blk"], it["c0"], it["kind"]
        if it["first"] and h == 0:
            b_loads(0)
        if J == 0 and it["bi"] == SKB + 1 and h + 1 < 8:
            b_loads(h + 1)
        KT, bKT = KTs[h % 2]
        QT, bQT = QTs[h % 2]
        ps_b = SBANKS[i % NSB]
        pT, bpT = pTs[i % NPT]
        ps = bank(ps_b)
        A("pe", lambda e: e.matmul(ps[:, c0:512], lhsT=KT[:, blk * 128:(blk + 1) * 128], rhs=QT[:, J * 512 + c0:(J + 1) * 512],
                                   start=True, stop=True), reads=[(bKT, (blk // 32, (blk % 32) // 8)), bQT], writes=[pbuf[ps_b]])
        A("act", lambda e: e.activation(out=pT[:, c0:512], in_=ps[:, c0:512], func=AF.Exp), reads=[pbuf[ps_b]], writes=[bpT])
        if kind == 1:
            A("pool", lambda e: e.tensor_tensor(out=pT[:, c0:c0 + 128], in0=pT[:, c0:c0 + 128], in1=maskE[:], op=ALU.mult),
              reads=[bpT, bmaskE], writes=[bpT])
        elif kind == 2:
            A("pool", lambda e: e.tensor_tensor(out=pT[:, c0:c0 + 128], in0=pT[:, c0:c0 + 128], in1=maskO[:], op=ALU.mult),
              reads=[bpT, bmaskO], writes=[bpT])

    pend_b = []

    def b_stage2(i, it):
        h, J, blk, c0, bi, nb = it["h"], it["J"], it["blk"], it["c0"], it["bi"], it["nb"]
        VA, bVA = VAs[h % 2]
        pT, bpT = pTs[i % NPT]
        po_b = 3 + (it["ng"] % 2)
        po = bank(po_b)
        A("pe", lambda e: e.matmul(po[0:80, c0:512], lhsT=VA[:, blk, :], rhs=pT[:, c0:512], start=(bi == 0), stop=(bi == nb - 1)),
          reads=[(bVA, (blk // 32, (blk % 32) // 8)), bpT], writes=[pbuf[po_b]])
        if bi == nb - 1:
            A("dve", lambda e: e.reciprocal(out=rrow[64:65, :], in_=po[64:65, :]), reads=[pbuf[po_b]], writes=[brrow])

            def epi():
                A("pe", lambda e: e.matmul(bank(BC_B)[0:64, :], lhsT=onesf[64:65, 0:64], rhs=rrow[64:65, :], start=True, stop=True),
                  reads=[bonesf, brrow], writes=[pbuf[BC_B]])
                A("dve", lambda e: e.tensor_copy(out=bcs[:], in_=bank(BC_B)[0:64, :]), reads=[pbuf[BC_B]], writes=[bbcs])
                on, bon = oTn[it["ng"] % 2]
                A("dve", lambda e: e.tensor_tensor(out=on[:], in0=po[0:64, :], in1=bcs[:], op=ALU.mult), reads=[pbuf[po_b], bbcs], writes=[bon])
                A("sp", lambda e: e.dma_start(out=oTB_s[h][:, J * 512:(J + 1) * 512], in_=on[:]), reads=[bon], writes=[(boTB_s, h * 8 + J)], dma=True)
            pend_b.append([6, epi])

    expB, bexpB = k.sb("expB", [128, 8, 6, 128], F32)
    for h in range(8):
        A("sp", lambda e, h=h: e.dma_start(out=expB[:, h].rearrange("p b q -> p (b q)"), in_=biasA[:, h * 768:(h + 1) * 768]),
          writes=[(bexpB, h)], dma=True)
        A("act", lambda e, h=h: e.activation(out=expB[:, h].rearrange("p b q -> p (b q)"), in_=expB[:, h].rearrange("p b q -> p (b q)"), func=AF.Exp),
          reads=[(bexpB, h)], writes=[(bexpB, h)])
    NR = 4
    ring = [(k.sb("rk%d" % i, [128, 2, 512], BF16), k.sb("rv%d" % i, [128, 2, 640], BF16)) for i in range(NR)]
    qTs_ = [k.sb("qTa%d" % i, [128, 4, 128], BF16) for i in range(2)]
    NSA = 3
    pe32 = [k.sb("pe32_%d" % i, [128, 3, 128], BF16) for i in range(NSA)]
    pTa = [k.sb("pTa%d" % i, [128, 3, 128], BF16) for i in range(NSA)]
    poS, bpoS = k.sb("poS", [128, 512], F32)
    rrA, brrA = k.sb("rrA", [128, 512], F32)
    oTAn = [k.sb("oTAn%d" % i, [64, 8, 128], BF16) for i in range(2)]
    PSA_B = 6
    POA_B = 7
    poa = bank(POA_B)

    def a_loads(m):
        (rk, brk), (rv, brv) = ring[m % NR]
        qT, bqT = qTs_[m % 2]
        for r_ in range(2):
            A("sp", lambda e, r_=r_: e.dma_start(out=rk[:, r_, :], in_=kg_ka(r_, m)), reads=[(bkg, m // 2)], writes=[(brk, r_)], dma=True)
            A("sp", lambda e, r_=r_: e.dma_start(out=rv[:, r_, :], in_=kg_va(r_, m)), reads=[(bkg, m // 2)], writes=[(brv, r_)], dma=True)
        A("sp", lambda e: e.dma_start(out=qT[:].rearrange("p t q -> p (t q)"), in_=qaT_s[m]), reads=[bqaT_s], writes=[bqT], dma=True)

    aitems = []
    for m in range(NOWN):
        nbk = 2 * min(m + 1, 3)
        for h in range(8):
            for hf in range(2):
                b0, b1 = hf * 3, min(nbk, hf * 3 + 3)
                if b1 > b0:
                    aitems.append(dict(m=m, h=h, b0=b0, b1=b1, nbk=nbk))

    def a_stage1(i, it):
        m, h, b0, b1 = it["m"], it["h"], it["b0"], it["b1"]
        if h == 0 and b0 == 0 and m == 0:
            a_loads(0)
        if h == 2 and b0 == 0 and m + 1 < NOWN:
            a_loads(m + 1)
        qT, bqT = qTs_[m % 2]
        t_ = h // 2
        pp = (h % 2) * 64
        psa = bank(PSA_B)
        pe_t, bpe_t = pe32[i % NSA]
        pT_t, bpT_t = pTa[i % NSA]
        n_ = b1 - b0
        for b_ in range(b0, b1):
            (rk_, brk_), _ = ring[(m - b_ // 2) % NR]
            side = b_ % 2
            A("pe", lambda e, b_=b_, rk_=rk_, side=side: e.matmul(
                psa[:, (b_ - b0) * 128:(b_ - b0 + 1) * 128], lhsT=rk_[pp:pp + 64, side, t_ * 128:(t_ + 1) * 128], rhs=qT[pp:pp + 64, t_, :],
                start=True, stop=True), reads=[(brk_, side), bqT], writes=[pbuf[PSA_B]])
        A("act", lambda e: e.activation(out=pe_t[:].rearrange("p b q -> p (b q)")[:, 0:n_ * 128], in_=psa[:, 0:n_ * 128], func=AF.Exp),
          reads=[pbuf[PSA_B]], writes=[bpe_t])
        A("dve" if (i % 3) != 2 else "pool", lambda e: e.tensor_tensor(out=pT_t[:, 0:n_, :], in0=pe_t[:, 0:n_, :], in1=expB[:, h, b0:b1, :], op=ALU.mult),
          reads=[bpe_t, (bexpB, h)], writes=[bpT_t])

    pend_a = []

    def a_stage2(i, it):
        m, h, b0, b1, nbk = it["m"], it["h"], it["b0"], it["b1"], it["nbk"]
        pT_t, bpT_t = pTa[i % NSA]
        hq = h % 4
        for b_ in range(b0, b1):
            _, (rv_, brv_) = ring[(m - b_ // 2) % NR]
            side = b_ % 2
            A("pe", lambda e, b_=b_, rv_=rv_, side=side: e.matmul(
                poa[0:80, hq * 128:(hq + 1) * 128], lhsT=rv_[:, side, h * 80:(h + 1) * 80], rhs=pT_t[:, b_ - b0, :],
                start=(b_ == 0), stop=(b_ == nbk - 1)), reads=[(brv_, side), bpT_t], writes=[pbuf[POA_B]])
        if hq == 3 and b1 == nbk:
            A("dve", lambda e: e.tensor_copy(out=poS[0:80, :], in_=poa[0:80, :]), reads=[pbuf[POA_B]], writes=[bpoS])
            A("dve", lambda e: e.reciprocal(out=rrA[64:65, :], in_=poS[64:65, :]), reads=[bpoS], writes=[brrA])
            pend_a.append([2, lambda: a_epi(m, h // 4)])

    def a_epi(m, hh):
        A("pe", lambda e: e.matmul(bank(BC_B)[0:64, :], lhsT=onesf[64:65, 0:64], rhs=rrA[64:65, :], start=True, stop=True),
          reads=[bonesf, brrA], writes=[pbuf[BC_B]])
        on, bon = oTAn[m % 2]
        A("dve", lambda e: e.tensor_tensor(out=on[:, hh * 4:(hh + 1) * 4, :].rearrange("p h q -> p (h q)"), in0=poS[0:64, :], in1=bank(BC_B)[0:64, :], op=ALU.mult),
          reads=[bpoS, pbuf[BC_B]], writes=[(bon, hh)])
        if hh == 1:
            A("sp", lambda e: e.dma_start(out=oTA_s[:, :, m * 128:(m + 1) * 128].rearrange("h p q -> p h q"), in_=on[:]),
              reads=[bon], writes=[(boTA_s, m)], dma=True)

    def a_step(j):
        if j < len(aitems):
            a_stage1(j, aitems[j])
        for pe_ in list(pend_a):
            pe_[0] -= 1
            if pe_[0] <= 0:
                pe_[1]()
                pend_a.remove(pe_)
        if 1 <= j <= len(aitems):
            a_stage2(j - 1, aitems[j - 1])

    NA_STEPS = len(aitems) + 4
    aj = 0
    nB = len(items)
    while aj < 40:
        a_step(aj)
        aj += 1
    for i in range(nB + SKB):
        if i < nB:
            b_stage1(i, items[i])
        for pe_ in list(pend_b):
            pe_[0] -= 1
            if pe_[0] <= 0:
                pe_[1]()
                pend_b.remove(pe_)
        if i >= SKB:
            b_stage2(i - SKB, items[i - SKB])
        if i == 96:
            load_w(Wo, bWo, w["w_out"], 8, D)
        if i in (160, 320, 480, 640):
            c_ = (160, 320, 480, 640).index(i)
            vsrc = w["ffn2_w_gate"].rearrange("(kt p) f -> p kt f", p=128)
            A("pool", lambda e, c_=c_, vsrc=vsrc: e.dma_start(out=Wg2[:, :, c_ * FCH:(c_ + 1) * FCH], in_=vsrc[:, :, c_ * FCH:(c_ + 1) * FCH]),
              writes=[(bWg2, c_)], dma=True)
        while aj < NA_STEPS and (aj - 40) * nB <= i * (NA_STEPS - 40):
            a_step(aj)
            aj += 1
    for pe_ in pend_b:
        pe_[1]()
    while aj < NA_STEPS:
        a_step(aj)
        aj += 1
    for pe_ in pend_a:
        pe_[1]()

    k.new_phase(keep=off_after_w)
    Wg, bWg = Wg2, bWg2
    Wu, bWu = k.sb("Wu", [128, 8, DFF], BF16)
    load_w_cols([(Wu, bWu)], [w["ffn2_w_up"]])
    Wd, bWd = ffn_weights_d("ffn2")
    g2, bg2 = k.sb("g2", [128, D], F32)
    gf, bgf = k.sb("gf", [128, D], F32)
    load_bc(g2[:], bg2, w["ffn2_norm"], D)
    load_bc(gf[:], bgf, w["final_norm"], D)
    T, (ssf, bssf, rsf, brsf), xns, xnTs = ffn_tiles()
    xgs = [k.sb("xg2_%d" % i, [128, 2, D], F32) for i in range(2)]
    oTs = [k.sb("oT%d" % i, [128, 8, 256], BF16) for i in range(2)]
    out_v = out.rearrange("(g t p) c -> g p t c", t=2, p=128)
    oTA_v = oTA_s.rearrange("(h2 par) d t -> par d h2 t", par=2)
    oTB_v = oTB_s.rearrange("(h2 par) d t -> par d h2 t", par=2)
    final_stores = []
    NG3 = 16
    cnt = [0]

    def p3_load_x(g):
        xg, bxg = xgs[g % 2]
        A("sp", lambda e: e.dma_start(out=xg[:], in_=x1_v[g]), reads=[(bx1_s, g)], writes=[bxg], dma=True)

    def p3_load(g):
        oT, boT = oTs[g % 2]
        for par in range(2):
            A("sp", lambda e, par=par: e.dma_start(out=oT[par * 64:(par + 1) * 64, 0:4, :], in_=oTA_v[par][:, :, g * 256:(g + 1) * 256]),
              reads=[boTA_s], writes=[(boT, par)], dma=True)
            A("sp", lambda e, par=par: e.dma_start(out=oT[par * 64:(par + 1) * 64, 4:8, :], in_=oTB_v[par][:, :, g * 256:(g + 1) * 256]),
              reads=[boTB_s], writes=[(boT, 2 + par)], dma=True)

    def p3_wout(g, t_):
        xg, bxg = xgs[g % 2]
        oT, boT = oTs[g % 2]
        for hf in range(2):
            bd = 5 + hf
            pd = bank(bd)
            for kt in range(8):
                A("pe", lambda e, kt=kt, hf=hf, pd=pd: e.matmul(pd, lhsT=oT[:, kt, t_ * 128:(t_ + 1) * 128], rhs=Wo[:, kt, hf * 512:(hf + 1) * 512],
                                                              start=(kt == 0), stop=(kt == 7)), reads=[boT, bWo], writes=[pbuf[bd]])
            A("dve", lambda e, hf=hf, pd=pd: e.tensor_tensor(out=xg[:, t_, hf * 512:(hf + 1) * 512], in0=pd, in1=xg[:, t_, hf * 512:(hf + 1) * 512], op=ALU.add),
              reads=[pbuf[bd], (bxg, t_)], writes=[(bxg, t_)])

    def p3_pre_chain(g, t_):
        xg, bxg = xgs[g % 2]
        xn, bxn = xns[cnt[0] % 2]
        cnt[0] += 1
        norm_chain(xg[:, t_, :], bxg, t_, g2, bg2, xn[:], bxn, None, (ssf, bssf, rsf, brsf, xn, bxn))
        return xn, bxn

    def p3_final(g, t_):
        xg, bxg = xgs[g % 2]
        xn, bxn = xns[cnt[0] % 2]
        cnt[0] += 1
        norm_chain(xg[:, t_, :], bxg, t_, gf, bgf, xg[:, t_, :], bxg, t_, (ssf, bssf, rsf, brsf, xn, bxn))

    def p3_store(g):
        xg, bxg = xgs[g % 2]
        st_ = A("sp", lambda e: e.dma_start(out=out_v[g], in_=xg[:]), reads=[bxg], dma=True)
        final_stores.append(st_)

    def p3_hooks(g):
        hk = {}
        st = {}
        if g >= 1:
            gp = g - 1
            hk.setdefault(1, []).append(lambda: p3_final(gp, 0))
            hk.setdefault(3, []).append(lambda: (p3_final(gp, 1), p3_store(gp)))
        if g + 1 < NG3:
            gn = g + 1
            xnTn, bxnTn = xnTs[gn % 2]
            hk.setdefault(0, []).append(lambda: p3_load(gn))
            hk.setdefault(4, []).append(lambda: p3_load_x(gn))
            hk.setdefault(6, []).append(lambda: p3_wout(gn, 0))
            hk.setdefault(7, []).append(lambda: p3_wout(gn, 1))
            hk.setdefault(9, []).append(lambda: st.__setitem__("q0", p3_pre_chain(gn, 0)))
            hk.setdefault(14, []).append(lambda: transposes_to(st["q0"][0], st["q0"][1], xnTn[:, :, 0:128], bxnTn, 0))
            hk.setdefault(11, []).append(lambda: st.__setitem__("q1", p3_pre_chain(gn, 1)))
            hk.setdefault(16, []).append(lambda: transposes_to(st["q1"][0], st["q1"][1], xnTn[:, :, 128:256], bxnTn, 1))
        return hk

    p3_load(0)
    p3_load_x(0)
    for t_ in range(2):
        p3_wout(0, t_)
        xn, bxn = p3_pre_chain(0, t_)
        transposes_to(xn, bxn, xnTs[0][0][:, :, t_ * 128:(t_ + 1) * 128], xnTs[0][1], t_)
    for g in range(NG3):
        xg, bxg = xgs[g % 2]
        xnT, bxnT = xnTs[g % 2]
        ffn_group(xg, bxg, xnT, bxnT, Wg, bWg, Wu, bWu, Wd, bWd, T, p3_hooks(g))
    for t_ in range(2):
        p3_final(NG3 - 1, t_)
    p3_store(NG3 - 1)
    S.finalize_on(final_stores)
    S.emit(nc)
    return nc


_NC_CACHE = {}


def _host_inputs(inputs):
    x = np.ascontiguousarray(np.asarray(inputs["x"], dtype=np.float32))
    B, Sq, _ = x.shape
    rel_bias = np.asarray(inputs["a_rel_bias"], dtype=np.float32)[0]
    inv = (1.0 / (np.float32(10000.0) ** (np.arange(0, 32, 2, dtype=np.float32) / np.float32(32)))).astype(np.float32)
    k_i = np.arange(128)[:, None]
    q_i = np.arange(128)[None, :]
    shared = {}
    for nm in ["ffn1_norm", "ffn1_w_gate", "ffn1_w_up", "ffn1_w_down", "mix_norm", "w_in", "a_q_norm", "a_k_norm",
               "b_q_lat_norm", "b_w_uq", "b_kv_lat_norm", "b_w_ukv", "b_q_nope_norm", "b_q_rope_norm", "b_k_nope_norm",
               "b_k_rope_norm", "w_out", "ffn2_norm", "ffn2_w_gate", "ffn2_w_up", "ffn2_w_down", "final_norm"]:
        a = np.asarray(inputs[nm], dtype=np.float32)
        if a.ndim == 3:
            a = a[0]
        shared[nm] = np.ascontiguousarray(a)
    shared["ident"] = np.eye(128, dtype=np.float32)
    per_par = {}
    for p in range(2):
        bA = np.empty((128, 8, 6, 128), np.float32)
        for b in range(6):
            d = b // 2
            other = b % 2
            off = (1 - 2 * d - p) if other else (-2 * d - p)
            kpos = off * 128 + k_i
            rel = q_i - kpos
            ck = 2 * off + (k_i >= 64)
            cq = (q_i >= 64).astype(np.int64)
            vis = ((cq - ck) >= 0) & ((cq - ck) <= 8)
            idx = np.clip(rel, -128, 128) + 128
            vals = rel_bias[:, idx]
            sel = np.where(vis[None], vals, np.float32(-100.0))
            bA[:, :, b, :] = sel.transpose(1, 0, 2)
        gblk = np.arange(32) * 2 + p
        pos = (gblk[:, None] * 128 + np.arange(128)[None, :]).reshape(-1).astype(np.float32)
        ang = pos[:, None] * inv[None, :]
        c = np.cos(ang).astype(np.float32)
        s = np.sin(ang).astype(np.float32)
        cs = np.concatenate([c, c, -s, s], axis=1).astype(np.float32)
        diag = np.where((k_i >= 64) & (q_i < 64), 0.0, 1.0).astype(np.float32)
        ones = np.ones((128, 128), np.float32)
        zeros = np.zeros((128, 128), np.float32)
        per_par[p] = dict(biasA=np.ascontiguousarray(bA.reshape(128, -1)), cs=np.ascontiguousarray(cs),
                          maskE=(diag if p == 0 else ones), maskO=(zeros if p == 0 else diag), gblk=gblk)
    in_maps = []
    for c_ in range(8):
        b = c_ // 2
        p = c_ % 2
        xb = x[b].reshape(64, 128, 1024)
        xs = np.ascontiguousarray(xb[per_par[p]["gblk"]].reshape(32 * 128, 1024))
        m = dict(shared)
        m["xs"] = xs
        m["cs"] = per_par[p]["cs"]
        m["biasA"] = per_par[p]["biasA"]
        m["maskO"] = per_par[p]["maskO"]
        m["maskE"] = per_par[p]["maskE"]
        in_maps.append(m)
    return in_maps


def kernel(**inputs):
    in_maps = _host_inputs(inputs)
    if "nc" not in _NC_CACHE:
        _NC_CACHE["nc"] = build_program()
    nc = _NC_CACHE["nc"]
    res = run_bass_kernel_spmd(nc, in_maps, core_ids=list(range(8)))
    out = np.empty((4, 8192, 1024), np.float32)
    ov = out.reshape(4, 64, 128, 1024)
    for c_ in range(8):
        b = c_ // 2
        p = c_ % 2
        r = np.asarray(res.results[c_]["out"]).reshape(32, 128, 1024)
        ov[b, p::2] = r
    return out
```
